# Optimizing a Trainium2 kernel written in Bass

```python
import math
import jax, jax.numpy as jnp
from jax import lax
import numpy as np

D_MODEL = 2048
BATCH = 4
SEQ = 2048
DEPTH = 1
DEC_BATCH = 128
DEC_SEQ = 4
PAST_LEN = 2048
PAGE_SIZE = 128

MIX_WIDTH = D_MODEL
NSA_WIDTH = MIX_WIDTH // 2
CONV_DIM = MIX_WIDTH - NSA_WIDTH
HEAD_DIM = 128
N_HEADS = NSA_WIDTH // HEAD_DIM
N_KV_HEADS = 2
CMP_BLOCK = 32
CMP_STRIDE = 16
SEL_BLOCK = 64
N_SELECT = 16
WINDOW = 512
CONV_WIDTH = 31
N_CACHE_SLOTS = 4
N_WIN_SLOTS = 2
N_BRANCH = 3
Q_COLS = N_HEADS * HEAD_DIM
KV_COLS = N_CACHE_SLOTS * N_KV_HEADS * HEAD_DIM
WIN_COLS = N_WIN_SLOTS * N_KV_HEADS * HEAD_DIM
GATE_COLS = N_BRANCH * N_HEADS
IN_COLS = Q_COLS + KV_COLS + WIN_COLS + GATE_COLS + NSA_WIDTH + 2 * CONV_DIM + CONV_DIM
SEL_QROWS = 128
WIN_QBLK = 128
NORM_EPS = 1e-6
NEG_INF = -1e30
FORCE_BONUS = 1e6

kernel_name = 'nsa_conformer_hybrid_step'


def _rmsnorm(x, g):
    x32 = x.astype(jnp.float32)
    y = x32 * lax.rsqrt(jnp.mean(x32 * x32, -1, keepdims=True) + NORM_EPS)
    return (y * g.astype(jnp.float32)).astype(x.dtype)


def _layernorm(x, g, b):
    x32 = x.astype(jnp.float32)
    mu = jnp.mean(x32, -1, keepdims=True)
    var = jnp.mean(jnp.square(x32 - mu), -1, keepdims=True)
    return ((x32 - mu) * lax.rsqrt(var + NORM_EPS) * g.astype(jnp.float32) + b.astype(jnp.float32)).astype(x.dtype)


def _masked_softmax(s, mask):
    s = jnp.where(mask, s, NEG_INF)
    e = jnp.where(mask, jnp.exp(s - jnp.max(s, -1, keepdims=True)), 0.0)
    return e / jnp.maximum(jnp.sum(e, -1, keepdims=True), 1e-30)


def _block_size(n, target):
    b = max(1, min(n, target))
    while n % b:
        b -= 1
    return b


def _compress(kv_cmp, pe, w1, b1, w2, b2):
    L = kv_cmp.shape[1]
    n_cmp = (L - CMP_BLOCK) // CMP_STRIDE + 1
    idx = np.arange(n_cmp)[:, None] * CMP_STRIDE + np.arange(CMP_BLOCK)[None, :]
    blocks = kv_cmp[:, idx]
    h = jnp.einsum('bnlsgd,slde->bnsge', blocks, w1) + (jnp.einsum('sld,slde->se', pe, w1) + b1)[:, None, :]
    h = jax.nn.silu(h)
    return jnp.einsum('bnsge,sef->bnsgf', h, w2) + b2[:, None, :]


def _cmp_attend(q, kc, vc, qpos):
    B, T = q.shape[:2]
    qg = q.reshape(B, T, N_KV_HEADS, N_HEADS // N_KV_HEADS, HEAD_DIM)
    ends = jnp.arange(kc.shape[1]) * CMP_STRIDE + CMP_BLOCK - 1
    s = jnp.einsum('btgrd,bngd->bgrtn', qg, kc, preferred_element_type=jnp.float32) * HEAD_DIM ** -0.5
    p = _masked_softmax(s, ends[None, :] <= qpos[:, None])
    o = jnp.einsum('bgrtn,bngd->btgrd', p.astype(vc.dtype), vc)
    return o.reshape(B, T, N_HEADS, HEAD_DIM), p


def _select_blocks(p_cmp, qpos, n_sel):
    n_cmp = p_cmp.shape[-1]
    ci = np.arange(n_cmp)[:, None] * CMP_STRIDE
    sj = np.arange(n_sel)[None, :] * SEL_BLOCK
    overlap = jnp.asarray(((ci < sj + SEL_BLOCK) & (ci + CMP_BLOCK > sj)).astype(np.float32))
    imp = jnp.einsum('bgtn,ns->bgts', jnp.sum(p_cmp, axis=2), overlap)
    j = jnp.arange(n_sel)[None, :]
    cur = (qpos // SEL_BLOCK)[:, None]
    allowed = j * SEL_BLOCK <= qpos[:, None]
    forced = (j == 0) | (j == cur) | (j == cur - 1)
    score = jnp.where(allowed, imp + jnp.where(forced, FORCE_BONUS, 0.0), NEG_INF)
    vals, idx = lax.top_k(score, min(N_SELECT, n_sel))
    return idx, vals > 0.5 * NEG_INF


def _select_attend(q, ks, vs, sel_idx, sel_valid, qpos):
    B, T = q.shape[:2]
    L = ks.shape[1]
    n_sel = -(-L // SEL_BLOCK)
    pad = ((0, 0), (0, n_sel * SEL_BLOCK - L), (0, 0), (0, 0))
    kb = jnp.pad(ks, pad).reshape(B, n_sel, SEL_BLOCK, N_KV_HEADS, HEAD_DIM).transpose(0, 3, 1, 2, 4)
    vb = jnp.pad(vs, pad).reshape(B, n_sel, SEL_BLOCK, N_KV_HEADS, HEAD_DIM).transpose(0, 3, 1, 2, 4)
    r = N_HEADS // N_KV_HEADS
    qg = q.reshape(B, T, N_KV_HEADS, r, HEAD_DIM)
    K = sel_idx.shape[-1]
    qb = _block_size(T, SEL_QROWS // B)
    bi = jnp.arange(B)[:, None, None, None]
    gi = jnp.arange(N_KV_HEADS)[None, :, None, None]
    offs = jnp.arange(SEL_BLOCK)

    def one_block(i):
        s0 = i * qb
        qi = lax.dynamic_slice_in_dim(qg, s0, qb, axis=1)
        ii = lax.dynamic_slice_in_dim(sel_idx, s0, qb, axis=2)
        vi = lax.dynamic_slice_in_dim(sel_valid, s0, qb, axis=2)
        pi = lax.dynamic_slice_in_dim(qpos, s0, qb)
        kg = kb[bi, gi, ii].reshape(B, N_KV_HEADS, qb, K * SEL_BLOCK, HEAD_DIM)
        vg = vb[bi, gi, ii].reshape(B, N_KV_HEADS, qb, K * SEL_BLOCK, HEAD_DIM)
        kpos = ii[..., None] * SEL_BLOCK + offs
        mask = (vi[..., None] & (kpos <= pi[None, None, :, None, None])).reshape(B, N_KV_HEADS, qb, K * SEL_BLOCK)
        s = jnp.einsum('bqgrd,bgqnd->bgrqn', qi, kg, preferred_element_type=jnp.float32) * HEAD_DIM ** -0.5
        p = _masked_softmax(s, mask[:, :, None])
        o = jnp.einsum('bgrqn,bgqnd->bqgrd', p.astype(vg.dtype), vg)
        return o.reshape(B, qb, N_HEADS, HEAD_DIM)

    out = lax.map(one_block, jnp.arange(T // qb))
    return jnp.moveaxis(out, 0, 1).reshape(B, T, N_HEADS, HEAD_DIM)


def _window_attend(q, kw, vw, kpos, qpos):
    B, T = q.shape[:2]
    r = N_HEADS // N_KV_HEADS
    qg = q.reshape(B, T, N_KV_HEADS, r, HEAD_DIM)
    qb = _block_size(T, WIN_QBLK)

    def one_block(i):
        s0 = i * qb
        qi = lax.dynamic_slice_in_dim(qg, s0, qb, axis=1)
        ki = lax.dynamic_slice_in_dim(kw, s0, WINDOW + qb, axis=1)
        vi = lax.dynamic_slice_in_dim(vw, s0, WINDOW + qb, axis=1)
        kp = lax.dynamic_slice_in_dim(kpos, s0, WINDOW + qb)
        qp = lax.dynamic_slice_in_dim(qpos, s0, qb)
        dist = qp[:, None] - kp[None, :]
        mask = (dist >= 0) & (dist <= WINDOW) & (kp >= 0)[None, :]
        s = jnp.einsum('bqgrd,bkgd->bgrqk', qi, ki, preferred_element_type=jnp.float32) * HEAD_DIM ** -0.5
        p = _masked_softmax(s, mask)
        o = jnp.einsum('bgrqk,bkgd->bqgrd', p.astype(vi.dtype), vi)
        return o.reshape(B, qb, N_HEADS, HEAD_DIM)

    out = lax.map(one_block, jnp.arange(T // qb))
    return jnp.moveaxis(out, 0, 1).reshape(B, T, N_HEADS, HEAD_DIM)


def _causal_depthwise_conv(u_all, w, b):
    y = lax.conv_general_dilated(u_all, w[:, None, :], window_strides=(1,), padding='VALID',
                                 dimension_numbers=('NWC', 'WIO', 'NWC'), feature_group_count=CONV_DIM)
    return y + b


def _layer(x, c, kv_past, win_past, conv_past, w_ada, b_ada, norm_pre, norm_post, w_in,
           cmp_pe, cmp_w1, cmp_b1, cmp_w2, cmp_b2, conv_dw, conv_db, conv_ln_g, conv_ln_b, w_out):
    B, T, _ = x.shape
    pos0 = kv_past.shape[1]
    pw = win_past.shape[1]
    shift, scale, gate = jnp.split(jax.nn.silu(c) @ w_ada + b_ada, 3, axis=-1)
    h = _rmsnorm(x, norm_pre) * (1.0 + scale[:, None, :]) + shift[:, None, :]
    proj = h @ w_in
    cuts = np.cumsum([Q_COLS, KV_COLS, WIN_COLS, GATE_COLS, NSA_WIDTH, 2 * CONV_DIM]).tolist()
    q, kv_new, win_new, g_raw, z_nsa, glu_in, z_conv = jnp.split(proj, cuts, axis=-1)
    q = q.reshape(B, T, N_HEADS, HEAD_DIM)
    kv_new = kv_new.reshape(B, T, N_CACHE_SLOTS, N_KV_HEADS, HEAD_DIM)
    win_new = win_new.reshape(B, T, N_WIN_SLOTS, N_KV_HEADS, HEAD_DIM)
    qpos = pos0 + jnp.arange(T)

    kv_full = jnp.concatenate([kv_past, kv_new], axis=1)
    kc = _compress(kv_full[:, :, 0:2], cmp_pe, cmp_w1, cmp_b1, cmp_w2, cmp_b2)
    o_cmp, p_cmp = _cmp_attend(q, kc[:, :, 0], kc[:, :, 1], qpos)
    n_sel = -(-kv_full.shape[1] // SEL_BLOCK)
    sel_idx, sel_valid = _select_blocks(p_cmp, qpos, n_sel)
    o_slc = _select_attend(q, kv_full[:, :, 2], kv_full[:, :, 3], sel_idx, sel_valid, qpos)
    win_all = jnp.concatenate([win_past, win_new], axis=1)
    win_pad = jnp.pad(win_all, ((0, 0), (WINDOW - pw, 0), (0, 0), (0, 0), (0, 0)))
    kpos = pos0 - WINDOW + jnp.arange(WINDOW + T)
    o_win = _window_attend(q, win_pad[:, :, 0], win_pad[:, :, 1], kpos, qpos)
    g = jax.nn.sigmoid(g_raw.reshape(B, T, N_BRANCH, N_HEADS))[..., None]
    o = g[:, :, 0] * o_cmp + g[:, :, 1] * o_slc + g[:, :, 2] * o_win
    o_nsa = o.reshape(B, T, NSA_WIDTH) * jax.nn.silu(z_nsa)

    a, gl = jnp.split(glu_in, 2, axis=-1)
    u = a * jax.nn.sigmoid(gl)
    u_all = jnp.concatenate([conv_past, u], axis=1)
    cv = _causal_depthwise_conv(u_all, conv_dw, conv_db)
    cv = jax.nn.silu(_layernorm(cv, conv_ln_g, conv_ln_b)) * jax.nn.silu(z_conv)

    mix = jnp.concatenate([o_nsa, cv], axis=-1) @ w_out
    y = x + gate[:, None, :] * _rmsnorm(mix, norm_post)
    keep = min(WINDOW, pw + T)
    new_win = win_all[:, pw + T - keep:]
    new_conv = u_all[:, u_all.shape[1] - (CONV_WIDTH - 1):]
    return y, kv_new, new_win, new_conv


def setup_inputs(seed: int = 0) -> dict:
    key = jax.random.key(seed)
    ks = jax.random.split(key, 24)
    f32 = jnp.float32
    n_pages = PAST_LEN // PAGE_SIZE
    n_phys = (5 * DEC_BATCH * n_pages) // 4
    w_len = min(WINDOW, PAST_LEN)

    def nrm(k, shape, s=1.0):
        return jax.random.normal(k, shape, f32) * s

    page_table = jax.random.permutation(ks[5], n_phys)[:DEC_BATCH * n_pages].reshape(DEC_BATCH, n_pages).astype(jnp.int32)
    return {
        'x_prompt': nrm(ks[0], (BATCH, SEQ, D_MODEL)),
        'x_sample': nrm(ks[1], (DEC_BATCH, DEC_SEQ, D_MODEL)),
        'c_prompt': nrm(ks[2], (BATCH, D_MODEL)),
        'c_sample': nrm(ks[3], (DEC_BATCH, D_MODEL)),
        'cache_kv': nrm(ks[4], (DEPTH, n_phys, PAGE_SIZE, N_CACHE_SLOTS, N_KV_HEADS, HEAD_DIM)),
        'page_table': page_table,
        'state_win_kv': nrm(ks[6], (DEPTH, DEC_BATCH, w_len, N_WIN_SLOTS, N_KV_HEADS, HEAD_DIM)),
        'state_conv': nrm(ks[7], (DEPTH, DEC_BATCH, CONV_WIDTH - 1, CONV_DIM), 0.5),
        'w_ada': nrm(ks[8], (DEPTH, D_MODEL, 3 * D_MODEL), 0.5 * D_MODEL ** -0.5),
        'b_ada': nrm(ks[9], (DEPTH, 3 * D_MODEL), 0.01),
        'norm_pre': 1.0 + nrm(ks[10], (DEPTH, D_MODEL), 0.01),
        'norm_post': 1.0 + nrm(ks[11], (DEPTH, D_MODEL), 0.01),
        'w_in': nrm(ks[12], (DEPTH, D_MODEL, IN_COLS), D_MODEL ** -0.5),
        'cmp_pe': nrm(ks[13], (DEPTH, 2, CMP_BLOCK, HEAD_DIM), 0.1),
        'cmp_w1': nrm(ks[14], (DEPTH, 2, CMP_BLOCK, HEAD_DIM, HEAD_DIM), (CMP_BLOCK * HEAD_DIM) ** -0.5),
        'cmp_b1': nrm(ks[15], (DEPTH, 2, HEAD_DIM), 0.01),
        'cmp_w2': nrm(ks[16], (DEPTH, 2, HEAD_DIM, HEAD_DIM), HEAD_DIM ** -0.5),
        'cmp_b2': nrm(ks[17], (DEPTH, 2, HEAD_DIM), 0.01),
        'conv_dw': nrm(ks[18], (DEPTH, CONV_WIDTH, CONV_DIM), CONV_WIDTH ** -0.5),
        'conv_db': nrm(ks[19], (DEPTH, CONV_DIM), 0.01),
        'conv_ln_g': 1.0 + nrm(ks[20], (DEPTH, CONV_DIM), 0.01),
        'conv_ln_b': nrm(ks[21], (DEPTH, CONV_DIM), 0.01),
        'w_out': nrm(ks[22], (DEPTH, MIX_WIDTH, D_MODEL), MIX_WIDTH ** -0.5),
    }


def reference(x_prompt, x_sample, c_prompt, c_sample, cache_kv, page_table, state_win_kv, state_conv,
              w_ada, b_ada, norm_pre, norm_post, w_in, cmp_pe, cmp_w1, cmp_b1, cmp_w2, cmp_b2,
              conv_dw, conv_db, conv_ln_g, conv_ln_b, w_out):
    bp = x_prompt.shape[0]
    bs, n_pages = page_table.shape
    past_len = n_pages * cache_kv.shape[2]
    hp, hs = x_prompt, x_sample
    kv_p, kv_s, win_p, win_s, conv_p, conv_s = [], [], [], [], [], []
    for l in range(DEPTH):
        wl = (w_ada[l], b_ada[l], norm_pre[l], norm_post[l], w_in[l], cmp_pe[l], cmp_w1[l], cmp_b1[l],
              cmp_w2[l], cmp_b2[l], conv_dw[l], conv_db[l], conv_ln_g[l], conv_ln_b[l], w_out[l])
        hp, a1, a2, a3 = _layer(
            hp, c_prompt,
            jnp.zeros((bp, 0, N_CACHE_SLOTS, N_KV_HEADS, HEAD_DIM), hp.dtype),
            jnp.zeros((bp, 0, N_WIN_SLOTS, N_KV_HEADS, HEAD_DIM), hp.dtype),
            jnp.zeros((bp, CONV_WIDTH - 1, CONV_DIM), hp.dtype), *wl)
        kv_past = cache_kv[l][page_table].reshape(bs, past_len, N_CACHE_SLOTS, N_KV_HEADS, HEAD_DIM)
        hs, b1_, b2_, b3_ = _layer(hs, c_sample, kv_past, state_win_kv[l], state_conv[l], *wl)
        kv_p.append(a1); win_p.append(a2); conv_p.append(a3)
        kv_s.append(b1_); win_s.append(b2_); conv_s.append(b3_)
    kv_rows_prompt = jnp.stack(kv_p)
    kv_rows_sample = jnp.stack(kv_s)
    win_prompt = jnp.stack(win_p)
    win_sample = jnp.stack(win_s)
    conv_prompt = jnp.stack(conv_p)
    conv_sample = jnp.stack(conv_s)
    return (hp, hs, kv_rows_prompt, kv_rows_sample, win_prompt, win_sample, conv_prompt, conv_sample)
```

```python
import numpy as np
import ml_dtypes
import concourse.bass as bass
import concourse.mybir as mybir
from concourse.bass_utils import run_bass_kernel_spmd

F32 = mybir.dt.float32
BF16 = mybir.dt.bfloat16
I32 = mybir.dt.int32
AF = mybir.ActivationFunctionType
ALU = mybir.AluOpType
AX = mybir.AxisListType

D = 2048
KC = 16
NOWN = 1024
NCTX = 1024
NSEQ = 16
NST = 64
PAST = 2048
INC = 6680
C_Q, C_KV, C_WIN, C_G, C_ZN, C_A, C_GL, C_ZC = 0, 1024, 2048, 2560, 2584, 3608, 4632, 5656
BIG = 30000.0
SCL = 128 ** -0.5
EPS = 1e-6


class Dep:
    __slots__ = ("w", "r", "name", "dsem", "excl")

    def __init__(self, name="", excl=False):
        self.w = None
        self.r = {}
        self.name = name
        self.dsem = None
        self.excl = excl


class Tr:
    def __init__(self, nc):
        self.nc = nc
        self.E = {"pe": nc.tensor, "act": nc.scalar, "dve": nc.vector, "pool": nc.gpsimd, "sp": nc.sync}
        self.sem = {k: nc.alloc_semaphore("S_" + k) for k in ("pe", "act", "dve", "pool")}
        self.cnt = {k: 0 for k in self.sem}
        self.seen = {k: {} for k in self.E}
        self.dsems = {}
        self.ninst = 0

    def _wait(self, eng, key, val):
        if self.seen[eng].get(key, 0) >= val:
            return
        self.seen[eng][key] = val
        sem = self.sem[key] if key in self.sem else self.dsems[key][0]
        self.E[eng].wait_ge(sem, val)

    def _need(self, eng, key, val):
        if key == eng and eng == "pe":
            return
        self._wait(eng, key, val)

    def _deps(self, eng, R, W):
        for d in R:
            if d.w is not None:
                self._need(eng, *d.w)
            if d.excl:
                for k, v in d.r.items():
                    if k != eng:
                        self._need(eng, k, v)
        for d in W:
            if d.w is not None:
                self._need(eng, *d.w)
            for k, v in d.r.items():
                self._need(eng, k, v)

    def op(self, eng, fn, R=(), W=()):
        self._deps(eng, R, W)
        ins = fn()
        self.cnt[eng] += 1
        self.ninst += 1
        ins.then_inc(self.sem[eng], 1)
        c = self.cnt[eng]
        for d in R:
            if d.r.get(eng, 0) < c:
                d.r[eng] = c
        for d in W:
            d.w = (eng, c)
            d.r = {}

    def dma(self, q, out, in_, dep, load=True, extraR=(), extraW=(), **kw):
        if load:
            if q == "sp" and dep.w is not None and dep.dsem is not None and dep.w[0] == dep.dsem and not dep.r:
                self._deps(q, extraR, list(extraW))
            else:
                self._deps(q, extraR, [dep] + list(extraW))
        else:
            self._deps(q, [dep] + list(extraR), extraW)
        if dep.dsem is None:
            name = "D%d_%s" % (len(self.dsems), dep.name)
            dep.dsem = name
            self.dsems[name] = [self.nc.alloc_semaphore(name), 0]
        ent = self.dsems[dep.dsem]
        ent[1] += 16
        self.ninst += 1
        self.E[q].dma_start(out=out, in_=in_, **kw).then_inc(ent[0], 16)
        tok = (dep.dsem, ent[1])
        if load:
            dep.w = tok
            dep.r = {}
            for d in extraW:
                d.w = tok
                d.r = {}
        else:
            dep.r[tok[0]] = tok[1]
        for d in extraR:
            d.r[tok[0]] = tok[1]
        return tok

    def idma(self, out, in_, idx_ap, dep, extraR=()):
        q = "pool"
        if dep.w is not None and dep.dsem is not None and dep.w[0] == dep.dsem and not dep.r:
            self._deps(q, extraR, [])
        else:
            self._deps(q, extraR, [dep])
        if dep.dsem is None:
            name = "D%d_%s" % (len(self.dsems), dep.name)
            dep.dsem = name
            self.dsems[name] = [self.nc.alloc_semaphore(name), 0]
        ent = self.dsems[dep.dsem]
        ent[1] += 16
        self.ninst += 1
        self.nc.gpsimd.indirect_dma_start(
            out=out, out_offset=None, in_=in_,
            in_offset=bass.IndirectOffsetOnAxis(ap=idx_ap, axis=0)).then_inc(ent[0], 16)
        tok = (dep.dsem, ent[1])
        dep.w = tok
        dep.r = {}
        for d in extraR:
            d.r[tok[0]] = tok[1]

    def barrier(self):
        for e in self.E:
            for k in self.sem:
                if self.cnt[k]:
                    self._wait(e, k, self.cnt[k])
            for name, (sem, cnt) in self.dsems.items():
                if cnt:
                    self._wait(e, name, cnt)

    def finish(self):
        for name, (sem, cnt) in self.dsems.items():
            if cnt:
                self._wait("sp", name, cnt)
        for k in self.sem:
            if self.cnt[k]:
                self._wait("sp", k, self.cnt[k])


class Arena:
    def __init__(self, nc, lo=16512, hi=229344):
        self.nc, self.lo, self.hi = nc, lo, hi
        self.top = lo
        self.n = 0
        self.offs = {}

    def alloc(self, shape, dt, name="t"):
        esz = 2 if dt == BF16 else 4
        fre = 1
        for s in shape[1:]:
            fre *= s
        nbytes = (fre * esz + 63) // 64 * 64
        off = self.top
        assert off + nbytes <= self.hi, ("SBUF overflow", name, off + nbytes - self.hi)
        self.top += nbytes
        self.n += 1
        t = self.nc.alloc_sbuf_tensor_at("%s_%d" % (name, self.n), list(shape), dt, offset=off)
        self.offs[name] = off
        return t.ap()

    def alloc_at(self, off, shape, dt, name="t"):
        self.n += 1
        t = self.nc.alloc_sbuf_tensor_at("%s_%d" % (name, self.n), list(shape), dt, offset=off)
        return t.ap()

    def mark(self):
        return self.top

    def release(self, m):
        self.top = m


def build(n_phys=2560, dbg=False):
    nc = bass.Bass("TRN2", target_bir_lowering=False)
    tr = Tr(nc)
    ar = Arena(nc)

    def din(name, shape, dt=F32):
        return nc.dram_tensor(name, list(shape), dt, kind="ExternalInput").ap()

    def dout(name, shape, dt=F32):
        return nc.dram_tensor(name, list(shape), dt, kind="ExternalOutput").ap()

    xo = din("xo", [NOWN, D]); xc = din("xc", [NCTX, D]); xs = din("xs", [NST, D])
    cT_d = din("cT", [128, KC, 17])
    w_ada = din("w_ada", [D, 3 * D]); b_ada = din("b_ada", [1, 3 * D])
    npre_d = din("npre_fm", [128, KC]); npost_d = din("npost", [1, D])
    w_in = din("w_in", [D, INC]); w_out = din("w_out", [D, D])
    w1_d = din("cmp_w1", [2, 32, 128, 128]); pe_d = din("cmp_peT", [128, 2, 32])
    b1_d = din("cmp_b1T", [128, 2]); w2_d = din("cmp_w2", [2, 128, 128]); b2T_d = din("cmp_b2T", [128, 2])
    b2v_d = din("cmp_b2v", [1, 128])
    cdw_d = din("conv_dwT", [128, 8, 31]); cdb_d = din("conv_dbT", [128, 8])
    lng_d = din("conv_lngT", [128, 8]); lnb_d = din("conv_lnbT", [128, 8])
    cache = din("cache", [n_phys * 128, 1024]); ptab_d = din("ptab", [NSEQ, 16], I32)
    swin = din("swin", [NSEQ, 512, 512]); sconv = din("sconv", [NSEQ, 30, 1024])
    identb_d = din("ident_bf", [128, 128], BF16); identf_d = din("ident_f", [128, 128])
    onesb_d = din("ones_bf", [128, 128], BF16); onesf_d = din("ones_f", [128, 128])
    m4_d = din("masks4", [128, 4, 512], BF16)
    negcmp_d = din("negcmp", [127, 8, 512], BF16)
    E_d = din("Esel", [128, 2048], BF16); ovl_d = din("overlap", [128, 32])
    selb_d = din("selbias", [128, 8, 32]); allow_d = din("allowed", [128, 8, 32])
    selg_d = din("selg", [128, 24, 128], BF16)
    rp_d = din("rp", [128, 128]); rs_d = din("rs", [128, 64])
    cval_d = din("cval", [128, 1])
    perm_d = din("perm16", [128, 128], BF16)
    sbias_d = din("sbias", [128, 32]); mnew_d = din("mnew", [128, 16], BF16); mwin0_d = din("mwin0", [128, 16], BF16)
    y_p = dout("y_p", [NOWN, D]); kv_p = dout("kv_p", [NOWN, 1024]); win_p = dout("win_p", [512, 512])
    conv_p = dout("conv_p", [32, 1024])
    y_s = dout("y_s", [NST, D]); kv_s = dout("kv_s", [NST, 1024]); win_s = dout("win_s", [NSEQ, 512, 512])
    conv_s = dout("conv_s", [NSEQ, 30, 1024])
    dbg_o = {}

    PS = [nc.alloc_psum_tensor("ps%d" % i, [128, 512], F32).ap() for i in range(8)]
    PSd = [Dep("ps%d" % i, excl=True) for i in range(8)]

    def psbf(i):
        return PS[i].bitcast(BF16)

    def mm(out, lhsT, rhs, start, stop, R, W):
        tr.op("pe", lambda: nc.tensor.matmul(out, lhsT=lhsT, rhs=rhs, start=start, stop=stop), R, W)

    def tp(out, in_, ident, R, W):
        tr.op("pe", lambda: nc.tensor.transpose(out, in_, ident), R, W)

    def act(out, in_, func, R, W, bias=None, scale=None, accum=None):
        kw = {}
        if bias is not None:
            kw["bias"] = bias
        if scale is not None:
            kw["scale"] = scale
        if accum is not None:
            kw["accum_out"] = accum
        tr.op("act", lambda: nc.scalar.activation(out=out, in_=in_, func=func, **kw), R, W)

    def tt(eng, out, a, b, op, R, W):
        e = nc.vector if eng == "dve" else nc.gpsimd
        tr.op(eng, lambda: e.tensor_tensor(out=out, in0=a, in1=b, op=op), R, W)

    def ts(eng, out, a, s1, s2, op0, op1, R, W):
        e = nc.vector if eng == "dve" else nc.gpsimd
        if s2 is None:
            tr.op(eng, lambda: e.tensor_scalar(out=out, in0=a, scalar1=s1, scalar2=None, op0=op0), R, W)
        else:
            tr.op(eng, lambda: e.tensor_scalar(out=out, in0=a, scalar1=s1, scalar2=s2, op0=op0, op1=op1), R, W)

    def stt(eng, out, a, s, b, op0, op1, R, W):
        e = nc.vector if eng == "dve" else nc.gpsimd
        tr.op(eng, lambda: e.scalar_tensor_tensor(out=out, in0=a, scalar=s, in1=b, op0=op0, op1=op1), R, W)

    def cp(eng, out, in_, R, W):
        if eng == "act":
            tr.op("act", lambda: nc.scalar.copy(out=out, in_=in_), R, W)
        else:
            e = nc.vector if eng == "dve" else nc.gpsimd
            tr.op(eng, lambda: e.tensor_copy(out=out, in_=in_), R, W)

    CONST = Dep("const")

    def cload(shape, dt, src, q="sp", name="c", dep=None):
        t = ar.alloc(shape, dt, name)
        tr.dma(q, t, src, dep or CONST)
        return t

    identb = cload([128, 128], BF16, identb_d); identf = cload([128, 128], F32, identf_d)
    onesb = cload([128, 128], BF16, onesb_d); onesf = cload([128, 128], F32, onesf_d)
    rp = cload([128, 128], F32, rp_d); rsel = cload([128, 64], F32, rs_d)
    cval = cload([128, 1], F32, cval_d)
    npre = cload([128, KC], F32, npre_d)
    cdw = cload([128, 8, 31], F32, cdw_d); cdb = cload([128, 8], F32, cdb_d)
    lng = cload([128, 8], F32, lng_d); lnb = cload([128, 8], F32, lnb_d)
    b1T = cload([128, 2], F32, b1_d); b2T = cload([128, 2], F32, b2T_d)
    b2v = cload([127, 128], F32, b2v_d.partition_broadcast(127))
    CONSTP = Dep("constp")
    w1 = ar.alloc([128, 2, 32, 128], BF16, "w1")
    for s in range(2):
        tr.dma("pool", w1[:, s], w1_d[s].rearrange("l d e -> d l e"), CONSTP)
    peT = ar.alloc([128, 2, 32], BF16, "peT")
    tr.dma("pool", peT, pe_d, CONSTP)
    w2 = ar.alloc([128, 2, 128], BF16, "w2")
    tr.dma("pool", w2, w2_d.rearrange("s e f -> e s f"), CONSTP)
    A1T = ar.alloc([128, KC, 17], F32, "A1T"); shT = ar.alloc([128, KC, 17], F32, "shT")
    G2 = ar.alloc([128, D], F32, "G2")
    dA1 = Dep("A1T"); dG2 = Dep("G2")
    tr.op("pool", lambda: nc.gpsimd.memset(G2, 0.0), [], [dG2])
    hTs = ar.alloc([128, KC, NST], BF16, "hTs"); dhTs = Dep("hTs")
    hThalo = ar.alloc([128, KC, 32], BF16, "hThalo"); dhalo = Dep("hThalo")
    bias1 = ar.alloc([128, 2], F32, "bias1"); dbias1 = Dep("bias1")
    kcT = ar.alloc([128, 2, 127], BF16, "kcT"); vc = ar.alloc([127, 2, 128], BF16, "vc")
    dkc = Dep("kcT"); dvc = Dep("vc")
    sg = ar.alloc([128, NOWN], BF16, "sg"); dsg = Dep("sg")
    tr.op("pool", lambda: nc.gpsimd.memset(sg, 0.0), [], [dsg])
    ulast = ar.alloc([128, 8, 32], F32, "ulast"); dulast = Dep("ulast")
    sfm = {}
    zcs = ar.alloc([128, 8, NST], BF16, "zcs"); sfm["zc"] = (zcs, Dep("zcs"))
    us = ar.alloc([128, 8, NST], F32, "us"); sfm["u"] = (us, Dep("us"))
    kvs = ar.alloc([NST, 1536], F32, "kvs"); sfm["kvs"] = (kvs, Dep("kvs"))
    ksT = ar.alloc([128, 12, NST], BF16, "ksT"); sfm["ksT"] = (ksT, Dep("ksT"))
    qs = ar.alloc([128, 8, NST], BF16, "qs"); sfm["q"] = (qs, Dep("qs"))
    sgs = ar.alloc([128, NST], BF16, "sgs"); sfm["sg"] = (sgs, Dep("sgs"))
    tr.op("pool", lambda: nc.gpsimd.memset(sgs, 0.0), [], [sfm["sg"][1]])
    zns = ar.alloc([128, 8, NST], BF16, "zns"); sfm["zn"] = (zns, Dep("zns"))
    mixTs = ar.alloc([128, 16, NST], BF16, "mixTs"); dmixs = Dep("mixTs")


    def phase_A():
        m = ar.mark()
        cT = ar.alloc([128, KC, 17], F32, "cT"); dcT = Dep("cT")
        scT = ar.alloc([128, KC, 17], F32, "scT"); dscT = Dep("scT")
        ada = ar.alloc([17, 3 * D], F32, "ada"); dada = Dep("ada")
        bb = ar.alloc([17, 3 * D], F32, "bb"); dbb = Dep("bb")
        npb = ar.alloc([17, D], F32, "npb"); dnpb = Dep("npb")
        wsl = [ar.alloc([128, KC, 512], F32, "wa%d" % i) for i in range(2)]
        dws = [Dep("wa%d" % i) for i in range(2)]
        tr.dma("sp", cT, cT_d, dcT)
        tr.dma("sp", bb, b_ada.partition_broadcast(17), dbb)
        tr.dma("sp", npb, npost_d.partition_broadcast(17), dnpb)
        act(scT, cT, AF.Silu, [dcT], [dscT])
        wv = w_ada.rearrange("(k p) c -> p k c", p=128)
        for blk in range(12):
            s = blk % 2
            for hk in range(2):
                tr.dma("sp", wsl[s][:, hk * 8:(hk + 1) * 8, :], wv[:, hk * 8:(hk + 1) * 8, blk * 512:(blk + 1) * 512], dws[s])
            pb = blk % 2
            for k in range(KC):
                mm(PS[pb][0:17, :], scT[:, k, :], wsl[s][:, k, :], k == 0, k == KC - 1, [dscT, dws[s]], [PSd[pb]])
            tt("dve", ada[:, blk * 512:(blk + 1) * 512], PS[pb][0:17, :], bb[:, blk * 512:(blk + 1) * 512], ALU.add,
               [PSd[pb], dbb], [dada])
        for part in range(2):
            pst = PS[2 + part][:, 0:KC * 17].rearrange("p (k r) -> p k r", k=KC)
            for k in range(KC):
                tp(pst[:, k, :], ada[0:17, part * D + k * 128: part * D + (k + 1) * 128], identf[0:17, 0:17],
                   [dada, CONST], [PSd[2 + part]])
            if part == 0:
                cp("dve", shT, pst, [PSd[2]], [dA1])
            else:
                stt("dve", A1T, pst, 1.0, npre.unsqueeze(2).to_broadcast([128, KC, 17]), ALU.add, ALU.mult,
                    [PSd[3], CONST], [dA1])
        tt("dve", G2[0:17, :], ada[:, 2 * D:3 * D], npb, ALU.mult, [dada, dnpb], [dG2])
        ar.release(m)
        tr.barrier()

    def make_hT(xd, ntok, dst_fn, ddst, rowsel, tmp):
        xt, dxt, xn, dxn, junk, djunk, st, dst_ = tmp
        tr.dma("sp", xt[0:ntok, :], xd, dxt)
        act(junk[0:ntok, :], xt[0:ntok, :], AF.Square, [dxt], [djunk, dst_], accum=st[0:ntok, 0:1])
        act(st[0:ntok, 1:2], st[0:ntok, 0:1], AF.Sqrt, [dst_], [dst_], bias=EPS, scale=1.0 / D)
        tr.op("dve", lambda: nc.vector.reciprocal(out=st[0:ntok, 2:3], in_=st[0:ntok, 1:2]), [dst_], [dst_])
        ts("dve", xn[0:ntok, :], xt[0:ntok, :], st[0:ntok, 2:3], None, ALU.mult, None, [dxt, dst_], [dxn])
        for hb in range(2):
            pb = 4 + hb
            pv = psbf(pb)[:, 0:8 * ntok].rearrange("p (k t) -> p k t", k=8)
            for kk in range(8):
                k = hb * 8 + kk
                tp(pv[:, kk, :], xn[0:ntok, k * 128:(k + 1) * 128], identb[0:ntok, 0:ntok], [dxn, CONST], [PSd[pb]])
            for kk in range(8):
                k = hb * 8 + kk
                if rowsel is not None:
                    if kk % 2 == 0:
                        act(dst_fn(k), pv[:, kk, :], AF.Identity, [PSd[pb], dA1], [ddst],
                            bias=shT[:, k, rowsel:rowsel + 1], scale=A1T[:, k, rowsel:rowsel + 1])
                    else:
                        ts("dve", dst_fn(k), pv[:, kk, :], A1T[:, k, rowsel:rowsel + 1], shT[:, k, rowsel:rowsel + 1],
                           ALU.mult, ALU.add, [PSd[pb], dA1], [ddst])
                else:
                    o3 = dst_fn(k).rearrange("p (s t) -> p s t", t=4)
                    i3 = pv[:, kk, :].rearrange("p (s t) -> p s t", t=4)
                    tt("dve", o3, i3, A1T[:, k, 0:NSEQ].unsqueeze(2).to_broadcast([128, NSEQ, 4]), ALU.mult,
                       [PSd[pb], dA1], [ddst])
                    tt("dve", o3, o3, shT[:, k, 0:NSEQ].unsqueeze(2).to_broadcast([128, NSEQ, 4]), ALU.add,
                       [dA1], [ddst])

    def xtmp2():
        xn = ar.alloc([128, D], BF16, "xn"); dxn = Dep("xn")
        junk, djunk = xn, dxn
        res = []
        for i in range(2):
            xt = ar.alloc([128, D], F32, "xt"); st = ar.alloc([128, 4], F32, "st")
            res.append((xt, Dep("xt"), xn, dxn, junk, djunk, st, Dep("st")))
        return res

    WCOLS = 256

    class WStream:
        def __init__(self):
            self.sl = [ar.alloc([128, KC, WCOLS], BF16, "wsl%d" % i) for i in range(2)]
            self.d = [Dep("wsl%d" % i) for i in range(2)]
            self.i = 0
            self.wv = w_in.rearrange("(k p) c -> p k c", p=128)

        def load(self, c0, ncols):
            s = self.i % 2
            self.i += 1
            for hk in range(2):
                tr.dma("pool", self.sl[s][:, hk * 8:(hk + 1) * 8, 0:ncols],
                       self.wv[:, hk * 8:(hk + 1) * 8, c0:c0 + ncols], self.d[s])
            return self.sl[s], self.d[s]

    fm_rr = [0]

    def fm_chunk(wt, dw, c0, ncol, rhs_fn, R, nt):
        pb = fm_rr[0] % 2
        fm_rr[0] += 1
        for k in range(KC):
            mm(PS[pb][0:ncol, 0:nt], wt[:, k, c0:c0 + ncol], rhs_fn(k), k == 0, k == KC - 1, [dw] + R, [PSd[pb]])
        return PS[pb][0:ncol, 0:nt], pb

    tm_rr = [0]

    def tm_block(wt, dw, ncols, lhs_fn, R, ntok):
        pb = 2 + tm_rr[0] % 2
        tm_rr[0] += 1
        for k in range(KC):
            mm(PS[pb][0:ntok, 0:ncols], lhs_fn(k), wt[:, k, 0:ncols], k == 0, k == KC - 1, [dw] + R, [PSd[pb]])
        return PS[pb][0:ntok, 0:ncols], pb

    phase_A()

    mixT = ar.alloc([128, 16, NOWN], BF16, "mixT")
    dmix = [[Dep("mix%d_%d" % (c, t)) for t in range(2)] for c in range(16)]
    qT = ar.alloc([128, 8, NOWN], BF16, "qT"); dq = [Dep("q%d" % h) for h in range(8)]
    kslcT = ar.alloc([128, 2, 2048], BF16, "kslcT"); dkslc = Dep("kslcT")
    kwinT = ar.alloc([128, 2, 1536], BF16, "kwinT"); dkwin = Dep("kwinT")
    vslc = ar.alloc([128, 16, 2, 128], BF16, "vslc"); dvslc = Dep("vslc")
    vwin = ar.alloc([128, 12, 2, 128], BF16, "vwin"); dvwin = Dep("vwin")
    ACC = qT
    dacc = [Dep("acc%d" % j) for j in range(8)]
    kcmpT = mixT[:, 0:4, :].rearrange("p (g a) t -> p g (a t)", g=2).rearrange("p g (ph m) -> p g ph m", ph=16)
    vcmpT = mixT[:, 4:8, :].rearrange("p (g a) t -> p g (a t)", g=2).rearrange("p g (ph m) -> p g ph m", ph=16)
    dkcmp = Dep("kcmpT"); dvcmp = Dep("vcmpT")
    PHASE = ar.mark()

    m0 = ar.mark()
    hT = ar.alloc([128, KC, 1024], BF16, "hT")
    dhT = [Dep("hT%d" % t) for t in range(8)]
    m1 = ar.mark()
    tmpx = xtmp2()
    make_hT(xs, NST, lambda k: hTs[:, k, :], dhTs, None, tmpx[0])
    for t in range(8):
        make_hT(xc[t * 128:(t + 1) * 128, :], 128, (lambda k, t=t: hT[:, k, t * 128:(t + 1) * 128]), dhT[t], 16, tmpx[(t + 1) % 2])
    cp("pool", hThalo, hT[:, :, 992:1024], [dhT[7]], [dhalo])
    ar.release(m1)
    tr.barrier()

    ws = WStream()
    stg = [ar.alloc([128, WCOLS], F32, "stg%d" % i) for i in range(2)]
    dstg = [Dep("stg%d" % i) for i in range(2)]
    stg_rr = [0]

    def hgrp(tg):
        return lambda k: hT[:, k, tg * 512:(tg + 1) * 512]

    def ctx_pass():
        for b in range(6):
            wt, dw = ws.load(C_KV + b * 256, 256)
            if b in (0, 1, 2, 4):
                dstT, dd = {0: (kcmpT, dkcmp), 1: (vcmpT, dvcmp), 2: (kslcT, dkslc), 4: (kwinT, dkwin)}[b]
                for g in range(2):
                    for tg in range(2):
                        if b == 4 and tg == 0:
                            continue
                        ps, pb = fm_chunk(wt, dw, g * 128, 128, hgrp(tg), [dhT[4 * tg + i] for i in range(4)], 512)
                        if b == 4:
                            o = dstT[:, g, 0:512]
                        elif b == 2:
                            o = dstT[:, g, tg * 512:(tg + 1) * 512]
                        else:
                            o = dstT[:, g, :, tg * 32:(tg + 1) * 32].rearrange("p ph m -> p m ph")
                            ps = ps.rearrange("p (m ph) -> p m ph", ph=16)
                        cp("act" if (g + tg) % 2 else "dve", o, ps, [PSd[pb]], [dd])
            else:
                dstV, dd = (vslc, dvslc) if b == 3 else (vwin, dvwin)
                for t in range(8):
                    if b == 5 and t < 4:
                        continue
                    ps, pb = tm_block(wt, dw, 256, (lambda k, t=t: hT[:, k, t * 128:(t + 1) * 128]), [dhT[t]], 128)
                    o = dstV[:, t, :, :] if b == 3 else dstV[:, t - 4, :, :]
                    cp("act" if t % 2 else "dve", o, ps.rearrange("p (g d) -> p g d", g=2), [PSd[pb]], [dd])

    ctx_pass()
    tr.barrier()

    m2 = ar.mark()
    tmpx = xtmp2()
    for t in range(8):
        make_hT(xo[t * 128:(t + 1) * 128, :], 128, (lambda k, t=t: hT[:, k, t * 128:(t + 1) * 128]), dhT[t], 16, tmpx[t % 2])
    ar.release(m2)
    tr.barrier()


    def own_pass():
        allh = lambda tg: [dhT[4 * tg + i] for i in range(4)]
        for b in range(4):
            wt, dw = ws.load(C_ZC + b * 256, 256)
            for cc in range(2):
                j = 2 * b + cc
                for tg in range(2):
                    ps, pb = fm_chunk(wt, dw, cc * 128, 128, hgrp(tg), allh(tg), 512)
                    act(mixT[:, 8 + j, tg * 512:(tg + 1) * 512], ps, AF.Silu, [PSd[pb]], [dmix[8 + j][tg]])
                ps, pb = fm_chunk(wt, dw, cc * 128, 128, lambda k: hTs[:, k, :], [dhTs], NST)
                act(zcs[:, j, :], ps, AF.Silu, [PSd[pb]], [sfm["zc"][1]])
        ubuf = [ar.alloc([128, 32 + NOWN], F32, "ubuf%d" % i) for i in range(2)]
        dub = [Dep("ubuf%d" % i) for i in range(2)]
        sgt = ar.alloc([128, 512], F32, "sgt"); dsgt = Dep("sgt")
        ubb = ar.alloc([128, 32 + NOWN], BF16, "ubb"); dubb = Dep("ubb")
        diagJ = ar.alloc([128, 31, 128], BF16, "diagJ"); ddiag = Dep("diagJ")
        cv_rr = [0]
        for b in range(4):
            wa, dwa = ws.load(C_A + b * 256, 256)
            wg, dwg = ws.load(C_GL + b * 256, 256)
            for cc in range(2):
                j = 2 * b + cc
                ub, du = ubuf[j % 2], dub[j % 2]
                segs = [(lambda k: hThalo[:, k, :], [dhalo], 32, ub[:, 0:32]),
                        (hgrp(0), allh(0), 512, ub[:, 32:544]),
                        (hgrp(1), allh(1), 512, ub[:, 544:1056]),
                        (lambda k: hTs[:, k, :], [dhTs], NST, us[:, j, :])]
                for si, (rf, R, nt, dsta) in enumerate(segs):
                    dd = du if si < 3 else sfm["u"][1]
                    ps, pb = fm_chunk(wa, dwa, cc * 128, 128, rf, R, nt)
                    cp("dve", dsta, ps, [PSd[pb]], [dd])
                    ps, pb = fm_chunk(wg, dwg, cc * 128, 128, rf, R, nt)
                    act(sgt[:, 0:nt], ps, AF.Sigmoid, [PSd[pb]], [dsgt])
                    tt("dve", dsta, dsta, sgt[:, 0:nt], ALU.mult, [dsgt], [dd])
                ts("dve", ub[:, 0:32], ub[:, 0:32], cval[:, 0:1], None, ALU.mult, None, [CONST], [du])
                cp("pool", ulast[:, j, :], ub[:, 1024:1056], [du], [dulast])
                cp("pool", ubb, ub[:, 0:1056], [du], [dubb])
                tt("pool", diagJ, identb.unsqueeze(1).to_broadcast([128, 31, 128]),
                   cdw[:, j, :].unsqueeze(2).to_broadcast([128, 31, 128]), ALU.mult, [CONST], [ddiag])
                for tg in range(2):
                    pb = 2 + cv_rr[0] % 2
                    cv_rr[0] += 1
                    for w in range(31):
                        mm(PS[pb], diagJ[:, w, :], ubb[:, 2 + w + tg * 512: 2 + w + tg * 512 + 512], w == 0, w == 30,
                           [ddiag, dubb], [PSd[pb]])
                    act(ACC[:, j, tg * 512:(tg + 1) * 512], PS[pb], AF.Identity, [PSd[pb], CONST], [dacc[j]], bias=cdb[:, j:j + 1])
        sq = ar.alloc([128, 512], BF16, "sq"); dsq = Dep("sq")
        mean = ar.alloc([128, 512], F32, "mean"); dmean = Dep("mean")
        rstd = ar.alloc([128, 512], F32, "rstd"); drstd = Dep("rstd")
        tmpn = ar.alloc([128, 512], F32, "tmpn"); dtmpn = Dep("tmpn")
        for tg in range(2):
            sl = slice(tg * 512, (tg + 1) * 512)
            for j in range(8):
                mm(PS[4], onesb, ACC[:, j, sl], j == 0, j == 7, [CONST, dacc[j]], [PSd[4]])
            for j in range(8):
                act(sq, ACC[:, j, sl], AF.Square, [dacc[j]], [dsq])
                mm(PS[5], onesb, sq, j == 0, j == 7, [CONST, dsq], [PSd[5]])
            ts("dve", mean, PS[4], 1.0 / 1024, None, ALU.mult, None, [PSd[4]], [dmean])
            tt("dve", tmpn, mean, mean, ALU.mult, [dmean], [dtmpn])
            stt("dve", tmpn, PS[5], 1.0 / 1024, tmpn, ALU.mult, ALU.subtract, [PSd[5]], [dtmpn])
            act(tmpn, tmpn, AF.Sqrt, [dtmpn], [dtmpn], bias=EPS, scale=1.0)
            tr.op("dve", lambda: nc.vector.reciprocal(out=rstd, in_=tmpn), [dtmpn], [drstd])
            for j in range(8):
                tt("dve", tmpn, ACC[:, j, sl], mean, ALU.subtract, [dacc[j], dmean], [dtmpn])
                tt("dve", tmpn, tmpn, rstd, ALU.mult, [drstd], [dtmpn])
                act(tmpn, tmpn, AF.Silu, [dtmpn], [dtmpn], bias=lnb[:, j:j + 1], scale=lng[:, j:j + 1])
                tt("dve", mixT[:, 8 + j, sl], tmpn, mixT[:, 8 + j, sl], ALU.mult, [dtmpn], [dmix[8 + j][tg]])

        for b in range(6):
            wt, dw = ws.load(C_KV + b * 256, 256)
            for t in range(8):
                ps, pb = tm_block(wt, dw, 256, (lambda k, t=t: hT[:, k, t * 128:(t + 1) * 128]), [dhT[t]], 128)
                si = stg_rr[0] % 2
                stg_rr[0] += 1
                cp("dve" if t % 2 else "act", stg[si], ps, [PSd[pb]], [dstg[si]])
                if b < 4:
                    tr.dma("sp", kv_p[t * 128:(t + 1) * 128, b * 256:(b + 1) * 256], stg[si], dstg[si], load=False)
                elif t >= 4:
                    tr.dma("sp", win_p[(t - 4) * 128:(t - 3) * 128, (b - 4) * 256:(b - 3) * 256], stg[si], dstg[si], load=False)
                if b == 3:
                    cp("pool", vslc[:, 8 + t, :, :], stg[si].rearrange("p (g d) -> p g d", g=2), [dstg[si]], [dvslc])
                if b == 5:
                    cp("pool", vwin[:, 4 + t, :, :], stg[si].rearrange("p (g d) -> p g d", g=2), [dstg[si]], [dvwin])
            ps, pb = tm_block(wt, dw, 256, lambda k: hTs[:, k, :], [dhTs], NST)
            cp("dve", kvs[:, b * 256:(b + 1) * 256], ps, [PSd[pb]], [sfm["kvs"][1]])
            if b in (0, 1, 2, 4):
                dstT, dd = {0: (kcmpT, dkcmp), 1: (vcmpT, dvcmp), 2: (kslcT, dkslc), 4: (kwinT, dkwin)}[b]
                for g in range(2):
                    for tg in range(2):
                        ps, pb = fm_chunk(wt, dw, g * 128, 128, hgrp(tg), allh(tg), 512)
                        if b == 4:
                            o = dstT[:, g, 512 + tg * 512: 512 + (tg + 1) * 512]
                        elif b == 2:
                            o = dstT[:, g, 1024 + tg * 512: 1024 + (tg + 1) * 512]
                        else:
                            o = dstT[:, g, :, 64 + tg * 32: 64 + (tg + 1) * 32].rearrange("p ph m -> p m ph")
                            ps = ps.rearrange("p (m ph) -> p m ph", ph=16)
                        cp("act" if (g + tg) % 2 else "dve", o, ps, [PSd[pb]], [dd])
            if b >= 2:
                for g in range(2):
                    ps, pb = fm_chunk(wt, dw, g * 128, 128, lambda k: hTs[:, k, :], [dhTs], NST)
                    cp("dve", ksT[:, 2 * b + g, :], ps, [PSd[pb]], [sfm["ksT"][1]])
            if b == 1:
                compress_prompt()
        tr.dma("sp", kv_s, kvs[:, 0:1024], sfm["kvs"][1], load=False)

        for b in range(4):
            wt, dw = ws.load(C_Q + b * 256, 256)
            for cc in range(2):
                h = 2 * b + cc
                for tg in range(2):
                    ps, pb = fm_chunk(wt, dw, cc * 128, 128, hgrp(tg), allh(tg), 512)
                    cp("act" if tg else "dve", qT[:, h, tg * 512:(tg + 1) * 512], ps, [PSd[pb]], [dq[h]] + dacc)
                ps, pb = fm_chunk(wt, dw, cc * 128, 128, lambda k: hTs[:, k, :], [dhTs], NST)
                cp("dve", qs[:, h, :], ps, [PSd[pb]], [sfm["q"][1]])
        wt, dw = ws.load(C_G, 24)
        for tg in range(2):
            ps, pb = fm_chunk(wt, dw, 0, 24, hgrp(tg), allh(tg), 512)
            act(sg[0:24, tg * 512:(tg + 1) * 512], ps, AF.Sigmoid, [PSd[pb]], [dsg])
        ps, pb = fm_chunk(wt, dw, 0, 24, lambda k: hTs[:, k, :], [dhTs], NST)
        act(sgs[0:24, :], ps, AF.Sigmoid, [PSd[pb]], [sfm["sg"][1]])
        for b in range(4):
            wt, dw = ws.load(C_ZN + b * 256, 256)
            for cc in range(2):
                j = 2 * b + cc
                for tg in range(2):
                    ps, pb = fm_chunk(wt, dw, cc * 128, 128, hgrp(tg), allh(tg), 512)
                    act(mixT[:, j, tg * 512:(tg + 1) * 512], ps, AF.Silu, [PSd[pb]], [dmix[j][tg], dkcmp, dvcmp])
                ps, pb = fm_chunk(wt, dw, cc * 128, 128, lambda k: hTs[:, k, :], [dhTs], NST)
                act(zns[:, j, :], ps, AF.Silu, [PSd[pb]], [sfm["zn"][1]])

    def bias1_setup():
        for s in range(2):
            for l in range(32):
                mm(PS[6][:, s:s + 1], w1[:, s, l, :], peT[:, s, l:l + 1], l == 0, l == 31, [CONSTP], [PSd[6]])
        tt("dve", bias1, PS[6][:, 0:2], b1T, ALU.add, [PSd[6], CONST], [dbias1])

    def compress(srcK, srcV, R, kc_out, vc_out, dkc_o, dvc_o, hbuf, dhbuf):
        for s, src in ((0, srcK), (1, srcV)):
            pv = PS[6][:, 0:254].rearrange("p (g m) -> p g m", g=2)
            for l in range(32):
                mm(pv, w1[:, s, l, :], src[:, :, l % 16, l // 16: l // 16 + 127], l == 0, l == 31, [CONSTP] + R, [PSd[6]])
            act(hbuf, pv, AF.Silu, [PSd[6], dbias1], [dhbuf], bias=bias1[:, s:s + 1])
            if s == 0:
                pk = PS[7][:, 0:254].rearrange("p (g m) -> p g m", g=2)
                mm(pk, w2[:, 0, :], hbuf, True, True, [CONSTP, dhbuf], [PSd[7]])
                act(kc_out, pk, AF.Identity, [PSd[7], CONST], [dkc_o], bias=b2T[:, 0:1])
            else:
                for g in range(2):
                    mm(PS[7][0:127, g * 128:(g + 1) * 128], hbuf[:, g, :], w2[:, 1, :], True, True, [CONSTP, dhbuf], [PSd[7]])
                tt("dve", vc_out, PS[7][0:127, 0:256].rearrange("p (g f) -> p g f", g=2),
                   b2v.unsqueeze(1).to_broadcast([127, 2, 128]), ALU.add, [PSd[7], CONST], [dvc_o])

    hbuf = ar.alloc([128, 2, 127], BF16, "hbuf"); dhbuf = Dep("hbuf")

    def compress_prompt():
        bias1_setup()
        compress(kcmpT, vcmpT, [dkcmp, dvcmp], kcT, vc, dkc, dvc, hbuf, dhbuf)

    own_pass()
    tr.barrier()
    ar.release(m0)

    CONST2 = Dep("const2")
    Esel = cload([128, 2048], BF16, E_d, dep=CONST2); ovl = cload([128, 32], F32, ovl_d, dep=CONST2)
    selg = cload([128, 24, 128], BF16, selg_d, dep=CONST2)
    mKeep = ar.mark()
    masks4 = cload([128, 4, 512], BF16, m4_d, dep=CONST2)
    negcmp = cload([127, 8, 512], BF16, negcmp_d, dep=CONST2)
    selb = cload([128, 8, 32], F32, selb_d, dep=CONST2); allow = cload([128, 8, 32], F32, allow_d, dep=CONST2)
    mcv = ar.mark()
    cvo = ar.alloc([32, 1024], F32, "cvo"); dcvo = Dep("cvo")
    for j in range(8):
        tp(PS[6 + j // 4][0:32, (j % 4) * 128:(j % 4 + 1) * 128], ulast[:, j, :], identf, [dulast, CONST], [PSd[6 + j // 4]])
    for hb in range(2):
        cp("dve", cvo[:, hb * 512:(hb + 1) * 512], PS[6 + hb][0:32, :], [PSd[6 + hb]], [dcvo])
    tr.dma("sp", conv_p, cvo, dcvo, load=False)


    def attn_prompt():
        Pb = [ar.alloc([128, 512], BF16, "Pb%d" % i) for i in range(2)]
        dPb = [Dep("Pb%d" % i) for i in range(2)]
        Pf = ar.alloc([128, 512], F32, "Pf"); dPf = Dep("Pf")
        pn = ar.alloc([128, 512], F32, "pn"); dpn = Dep("pn")
        psT = ar.alloc([128, 128], F32, "psT"); dpsT = Dep("psT")
        rsB = ar.alloc([128, 512], F32, "rsB"); drs = Dep("rsB")
        coef = ar.alloc([128, 512], F32, "coef"); dcoef = Dep("coef")
        tmpo = ar.alloc([128, 512], F32, "tmpo"); dtmpo = Dep("tmpo")
        oacc = ar.alloc([128, 512], F32, "oacc"); doacc = Dep("oacc")
        score = ar.alloc([128, 32], F32, "score"); dscore = Dep("score")
        work = ar.alloc([128, 32], F32, "work"); dwork = Dep("work")
        m8 = ar.alloc([128, 16], F32, "m8"); dm8 = Dep("m8")
        selt = ar.alloc([128, 32], F32, "selt"); dselt = Dep("selt")
        nsT4 = ar.alloc([128, 4, 128], BF16, "nsT4"); dnsT4 = Dep("nsT4")
        srr = [0]
        prr = [0]
        tr.op("pool", lambda: nc.gpsimd.memset(Pf, 0.0), [], [dPf])
        tr.op("pool", lambda: nc.gpsimd.memset(psT, 0.0), [], [dpsT])
        tr.op("pool", lambda: nc.gpsimd.memset(nsT4, 0.0), [], [dnsT4])

        accsel = [0]

        def acc_banks():
            return (2, 3) if accsel[0] % 2 == 0 else (6, 7)

        def finish_branch(br, first):
            bo, bs_ = acc_banks()
            accsel[0] += 1
            if first:
                ts("dve", rsB, PS[bs_], 1e-30, None, ALU.max, None, [PSd[bs_]], [drs])
                tr.op("dve", lambda: nc.vector.reciprocal(out=rsB, in_=rsB), [drs], [drs])
            else:
                act(rsB, PS[bs_], AF.Ln, [PSd[bs_]], [drs], bias=1e-18, scale=1.0)
                act(rsB, rsB, AF.Exp, [drs], [drs], scale=-1.0)
            tt("dve", coef, PS[4], rsB, ALU.mult, [PSd[4], drs], [dcoef])
            if first:
                tt("dve", oacc, PS[bo], coef, ALU.mult, [PSd[bo], dcoef], [doacc])
            else:
                tt("dve", tmpo, PS[bo], coef, ALU.mult, [PSd[bo], dcoef], [dtmpo])
                tt("pool", oacc, oacc, tmpo, ALU.add, [dtmpo], [doacc])

        def gates(br, g, qi):
            for r in range(4):
                mm(PS[4][:, r * 128:(r + 1) * 128], selg[:, br * 8 + 4 * g + r, :], sg[:, qi * 128:(qi + 1) * 128],
                   True, True, [CONST2, dsg], [PSd[4]])

        pend = [None]

        def pv_part(t):
            v_ap, pbi, Rk, first, last, npart = t
            bo, bs_ = acc_banks()
            mm(PS[bo], v_ap, Pb[pbi][0:npart, :], first, last, [dPb[pbi]] + Rk, [PSd[bo]])
            mm(PS[bs_], onesb[0:npart, :], Pb[pbi][0:npart, :], first, last, [dPb[pbi], CONST], [PSd[bs_]])

        def flush_pv():
            if pend[0] is not None:
                pv_part(pend[0])
                pend[0] = None

        def tile_attend(kT_ap, v_ap, masks, qrhs, Rk, first, last, npart=128):
            sb = srr[0] % 2; srr[0] += 1
            pbi = prr[0] % 2; prr[0] += 1
            S = PS[sb][0:npart, :]
            mm(S, kT_ap, qrhs, True, len(masks) == 0, Rk, [PSd[sb]])
            for mi, (ml, mr, Rm) in enumerate(masks):
                mm(S, ml, mr, False, mi == len(masks) - 1, Rm, [PSd[sb]])
            act(Pb[pbi][0:npart, :], S, AF.Exp, [PSd[sb]], [dPb[pbi]], scale=SCL)
            flush_pv()
            pend[0] = (v_ap, pbi, Rk, first, last, npart)

        for qi in range(8):
            qt = 8 + qi
            tgq = qi // 4
            for g in range(2):
                qrhs = qT[:, 4 * g:4 * g + 4, qi * 128:(qi + 1) * 128]
                Rq = [dq[4 * g + r] for r in range(4)]
                sb = srr[0] % 2; srr[0] += 1
                S = PS[sb][0:127, :]
                mm(S, kcT[:, g, :], qrhs, True, False, Rq + [dkc], [PSd[sb]])
                mm(S, identb[0:127, 0:127], negcmp[:, qi, :], False, True, [CONST, CONST2], [PSd[sb]])
                act(Pf[0:127, :], S, AF.Exp, [PSd[sb]], [dPf], scale=SCL)
                pbi = prr[0] % 2; prr[0] += 1
                cp("pool", Pb[pbi][0:127, :], Pf[0:127, :], [dPf], [dPb[pbi]])
                bo, bs_ = acc_banks()
                mm(PS[bs_], onesf, Pf, True, True, [dPf, CONST], [PSd[bs_]])
                mm(PS[bo], vc[:, g, :], Pb[pbi][0:127, :], True, True, [dPb[pbi], dvc], [PSd[bo]])
                gates(0, g, qi)
                finish_branch(0, True)
                def win_tiles(lo_, hi_):
                    for wi in range(lo_, hi_):
                        kt = qt - 4 + wi
                        w = kt - 4
                        masks = []
                        if wi == 0:
                            masks.append((identb, masks4[:, 3 if kt < 8 else 1, :], [CONST, CONST2]))
                        elif wi == 4:
                            masks.append((identb, masks4[:, 0, :], [CONST, CONST2]))
                        elif kt < 8:
                            masks.append((identb, masks4[:, 2, :], [CONST, CONST2]))
                        tile_attend(kwinT[:, g, w * 128:(w + 1) * 128], vwin[:, w, g, :], masks, qrhs,
                                    Rq + [dkwin, dvwin], wi == 0, wi == 4)

                tt("dve", pn[0:127, :], Pf[0:127, :], rsB[0:127, :], ALU.mult, [dPf, drs], [dpn])
                tr.op("dve", lambda: nc.vector.tensor_reduce(
                    out=psT[0:127, :], in_=pn[0:127, :].rearrange("p (r i) -> p i r", r=4), axis=AX.X, op=ALU.add),
                    [dpn], [dpsT])
                win_tiles(0, 3)
                mm(PS[5][:, 0:32], psT, ovl, True, True, [dpsT, CONST2], [PSd[5]])
                tt("dve", score, PS[5][:, 0:32], selb[:, qi, :], ALU.add, [PSd[5], CONST2], [dscore])
                tr.op("dve", lambda: nc.vector.max(out=m8[:, 0:8], in_=score), [dscore], [dm8])
                tr.op("dve", lambda: nc.vector.match_replace(out=work, in_to_replace=m8[:, 0:8], in_values=score,
                                                             imm_value=-3.0e38), [dm8, dscore], [dwork])
                tr.op("dve", lambda: nc.vector.max(out=m8[:, 8:16], in_=work), [dwork], [dm8])
                ts("dve", selt, score, m8[:, 15:16], None, ALU.is_ge, None, [dscore, dm8], [dselt])
                tt("dve", selt, selt, allow[:, qi, :], ALU.mult, [CONST2], [dselt])
                ts("dve", selt, selt, -1.0, BIG, ALU.add, ALU.mult, [], [dselt])
                win_tiles(3, 5)
                gates(2, g, qi)
                tp(PS[5][0:32, 128:256], selt, identf, [dselt, CONST], [PSd[5]])
                cp("act", nsT4[0:32], PS[5][0:32, 128:256].unsqueeze(1).to_broadcast([32, 4, 128]), [PSd[5]], [dnsT4])
                flush_pv()
                finish_branch(2, False)
                gates(1, g, qi)
                for kt in range(qt + 1):
                    masks = [(Esel[:, kt * 128:(kt + 1) * 128], nsT4, [CONST2, dnsT4])]
                    if kt == qt:
                        masks.append((identb, masks4[:, 0, :], [CONST, CONST2]))
                    tile_attend(kslcT[:, g, kt * 128:(kt + 1) * 128], vslc[:, kt, g, :], masks, qrhs,
                                Rq + [dkslc, dvslc], kt == 0, kt == qt)
                flush_pv()
                finish_branch(1, False)
                dm = [dmix[4 * g + r][tgq] for r in range(4)]
                mo = mixT[:, 4 * g:4 * g + 4, qi * 128:(qi + 1) * 128]
                tt("dve", mo, oacc.rearrange("p (r i) -> p r i", r=4), mo, ALU.mult, [doacc], dm)

    if dbg:
        for nm, ap_, dd in (("d_qT", qT, None), ("d_kslcT", kslcT, None),
                            ("d_kwinT", kwinT, None), ("d_kcT", kcT, None), ("d_vc", vc, None),
                            ("d_vslc", vslc, None), ("d_vwin", vwin, None)):
            shp = list(ap_.shape)
            o = dout(nm, shp, BF16)
            alld = [d for row in dmix for d in row] + dq + [dkslc, dkwin, dkc, dvc, dsg, dvslc, dvwin]
            tr.dma("sp", o, ap_, Dep(nm), load=False, extraR=alld)

    mF = ar.mark()
    attn_prompt()
    tr.barrier()
    ar.release(mF)


    def sample_phase():
        nonlocal ar
        SOFF = ar.offs["qT"]
        assert SOFF + 44 * 1024 <= m0
        ar.top = mKeep
        CS = Dep("const_s")
        sbias = cload([128, 32], F32, sbias_d, dep=CS); mnew = cload([128, 16], BF16, mnew_d, dep=CS)
        mwin0 = cload([128, 16], BF16, mwin0_d, dep=CS)
        perm = cload([128, 128], BF16, perm_d, dep=CS)
        pti = ar.alloc([128, NSEQ * 16], I32, "pti"); dpti = Dep("pti")
        iop = ar.alloc([128, 1], I32, "iop"); diop = Dep("iop")
        idx = ar.alloc([128, NSEQ * 16], I32, "idx"); didx = Dep("idx")
        tr.dma("sp", pti, ptab_d.rearrange("(o s) p -> o (s p)", o=1).partition_broadcast(128), dpti)
        tr.op("pool", lambda: nc.gpsimd.iota(iop, pattern=[[0, 1]], base=0, channel_multiplier=1), [], [diop])
        ts("dve", idx, pti, 128, iop[:, 0:1], ALU.mult, ALU.add, [dpti, diop], [didx])
        pgs = [ar.alloc_at(SOFF, [128, 16, 1024], BF16, "pg0"), ar.alloc([128, 16, 1024], BF16, "pg1")]
        dpgs = [Dep("pg0"), Dep("pg1")]
        kcmpS = ar.alloc_at(SOFF + 32768, [128, 2, 16, 128], BF16, "kcmpS"); dkcmpS = Dep("kcmpS")
        vcmpS = ar.alloc([128, 2, 16, 128], BF16, "vcmpS"); dvcmpS = Dep("vcmpS")
        kslcS = ar.alloc([128, 2, 2048], BF16, "kslcS"); dkslcS = Dep("kslcS")
        wps = [ar.alloc_at(SOFF + 40960, [128, 4, 512], BF16, "wp0"), ar.alloc([128, 4, 512], BF16, "wp1")]
        dwps = [Dep("wp0"), Dep("wp1")]
        kwS = ar.alloc([128, 2, 512], BF16, "kwS"); dkwS = Dep("kwS")
        vnew = ar.alloc([128, 4, 128], BF16, "vnew"); dvnew = Dep("vnew")
        kcS = ar.alloc([128, 2, 127], BF16, "kcS"); dkcS = Dep("kcS")
        vcS = ar.alloc([127, 2, 128], BF16, "vcS"); dvcS = Dep("vcS")
        hb2 = ar.alloc([128, 2, 127], BF16, "hb2"); dhb2 = Dep("hb2")
        PfS = ar.alloc([128, 32], F32, "PfS"); dPfS = Dep("PfS")
        PbS = ar.alloc([128, 32], BF16, "PbS"); dPbS = Dep("PbS")
        pnS = ar.alloc([128, 32], F32, "pnS"); dpnS = Dep("pnS")
        psTS = ar.alloc([128, 8], F32, "psTS"); dpsTS = Dep("psTS")
        rsS = ar.alloc([128, 32], F32, "rsS"); drsS = Dep("rsS")
        coefS = ar.alloc([128, 32], F32, "coefS"); dcoefS = Dep("coefS")
        tmpS = ar.alloc([128, 32], F32, "tmpS"); dtmpS = Dep("tmpS")
        accS = ar.alloc([128, 32], F32, "accS"); daccS = Dep("accS")
        scS = ar.alloc([128, 32], F32, "scS"); dscS = Dep("scS")
        wkS = ar.alloc([128, 32], F32, "wkS"); dwkS = Dep("wkS")
        m8S = ar.alloc([128, 16], F32, "m8S"); dm8S = Dep("m8S")
        selS = ar.alloc([128, 32], F32, "selS"); dselS = Dep("selS")
        nsS = ar.alloc([128, 2, 4, 4], BF16, "nsS"); dnsS = Dep("nsS")
        Pk = ar.alloc([128, 17, 16], BF16, "Pk"); dPk = Dep("Pk")
        gB = ar.alloc([128, 24, NST], F32, "gB"); dgB = Dep("gB")
        for t_, d_ in ((PfS, dPfS), (psTS, dpsTS), (nsS, dnsS), (Pk, dPk), (vnew, dvnew), (scS, dscS), (selS, dselS)):
            tr.op("pool", lambda t_=t_: nc.gpsimd.memset(t_, 0.0), [], [d_])
        dsgs = sfm["sg"][1]; dqs = sfm["q"][1]; dks = sfm["ksT"][1]
        for k in range(24):
            pb = k // 8
            mm(PS[pb][:, (k % 8) * 64:(k % 8 + 1) * 64], selg[:, k, :], sgs, True, True, [CONST2, dsgs], [PSd[pb]])
            if k % 8 == 7:
                cp("dve", gB[:, pb * 8:(pb + 1) * 8, :], PS[pb].rearrange("p (k t) -> p k t", k=8), [PSd[pb]], [dgB])
        trr = [0]
        srr = [0]

        def trbank():
            b = trr[0] % 2; trr[0] += 1
            return b

        def sbank():
            b = 2 + srr[0] % 2; srr[0] += 1
            return b

        def branch_finish(g, br, first, s):
            sl = slice(g * 16, (g + 1) * 16)
            ts("dve", rsS[:, sl], PS[5][:, 0:16], 1e-30, None, ALU.max, None, [PSd[5]], [drsS])
            tr.op("dve", lambda: nc.vector.reciprocal(out=rsS[:, sl], in_=rsS[:, sl]), [drsS], [drsS])
            gate = gB[:, br * 8 + 4 * g: br * 8 + 4 * g + 4, 4 * s:4 * s + 4]
            tt("dve", coefS[:, sl].rearrange("p (r t) -> p r t", r=4), rsS[:, sl].rearrange("p (r t) -> p r t", r=4), gate,
               ALU.mult, [drsS, dgB], [dcoefS])
            tt("dve", tmpS[:, sl], PS[4][:, 0:16], coefS[:, sl], ALU.mult, [PSd[4], dcoefS], [dtmpS])
            tt("dve", accS[:, sl], accS[:, sl], tmpS[:, sl], ALU.add, [dtmpS], [daccS])

        def prefetch(s):
            for p_ in range(16):
                tr.idma(pgs[s % 2][:, p_, :], cache, idx[:, s * 16 + p_: s * 16 + p_ + 1], dpgs[s % 2], extraR=[didx])
            tr.dma("pool", wps[s % 2], swin[s].rearrange("(a p) c -> p a c", p=128), dwps[s % 2])

        prefetch(0)
        for s in range(NSEQ):
            if s + 1 < NSEQ:
                prefetch(s + 1)
            pg, dpg, wp, dwp = pgs[s % 2], dpgs[s % 2], wps[s % 2], dwps[s % 2]
            for slot, (dstT, dd) in enumerate(((kcmpS, dkcmpS), (vcmpS, dvcmpS), (kslcS, dkslcS))):
                for g in range(2):
                    if slot < 2:
                        for pgp in range(4):
                            b = trbank()
                            for a in range(4):
                                mm(PS[b][:, a * 128:(a + 1) * 128], pg[:, pgp * 4 + a, slot * 256 + g * 128: slot * 256 + (g + 1) * 128],
                                   perm, True, True, [dpg, CS], [PSd[b]])
                            cp("act" if (g + pgp) % 2 else "dve",
                               dstT[:, g, :, pgp * 32:(pgp + 1) * 32].rearrange("p ph (a m) -> p ph a m", a=4),
                               PS[b].rearrange("p (a ph m) -> p ph a m", a=4, ph=16), [PSd[b]], [dd])
                    else:
                        for hb in range(2):
                            b = trbank()
                            pv = psbf(b).rearrange("p (a t) -> p a t", a=8)
                            for a in range(8):
                                tp(pv[:, a, :], pg[:, hb * 8 + a, slot * 256 + g * 128: slot * 256 + (g + 1) * 128], identb,
                                   [dpg, CONST], [PSd[b]])
                            cp("act" if (g + hb) % 2 else "dve", dstT[:, g, hb * 1024:(hb + 1) * 1024], psbf(b), [PSd[b]], [dd])
            b = trbank()
            pv = psbf(b).rearrange("p (g a t) -> p g a t", g=2, a=4)
            for g in range(2):
                for a in range(4):
                    tp(pv[:, g, a, :], wp[:, a, g * 128:(g + 1) * 128], identb, [dwp, CONST], [PSd[b]])
            cp("dve", kwS, psbf(b).rearrange("p (g t) -> p g t", g=2), [PSd[b]], [dkwS])
            b = trbank()
            for c4 in range(4):
                src_chunk = (6 + c4) if c4 < 2 else (10 + c4 - 2)
                tp(psbf(b)[0:4, c4 * 128:(c4 + 1) * 128], ksT[:, src_chunk, 4 * s:4 * s + 4], identb, [dks, CONST], [PSd[b]])
            cp("dve", vnew[0:4], psbf(b)[0:4, 0:512].rearrange("p (c d) -> p c d", c=4), [PSd[b]], [dvnew])
            compress(kcmpS, vcmpS, [dkcmpS, dvcmpS], kcS, vcS, dkcS, dvcS, hb2, dhb2)
            sb = sbank()
            for g in range(2):
                mm(PS[sb][0:127, g * 16:(g + 1) * 16], kcS[:, g, :], qs[:, 4 * g:4 * g + 4, 4 * s:4 * s + 4], True, True,
                   [dkcS, dqs], [PSd[sb]])
            act(PfS[0:127, :], PS[sb][0:127, 0:32], AF.Exp, [PSd[sb]], [dPfS], scale=SCL)
            cp("pool", PbS[0:127, :], PfS[0:127, :], [dPfS], [dPbS])
            mm(PS[5][:, 0:32], onesf, PfS, True, True, [dPfS, CONST], [PSd[5]])
            for g in range(2):
                mm(PS[4][:, g * 16:(g + 1) * 16], vcS[:, g, :], PbS[0:127, g * 16:(g + 1) * 16], True, True, [dPbS, dvcS], [PSd[4]])
            ts("dve", rsS, PS[5][:, 0:32], 1e-30, None, ALU.max, None, [PSd[5]], [drsS])
            tr.op("dve", lambda: nc.vector.reciprocal(out=rsS, in_=rsS), [drsS], [drsS])
            tt("dve", coefS.rearrange("p (h t) -> p h t", h=8), rsS.rearrange("p (h t) -> p h t", h=8),
               gB[:, 0:8, 4 * s:4 * s + 4], ALU.mult, [drsS, dgB], [dcoefS])
            tt("dve", accS, PS[4][:, 0:32], coefS, ALU.mult, [PSd[4], dcoefS], [daccS])
            tt("dve", pnS[0:127, :], PfS[0:127, :], rsS[0:127, :], ALU.mult, [dPfS, drsS], [dpnS])
            tr.op("dve", lambda: nc.vector.tensor_reduce(
                out=psTS[0:127, :].rearrange("p (g t) -> p g t", g=2),
                in_=pnS[0:127, :].rearrange("p (g r t) -> p g t r", g=2, r=4), axis=AX.X, op=ALU.add), [dpnS], [dpsTS])
            sb = sbank()
            mm(PS[sb][0:8, 0:32], psTS, ovl, True, True, [dpsTS, CONST2], [PSd[sb]])
            tt("dve", scS[0:8, :], PS[sb][0:8, 0:32], sbias[0:8, :], ALU.add, [PSd[sb], CS], [dscS])
            tr.op("dve", lambda: nc.vector.max(out=m8S[0:8, 0:8], in_=scS[0:8, :]), [dscS], [dm8S])
            tr.op("dve", lambda: nc.vector.match_replace(out=wkS[0:8, :], in_to_replace=m8S[0:8, 0:8], in_values=scS[0:8, :],
                                                         imm_value=-3.0e38), [dm8S, dscS], [dwkS])
            tr.op("dve", lambda: nc.vector.max(out=m8S[0:8, 8:16], in_=wkS[0:8, :]), [dwkS], [dm8S])
            ts("dve", selS[0:8, :], scS[0:8, :], m8S[0:8, 14:15], None, ALU.is_ge, None, [dscS, dm8S], [dselS])
            ts("dve", selS[0:8, :], selS[0:8, :], -1.0, BIG, ALU.add, ALU.mult, [], [dselS])
            sb = sbank()
            tp(PS[sb][0:32, 0:8], selS[0:8, :], identf[0:8, 0:8], [dselS, CONST], [PSd[sb]])
            cp("dve", nsS[0:32], PS[sb][0:32, 0:8].rearrange("p (g t) -> p g t", g=2).unsqueeze(2).to_broadcast([32, 2, 4, 4]),
               [PSd[sb]], [dnsS])
            for g in range(2):
                qg = qs[:, 4 * g:4 * g + 4, 4 * s:4 * s + 4]
                sb = sbank()
                for kt in range(16):
                    mm(PS[sb][:, kt * 16:(kt + 1) * 16], kslcS[:, g, kt * 128:(kt + 1) * 128], qg, True, False, [dkslcS, dqs], [PSd[sb]])
                    mm(PS[sb][:, kt * 16:(kt + 1) * 16], Esel[:, kt * 128:(kt + 1) * 128], nsS[:, g], False, True, [CONST2, dnsS], [PSd[sb]])
                mm(PS[sb][0:4, 256:272], ksT[:, 4 + g, 4 * s:4 * s + 4], qg, True, False, [dks, dqs], [PSd[sb]])
                mm(PS[sb][0:4, 256:272], identb[:, 0:4], mnew, False, True, [CONST, CS], [PSd[sb]])
                act(Pk[:, 0:16, :], PS[sb][:, 0:256].rearrange("p (k c) -> p k c", k=16), AF.Exp, [PSd[sb]], [dPk], scale=SCL)
                act(Pk[0:4, 16, :], PS[sb][0:4, 256:272], AF.Exp, [PSd[sb]], [dPk], scale=SCL)
                for kt in range(17):
                    va = pg[:, kt, 768 + g * 128: 768 + (g + 1) * 128] if kt < 16 else vnew[:, g, :]
                    mm(PS[4][:, 0:16], va, Pk[:, kt, :], kt == 0, kt == 16, [dpg, dvnew, dPk], [PSd[4]])
                for kt in range(17):
                    mm(PS[5][:, 0:16], onesb, Pk[:, kt, :], kt == 0, kt == 16, [CONST, dPk], [PSd[5]])
                branch_finish(g, 1, False, s)
                sb = sbank()
                for a in range(4):
                    mm(PS[sb][:, a * 16:(a + 1) * 16], kwS[:, g, a * 128:(a + 1) * 128], qg, True, a != 0, [dkwS, dqs], [PSd[sb]])
                    if a == 0:
                        mm(PS[sb][:, 0:16], identb, mwin0, False, True, [CONST, CS], [PSd[sb]])
                mm(PS[sb][0:4, 256:272], ksT[:, 8 + g, 4 * s:4 * s + 4], qg, True, False, [dks, dqs], [PSd[sb]])
                mm(PS[sb][0:4, 256:272], identb[:, 0:4], mnew, False, True, [CONST, CS], [PSd[sb]])
                act(Pk[:, 0:4, :], PS[sb][:, 0:64].rearrange("p (k c) -> p k c", k=4), AF.Exp, [PSd[sb]], [dPk], scale=SCL)
                act(Pk[0:4, 16, :], PS[sb][0:4, 256:272], AF.Exp, [PSd[sb]], [dPk], scale=SCL)
                for a in range(5):
                    va = wp[:, a, 256 + g * 128: 256 + (g + 1) * 128] if a < 4 else vnew[:, 2 + g, :]
                    pk = Pk[:, a, :] if a < 4 else Pk[:, 16, :]
                    mm(PS[4][:, 0:16], va, pk, a == 0, a == 4, [dwp, dvnew, dPk], [PSd[4]])
                for a in range(5):
                    pk = Pk[:, a, :] if a < 4 else Pk[:, 16, :]
                    mm(PS[5][:, 0:16], onesb, pk, a == 0, a == 4, [CONST, dPk], [PSd[5]])
                branch_finish(g, 2, False, s)
            tt("dve", mixTs[:, 0:8, 4 * s:4 * s + 4], accS.rearrange("p (h t) -> p h t", h=8), zns[:, :, 4 * s:4 * s + 4],
               ALU.mult, [daccS, sfm["zn"][1]], [dmixs])

        tr.barrier()
        ar_main = ar
        ar = Arena(nc, lo=SOFF, hi=m0)
        ucat = ar.alloc([128, 8, NSEQ, 34], F32, "ucat"); ducat = Dep("ucat")
        sct = [ar.alloc([30, 1024], F32, "sct%d" % i) for i in range(2)]; dsct = [Dep("sct%d" % i) for i in range(2)]
        dus = sfm["u"][1]
        for s in range(NSEQ):
            st_, dst2 = sct[s % 2], dsct[s % 2]
            tr.dma("sp", st_, sconv[s], dst2)
            b = s % 2
            for j in range(8):
                tp(PS[b][:, j * 30:(j + 1) * 30], st_[:, j * 128:(j + 1) * 128], identf[0:30, 0:30], [dst2, CONST], [PSd[b]])
            cp("dve" if s % 2 else "act", ucat[:, :, s, 0:30], PS[b][:, 0:240].rearrange("p (j t) -> p j t", j=8), [PSd[b]], [ducat])
        cp("dve", ucat[:, :, :, 30:34], us.rearrange("p j (s t) -> p j s t", t=4), [dus], [ducat])
        cacS = ar.alloc([128, 8, NSEQ, 4], F32, "cacS"); dcacS = Dep("cacS")
        ctmp = ar.alloc([128, 8, NSEQ, 4], F32, "ctmp"); dctmp = Dep("ctmp")
        accb = ar.alloc([128, 8, NST], BF16, "accb"); daccb = Dep("accb")

        def bc(ap2):
            return ap2.unsqueeze(2).unsqueeze(3).to_broadcast([128, 8, NSEQ, 4])

        for w in range(31):
            src = ucat[:, :, :, w:w + 4]
            if w == 0:
                tt("dve", cacS, src, bc(cdw[:, :, 0]), ALU.mult, [ducat, CONST], [dcacS])
                tt("dve", cacS, cacS, bc(cdb), ALU.add, [CONST], [dcacS])
            else:
                tt("dve", ctmp, src, bc(cdw[:, :, w]), ALU.mult, [ducat, CONST], [dctmp])
                tt("dve", cacS, cacS, ctmp, ALU.add, [dctmp], [dcacS])
        cp("dve", accb, cacS.rearrange("p j s t -> p j (s t)"), [dcacS], [daccb])
        sqS = ar.alloc([128, NST], BF16, "sqS"); dsqS = Dep("sqS")
        meanS = ar.alloc([128, NST], F32, "meanS"); dmeanS = Dep("meanS")
        rstdS = ar.alloc([128, NST], F32, "rstdS"); drstdS = Dep("rstdS")
        tnS = ar.alloc([128, NST], F32, "tnS"); dtnS = Dep("tnS")
        for j in range(8):
            mm(PS[6][:, 0:NST], onesb, accb[:, j, :], j == 0, j == 7, [CONST, daccb], [PSd[6]])
        for j in range(8):
            act(sqS, accb[:, j, :], AF.Square, [daccb], [dsqS])
            mm(PS[7][:, 0:NST], onesb, sqS, j == 0, j == 7, [CONST, dsqS], [PSd[7]])
        ts("dve", meanS, PS[6][:, 0:NST], 1.0 / 1024, None, ALU.mult, None, [PSd[6]], [dmeanS])
        tt("dve", tnS, meanS, meanS, ALU.mult, [dmeanS], [dtnS])
        stt("dve", tnS, PS[7][:, 0:NST], 1.0 / 1024, tnS, ALU.mult, ALU.subtract, [PSd[7]], [dtnS])
        act(tnS, tnS, AF.Sqrt, [dtnS], [dtnS], bias=EPS, scale=1.0)
        tr.op("dve", lambda: nc.vector.reciprocal(out=rstdS, in_=tnS), [dtnS], [drstdS])
        for j in range(8):
            tt("dve", tnS, accb[:, j, :], meanS, ALU.subtract, [daccb, dmeanS], [dtnS])
            tt("dve", tnS, tnS, rstdS, ALU.mult, [drstdS], [dtnS])
            act(tnS, tnS, AF.Silu, [dtnS], [dtnS], bias=lnb[:, j:j + 1], scale=lng[:, j:j + 1])
            tt("dve", mixTs[:, 8 + j, :], tnS, zcs[:, j, :], ALU.mult, [dtnS, sfm["zc"][1]], [dmixs])
        dd1 = Dep("o_conv_s"); dd2 = Dep("o_win_s")
        tr.dma("sp", conv_s[:, 0:26, :], sconv[:, 4:30, :], dd1)
        tr.dma("sp", win_s[:, 0:508, :], swin[:, 4:512, :], dd2)
        utm = ar.alloc([NST, 1024], F32, "utm"); dutm = Dep("utm")
        for j in range(8):
            tp(PS[j // 4][0:NST, (j % 4) * 128:(j % 4 + 1) * 128], us[:, j, :], identf, [dus, CONST], [PSd[j // 4]])
        for hb in range(2):
            cp("dve", utm[:, hb * 512:(hb + 1) * 512], PS[hb][0:NST, :], [PSd[hb]], [dutm])
        tr.dma("sp", conv_s[:, 26:30, :], utm, dutm, load=False)
        tr.dma("sp", win_s[:, 508:512, :], kvs[:, 1024:1536], sfm["kvs"][1], load=False)
        ar = ar_main

    sample_phase()
    tr.barrier()

    WOFF = ar.offs["qT"]
    wo = ar.alloc_at(WOFF, [128, KC, D], BF16, "wo"); dwo = Dep("wo")
    GTOP = WOFF + KC * D * 2
    wov = w_out.rearrange("(k p) c -> p k c", p=128)
    for cb in range(4):
        for hk in range(2):
            tr.dma("pool", wo[:, hk * 8:(hk + 1) * 8, cb * 512:(cb + 1) * 512],
                   wov[:, hk * 8:(hk + 1) * 8, cb * 512:(cb + 1) * 512], dwo)
    ar.top = GTOP
    G2p = ar.alloc([128, D], F32, "G2p"); dG2p = Dep("G2p")
    for cb in range(4):
        mm(PS[cb], rp, G2[:, cb * 512:(cb + 1) * 512], True, True, [CONST, dG2], [PSd[cb]])
        cp("dve", G2p[:, cb * 512:(cb + 1) * 512], PS[cb], [PSd[cb]], [dG2p])
    mixo = [ar.alloc([128, D], F32, "mixo%d" % i) for i in range(2)]; dmixo = [Dep("mixo%d" % i) for i in range(2)]
    xr = [ar.alloc([128, D], F32, "xr%d" % i) for i in range(2)]; dxr = [Dep("xr%d" % i) for i in range(2)]
    sso = ar.alloc([128, 8], F32, "sso"); dsso = Dep("sso")
    jk = ar.alloc([128, 512], BF16, "jk"); djk = Dep("jk")

    def out_proj(ntok, lhs_fn, Rl, G2x, dG2x, xsrc, ydst, it):
        mo, dmo = mixo[it % 2], dmixo[it % 2]
        xt_, dxt_ = xr[it % 2], dxr[it % 2]
        tr.dma("sp", xt_[0:ntok, :], xsrc, dxt_)
        for cb in range(4):
            pb = (it % 2) * 4 + cb
            for k in range(KC):
                mm(PS[pb][0:ntok, :], lhs_fn(k), wo[:, k, cb * 512:(cb + 1) * 512], k == 0, k == KC - 1, Rl + [dwo], [PSd[pb]])
            act(jk[0:ntok, :], PS[pb][0:ntok, :], AF.Square, [PSd[pb]], [djk, dsso], accum=sso[0:ntok, cb:cb + 1])
            cp("dve", mo[0:ntok, cb * 512:(cb + 1) * 512], PS[pb][0:ntok, :], [PSd[pb], djk], [dmo])
        tr.op("dve", lambda: nc.vector.tensor_reduce(out=sso[0:ntok, 4:5], in_=sso[0:ntok, 0:4], axis=AX.X, op=ALU.add),
              [dsso], [dsso])
        act(sso[0:ntok, 5:6], sso[0:ntok, 4:5], AF.Sqrt, [dsso], [dsso], bias=EPS, scale=1.0 / D)
        tr.op("dve", lambda: nc.vector.reciprocal(out=sso[0:ntok, 6:7], in_=sso[0:ntok, 5:6]), [dsso], [dsso])
        stt("dve", mo[0:ntok, :], mo[0:ntok, :], sso[0:ntok, 6:7], G2x[0:ntok, :], ALU.mult, ALU.mult, [dsso, dG2x], [dmo])
        tt("pool", mo[0:ntok, :], mo[0:ntok, :], xt_[0:ntok, :], ALU.add, [dxt_], [dmo])
        tr.dma("sp", ydst, mo[0:ntok, :], dmo, load=False)

    G2s = ar.alloc([NST, D], F32, "G2s"); dG2s = Dep("G2s")
    for cb in range(4):
        mm(PS[cb][0:NST, :], rsel, G2[:, cb * 512:(cb + 1) * 512], True, True, [CONST, dG2], [PSd[cb]])
        cp("dve", G2s[:, cb * 512:(cb + 1) * 512], PS[cb][0:NST, :], [PSd[cb]], [dG2s])
    out_proj(NST, (lambda k: mixTs[:, k, :]), [dmixs], G2s, dG2s, xs, y_s, 8)
    for t in range(8):
        out_proj(128, (lambda k, t=t: mixT[:, k, t * 128:(t + 1) * 128]), [dmix[c][t // 4] for c in range(16)],
                 G2p, dG2p, xo[t * 128:(t + 1) * 128, :], y_p[t * 128:(t + 1) * 128, :], t)

    if dbg:
        o = dout("d_mixT", list(mixT.shape), BF16)
        tr.dma("sp", o, mixT, Dep("d_mixT"), load=False, extraR=[d for row in dmix for d in row])

    tr.finish()
    return nc


def _consts(half):
    v = float(half)
    c = {}
    c["ident_bf"] = np.eye(128, dtype=np.float32).astype(ml_dtypes.bfloat16)
    c["ident_f"] = np.eye(128, dtype=np.float32)
    c["ones_bf"] = np.ones((128, 128), np.float32).astype(ml_dtypes.bfloat16)
    c["ones_f"] = np.ones((128, 128), np.float32)
    j = np.arange(128)[:, None]
    i = np.arange(128)[None, :]
    negtri = np.where(j <= i, 0.0, -BIG)
    neglow = np.where(j >= i, 0.0, -BIG)
    ctxmid = np.full((128, 128), -BIG * (1 - v))
    ctxlow = np.minimum(neglow, ctxmid) if v == 0 else neglow
    m4 = np.stack([np.tile(m[:, None, :], (1, 4, 1)).reshape(128, 512) for m in (negtri, neglow, ctxmid, ctxlow)], axis=1)
    c["masks4"] = m4.astype(np.float32).astype(ml_dtypes.bfloat16)
    m = np.arange(127)[:, None, None]
    qi = np.arange(8)[None, :, None]
    ii = np.arange(128)[None, None, :]
    valid = (16 * m + 31 <= 1024 + 128 * qi + ii) & (m >= 64 * (1 - half))
    ncm = np.where(valid, 0.0, -BIG)
    c["negcmp"] = np.tile(ncm[:, :, None, :], (1, 1, 4, 1)).reshape(127, 8, 512).astype(np.float32).astype(ml_dtypes.bfloat16)
    s = np.arange(128)[:, None]
    key = np.arange(2048)[None, :]
    c["Esel"] = (key // 64 == s).astype(np.float32).astype(ml_dtypes.bfloat16)
    mm_ = np.arange(127)[:, None] * 16
    sj = np.arange(32)[None, :] * 64
    c["overlap"] = np.concatenate([((mm_ < sj + 64) & (mm_ + 32 > sj)).astype(np.float32), np.zeros((1, 32), np.float32)], 0)
    tl = 1024 + 128 * np.arange(8)[None, :, None] + np.arange(128)[:, None, None]
    sb = np.arange(32)[None, None, :]
    s0 = 16 * (1 - half)
    allowed = (64 * sb <= tl) & (sb >= s0)
    cur = tl // 64
    forced = (sb == s0) | (sb == cur) | (sb == cur - 1)
    c["selbias"] = np.where(allowed, np.where(forced, 1e6, 0.0), -1e30).astype(np.float32)
    c["allowed"] = allowed.astype(np.float32)
    sg_ = np.zeros((128, 24, 128), np.float32)
    for k in range(24):
        sg_[k, k, :] = 1.0
    c["selg"] = sg_.astype(ml_dtypes.bfloat16)
    rp = np.zeros((128, 128), np.float32); rp[16, :] = 1.0
    rs = np.zeros((128, 64), np.float32)
    for s_ in range(16):
        rs[s_, 4 * s_:4 * s_ + 4] = 1.0
    c["rp"] = rp; c["rs"] = rs
    c["cval"] = np.full((128, 1), v, np.float32)
    sb_ = np.zeros((128, 32), np.float32); sb_[:, 0] = 1e6; sb_[:, 31] = 1e6
    c["sbias"] = sb_
    pm = np.zeros((128, 128), np.float32)
    for mp in range(8):
        for ph in range(16):
            pm[16 * mp + ph, ph * 8 + mp] = 1.0
    c["perm16"] = pm.astype(ml_dtypes.bfloat16)
    tq = np.arange(16)[None, :] % 4
    jp = np.arange(128)[:, None]
    c["mnew"] = np.where((jp < 4) & (jp > tq), -BIG, 0.0).astype(np.float32).astype(ml_dtypes.bfloat16)
    c["mwin0"] = np.where(jp < tq, -BIG, 0.0).astype(np.float32).astype(ml_dtypes.bfloat16)
    return c


def fm16(vec):
    return np.ascontiguousarray(vec.reshape(-1, 128).T)


def core_inputs(inp, core, nb_prompt, seqs):
    b, half = core // 2, core % 2
    f = lambda a: np.ascontiguousarray(np.asarray(a, dtype=np.float32))
    xp = np.asarray(inp["x_prompt"])[b]
    m = {}
    m["xo"] = f(xp[half * 1024:(half + 1) * 1024])
    m["xc"] = f(xp[0:1024])
    m["xs"] = f(np.asarray(inp["x_sample"])[seqs].reshape(NST, D))
    c17 = np.concatenate([np.asarray(inp["c_sample"])[seqs], np.asarray(inp["c_prompt"])[b:b + 1]], 0)
    m["cT"] = f(c17.T.reshape(KC, 128, 17).transpose(1, 0, 2))
    m["w_ada"] = f(inp["w_ada"][0]); m["b_ada"] = f(inp["b_ada"][0][None, :])
    m["npre_fm"] = f(fm16(np.asarray(inp["norm_pre"][0]))); m["npost"] = f(np.asarray(inp["norm_post"][0])[None, :])
    m["w_in"] = f(inp["w_in"][0]); m["w_out"] = f(inp["w_out"][0])
    m["cmp_w1"] = f(inp["cmp_w1"][0])
    m["cmp_peT"] = f(np.asarray(inp["cmp_pe"][0]).transpose(2, 0, 1))
    m["cmp_b1T"] = f(np.asarray(inp["cmp_b1"][0]).T); m["cmp_w2"] = f(inp["cmp_w2"][0])
    m["cmp_b2T"] = f(np.asarray(inp["cmp_b2"][0]).T); m["cmp_b2v"] = f(np.asarray(inp["cmp_b2"][0])[1:2, :])
    m["conv_dwT"] = f(np.asarray(inp["conv_dw"][0]).T.reshape(8, 128, 31).transpose(1, 0, 2))
    for nm, key in (("conv_dbT", "conv_db"), ("conv_lngT", "conv_ln_g"), ("conv_lnbT", "conv_ln_b")):
        m[nm] = f(np.asarray(inp[key][0]).reshape(8, 128).T)
    ck = np.asarray(inp["cache_kv"][0])
    m["cache"] = ck.reshape(ck.shape[0] * 128, 1024)
    m["ptab"] = np.ascontiguousarray(np.asarray(inp["page_table"])[seqs].astype(np.int32))
    m["swin"] = f(np.asarray(inp["state_win_kv"][0])[seqs].reshape(NSEQ, 512, 512))
    m["sconv"] = f(np.asarray(inp["state_conv"][0])[seqs])
    m.update(_consts(half))
    return m


_NC_CACHE = {}


def run(inputs, cores, n_phys, dbg=False):
    key = (n_phys, dbg)
    nc = build(n_phys=n_phys, dbg=dbg)
    in_maps = []
    for ci, core in enumerate(cores):
        seqs = np.arange(ci * NSEQ, (ci + 1) * NSEQ)
        in_maps.append(core_inputs(inputs, core, None, seqs))
    res = run_bass_kernel_spmd(nc, in_maps, core_ids=list(range(len(cores))))
    return res.results


def kernel(**inputs):
    n_phys = int(np.asarray(inputs["cache_kv"]).shape[1])
    res = run(inputs, list(range(8)), n_phys)
    B = 4
    y_p = np.zeros((B, 2048, D), np.float32)
    kv_p = np.zeros((1, B, 2048, 4, 2, 128), np.float32)
    win_p = np.zeros((1, B, 512, 2, 2, 128), np.float32)
    conv_p = np.zeros((1, B, 30, 1024), np.float32)
    y_s = np.zeros((128, 4, D), np.float32)
    kv_s = np.zeros((1, 128, 4, 4, 2, 128), np.float32)
    win_s = np.zeros((1, 128, 512, 2, 2, 128), np.float32)
    conv_s = np.zeros((1, 128, 30, 1024), np.float32)
    for c in range(8):
        r = res[c]
        b, half = c // 2, c % 2
        y_p[b, half * 1024:(half + 1) * 1024] = r["y_p"]
        kv_p[0, b, half * 1024:(half + 1) * 1024] = r["kv_p"].reshape(1024, 4, 2, 128)
        if half == 1:
            win_p[0, b] = r["win_p"].reshape(512, 2, 2, 128)
            conv_p[0, b] = r["conv_p"][2:32]
        sl = slice(c * NSEQ, (c + 1) * NSEQ)
        y_s[sl] = r["y_s"].reshape(NSEQ, 4, D)
        kv_s[0, sl] = r["kv_s"].reshape(NSEQ, 4, 4, 2, 128)
        win_s[0, sl] = r["win_s"].reshape(NSEQ, 512, 2, 2, 128)
        conv_s[0, sl] = r["conv_s"]
    return (y_p, y_s, kv_p, kv_s, win_p, win_s, conv_p, conv_s)
```

```python
import numpy as np
import ml_dtypes
import concourse.bass as bass
import concourse.mybir as mybir
from concourse.bass_utils import run_bass_kernel_spmd

F32 = mybir.dt.float32
BF16 = mybir.dt.bfloat16
I32 = mybir.dt.int32
AF = mybir.ActivationFunctionType
ALU = mybir.AluOpType
AX = mybir.AxisListType

D = 2048
KC = 16
NOWN = 1024
NCTX = 1024
NSEQ = 16
NST = 64
PAST = 2048
INC = 6680
C_Q, C_KV, C_WIN, C_G, C_ZN, C_A, C_GL, C_ZC = 0, 1024, 2048, 2560, 2584, 3608, 4632, 5656
BIG = 30000.0
SCL = 128 ** -0.5
EPS = 1e-6


class Dep:
    __slots__ = ("w", "r", "name", "dsem", "excl", "npair")

    def __init__(self, name="", excl=False):
        self.w = None
        self.r = {}
        self.name = name
        self.dsem = None
        self.npair = 0
        self.excl = excl


class Tr:
    def __init__(self, nc):
        self.nc = nc
        self.E = {"pe": nc.tensor, "act": nc.scalar, "dve": nc.vector, "pool": nc.gpsimd, "sp": nc.sync}
        self.sem = {k: nc.alloc_semaphore("S_" + k) for k in ("pe", "act", "dve", "pool")}
        self.cnt = {k: 0 for k in self.sem}
        self.seen = {k: {} for k in self.E}
        self.dsems = {}
        self.ninst = 0

    def _wait(self, eng, key, val):
        if self.seen[eng].get(key, 0) >= val:
            return
        self.seen[eng][key] = val
        sem = self.sem[key] if key in self.sem else self.dsems[key][0]
        self.E[eng].wait_ge(sem, val)

    def _need(self, eng, key, val):
        if key == eng and eng == "pe":
            return
        self._wait(eng, key, val)

    def _deps(self, eng, R, W):
        for d in R:
            if d.w is not None:
                self._need(eng, *d.w)
            if d.excl:
                for k, v in d.r.items():
                    if k != eng:
                        self._need(eng, k, v)
        for d in W:
            if d.w is not None:
                self._need(eng, *d.w)
            for k, v in d.r.items():
                self._need(eng, k, v)

    def op(self, eng, fn, R=(), W=()):
        self._deps(eng, R, W)
        ins = fn()
        self.cnt[eng] += 1
        self.ninst += 1
        ins.then_inc(self.sem[eng], 1)
        c = self.cnt[eng]
        for d in R:
            if d.r.get(eng, 0) < c:
                d.r[eng] = c
        for d in W:
            d.w = (eng, c)
            d.r = {}

    def dma(self, q, out, in_, dep, load=True, extraR=(), extraW=(), **kw):
        if load:
            same = dep.w is not None and dep.dsem is not None and dep.w[0] == dep.dsem and not dep.r
            if q != "sp":
                dep.npair = (getattr(dep, "npair", 0) + 1) if same else 0
                same = same and dep.npair % 2 == 1
            if same:
                self._deps(q, extraR, list(extraW))
            else:
                self._deps(q, extraR, [dep] + list(extraW))
        else:
            self._deps(q, [dep] + list(extraR), extraW)
        if dep.dsem is None:
            name = "D%d_%s" % (len(self.dsems), dep.name)
            dep.dsem = name
            self.dsems[name] = [self.nc.alloc_semaphore(name), 0]
        ent = self.dsems[dep.dsem]
        ent[1] += 16
        self.ninst += 1
        self.E[q].dma_start(out=out, in_=in_, **kw).then_inc(ent[0], 16)
        tok = (dep.dsem, ent[1])
        if load:
            dep.w = tok
            dep.r = {}
            for d in extraW:
                d.w = tok
                d.r = {}
        else:
            dep.r[tok[0]] = tok[1]
        for d in extraR:
            d.r[tok[0]] = tok[1]
        return tok

    def idma(self, out, in_, idx_ap, dep, extraR=()):
        q = "pool"
        if dep.w is not None and dep.dsem is not None and dep.w[0] == dep.dsem and not dep.r:
            self._deps(q, extraR, [])
        else:
            self._deps(q, extraR, [dep])
        if dep.dsem is None:
            name = "D%d_%s" % (len(self.dsems), dep.name)
            dep.dsem = name
            self.dsems[name] = [self.nc.alloc_semaphore(name), 0]
        ent = self.dsems[dep.dsem]
        ent[1] += 16
        self.ninst += 1
        self.nc.gpsimd.indirect_dma_start(
            out=out, out_offset=None, in_=in_,
            in_offset=bass.IndirectOffsetOnAxis(ap=idx_ap, axis=0)).then_inc(ent[0], 16)
        tok = (dep.dsem, ent[1])
        dep.w = tok
        dep.r = {}
        for d in extraR:
            d.r[tok[0]] = tok[1]

    def barrier(self):
        for e in self.E:
            for k in self.sem:
                if self.cnt[k]:
                    self._wait(e, k, self.cnt[k])
            for name, (sem, cnt) in self.dsems.items():
                if cnt:
                    self._wait(e, name, cnt)

    def finish(self):
        for name, (sem, cnt) in self.dsems.items():
            if cnt:
                self._wait("sp", name, cnt)
        for k in self.sem:
            if self.cnt[k]:
                self._wait("sp", k, self.cnt[k])


class Arena:
    def __init__(self, nc, lo=16512, hi=229344):
        self.nc, self.lo, self.hi = nc, lo, hi
        self.top = lo
        self.n = 0
        self.offs = {}

    def alloc(self, shape, dt, name="t"):
        esz = 2 if dt == BF16 else 4
        fre = 1
        for s in shape[1:]:
            fre *= s
        nbytes = (fre * esz + 63) // 64 * 64
        off = self.top
        assert off + nbytes <= self.hi, ("SBUF overflow", name, off + nbytes - self.hi)
        self.top += nbytes
        self.n += 1
        t = self.nc.alloc_sbuf_tensor_at("%s_%d" % (name, self.n), list(shape), dt, offset=off)
        self.offs[name] = off
        return t.ap()

    def alloc_at(self, off, shape, dt, name="t"):
        self.n += 1
        t = self.nc.alloc_sbuf_tensor_at("%s_%d" % (name, self.n), list(shape), dt, offset=off)
        return t.ap()

    def mark(self):
        return self.top

    def release(self, m):
        self.top = m


def build(n_phys=2560, dbg=False):
    nc = bass.Bass("TRN2", target_bir_lowering=False)
    tr = Tr(nc)
    ar = Arena(nc)

    def din(name, shape, dt=F32):
        return nc.dram_tensor(name, list(shape), dt, kind="ExternalInput").ap()

    def dout(name, shape, dt=F32):
        return nc.dram_tensor(name, list(shape), dt, kind="ExternalOutput").ap()

    xo = din("xo", [NOWN, D]); xc = din("xc", [NCTX, D]); xs = din("xs", [NST, D])
    cT_d = din("cT", [128, KC, 17])
    w_ada = din("w_ada", [D, 3 * D]); b_ada = din("b_ada", [1, 3 * D])
    npre_d = din("npre_fm", [128, KC]); npost_d = din("npost", [1, D])
    w_in = din("w_in", [D, INC]); w_out = din("w_out", [D, D])
    w1_d = din("cmp_w1", [2, 32, 128, 128]); pe_d = din("cmp_peT", [128, 2, 32])
    b1_d = din("cmp_b1T", [128, 2]); w2_d = din("cmp_w2", [2, 128, 128]); b2T_d = din("cmp_b2T", [128, 2])
    b2v_d = din("cmp_b2v", [1, 128])
    cdw_d = din("conv_dwT", [128, 8, 31]); cdb_d = din("conv_dbT", [128, 8])
    lng_d = din("conv_lngT", [128, 8]); lnb_d = din("conv_lnbT", [128, 8])
    cache = din("cache", [n_phys * 128, 1024]); ptab_d = din("ptab", [NSEQ, 16], I32)
    swin = din("swin", [NSEQ, 512, 512]); sconv = din("sconv", [NSEQ, 30, 1024])
    identb_d = din("ident_bf", [128, 128], BF16); identf_d = din("ident_f", [128, 128])
    onesb_d = din("ones_bf", [128, 128], BF16); onesf_d = din("ones_f", [128, 128])
    m4_d = din("masks4", [128, 4, 512], BF16)
    negcmp_d = din("negcmp", [127, 8, 512], BF16)
    E_d = din("Esel", [128, 2048], BF16); ovl_d = din("overlap", [128, 32])
    selb_d = din("selbias", [128, 8, 32]); allow_d = din("allowed", [128, 8, 32])
    selg_d = din("selg", [128, 24, 128], BF16)
    rp_d = din("rp", [128, 128]); rs_d = din("rs", [128, 64])
    cval_d = din("cval", [128, 1])
    perm_d = din("perm16", [128, 128], BF16)
    sbias_d = din("sbias", [128, 32]); mnew_d = din("mnew", [128, 16], BF16); mwin0_d = din("mwin0", [128, 16], BF16)
    y_p = dout("y_p", [NOWN, D]); kv_p = dout("kv_p", [NOWN, 1024]); win_p = dout("win_p", [512, 512])
    conv_p = dout("conv_p", [32, 1024])
    y_s = dout("y_s", [NST, D]); kv_s = dout("kv_s", [NST, 1024]); win_s = dout("win_s", [NSEQ, 512, 512])
    conv_s = dout("conv_s", [NSEQ, 30, 1024])
    dbg_o = {}

    PS = [nc.alloc_psum_tensor("ps%d" % i, [128, 512], F32).ap() for i in range(8)]
    PSd = [Dep("ps%d" % i, excl=True) for i in range(8)]

    def psbf(i):
        return PS[i].bitcast(BF16)

    def mm(out, lhsT, rhs, start, stop, R, W):
        tr.op("pe", lambda: nc.tensor.matmul(out, lhsT=lhsT, rhs=rhs, start=start, stop=stop), R, W)

    def tp(out, in_, ident, R, W):
        tr.op("pe", lambda: nc.tensor.transpose(out, in_, ident), R, W)

    def act(out, in_, func, R, W, bias=None, scale=None, accum=None):
        kw = {}
        if bias is not None:
            kw["bias"] = bias
        if scale is not None:
            kw["scale"] = scale
        if accum is not None:
            kw["accum_out"] = accum
        tr.op("act", lambda: nc.scalar.activation(out=out, in_=in_, func=func, **kw), R, W)

    def tt(eng, out, a, b, op, R, W):
        e = nc.vector if eng == "dve" else nc.gpsimd
        tr.op(eng, lambda: e.tensor_tensor(out=out, in0=a, in1=b, op=op), R, W)

    def ts(eng, out, a, s1, s2, op0, op1, R, W):
        e = nc.vector if eng == "dve" else nc.gpsimd
        if s2 is None:
            tr.op(eng, lambda: e.tensor_scalar(out=out, in0=a, scalar1=s1, scalar2=None, op0=op0), R, W)
        else:
            tr.op(eng, lambda: e.tensor_scalar(out=out, in0=a, scalar1=s1, scalar2=s2, op0=op0, op1=op1), R, W)

    def stt(eng, out, a, s, b, op0, op1, R, W):
        e = nc.vector if eng == "dve" else nc.gpsimd
        tr.op(eng, lambda: e.scalar_tensor_tensor(out=out, in0=a, scalar=s, in1=b, op0=op0, op1=op1), R, W)

    def cp(eng, out, in_, R, W):
        if eng == "act":
            tr.op("act", lambda: nc.scalar.copy(out=out, in_=in_), R, W)
        else:
            e = nc.vector if eng == "dve" else nc.gpsimd
            tr.op(eng, lambda: e.tensor_copy(out=out, in_=in_), R, W)

    CONST = Dep("const")

    def cload(shape, dt, src, q="sp", name="c", dep=None):
        t = ar.alloc(shape, dt, name)
        tr.dma(q, t, src, dep or CONST)
        return t

    identb = cload([128, 128], BF16, identb_d); identf = cload([128, 128], F32, identf_d)
    onesb = cload([128, 128], BF16, onesb_d); onesf = cload([128, 128], F32, onesf_d)
    rp = cload([128, 128], F32, rp_d); rsel = cload([128, 64], F32, rs_d)
    cval = cload([128, 1], F32, cval_d)
    npre = cload([128, KC], F32, npre_d)
    cdw = cload([128, 8, 31], F32, cdw_d); cdb = cload([128, 8], F32, cdb_d)
    lng = cload([128, 8], F32, lng_d); lnb = cload([128, 8], F32, lnb_d)
    b1T = cload([128, 2], F32, b1_d); b2T = cload([128, 2], F32, b2T_d)
    b2v = cload([127, 128], F32, b2v_d.partition_broadcast(127))
    CONSTP = Dep("constp")
    w1 = ar.alloc([128, 2, 32, 128], BF16, "w1")
    for s in range(2):
        tr.dma("pool", w1[:, s], w1_d[s].rearrange("l d e -> d l e"), CONSTP)
    peT = ar.alloc([128, 2, 32], BF16, "peT")
    tr.dma("pool", peT, pe_d, CONSTP)
    w2 = ar.alloc([128, 2, 128], BF16, "w2")
    tr.dma("pool", w2, w2_d.rearrange("s e f -> e s f"), CONSTP)
    A1T = ar.alloc([128, KC, 17], F32, "A1T"); shT = ar.alloc([128, KC, 17], F32, "shT")
    G2 = ar.alloc([128, D], F32, "G2")
    dA1 = Dep("A1T"); dG2 = Dep("G2")
    tr.op("pool", lambda: nc.gpsimd.memset(G2, 0.0), [], [dG2])
    hTs = ar.alloc([128, KC, NST], BF16, "hTs"); dhTs = Dep("hTs")
    hThalo = ar.alloc([128, KC, 32], BF16, "hThalo"); dhalo = Dep("hThalo")
    bias1 = ar.alloc([128, 2], F32, "bias1"); dbias1 = Dep("bias1")
    kcT = ar.alloc([128, 2, 127], BF16, "kcT"); vc = ar.alloc([127, 2, 128], BF16, "vc")
    dkc = Dep("kcT"); dvc = Dep("vc")
    sg = ar.alloc([128, NOWN], BF16, "sg"); dsg = Dep("sg")
    tr.op("pool", lambda: nc.gpsimd.memset(sg, 0.0), [], [dsg])
    ulast = ar.alloc([128, 8, 32], F32, "ulast"); dulast = Dep("ulast")
    sfm = {}
    zcs = ar.alloc([128, 8, NST], BF16, "zcs"); sfm["zc"] = (zcs, Dep("zcs"))
    us = ar.alloc([128, 8, NST], F32, "us"); sfm["u"] = (us, Dep("us"))
    kvs = ar.alloc([NST, 1536], F32, "kvs"); sfm["kvs"] = (kvs, Dep("kvs"))
    ksT = ar.alloc([128, 12, NST], BF16, "ksT"); sfm["ksT"] = (ksT, Dep("ksT"))
    qs = ar.alloc([128, 8, NST], BF16, "qs"); sfm["q"] = (qs, Dep("qs"))
    sgs = ar.alloc([128, NST], BF16, "sgs"); sfm["sg"] = (sgs, Dep("sgs"))
    tr.op("pool", lambda: nc.gpsimd.memset(sgs, 0.0), [], [sfm["sg"][1]])
    zns = ar.alloc([128, 8, NST], BF16, "zns"); sfm["zn"] = (zns, Dep("zns"))
    mixTs = ar.alloc([128, 16, NST], BF16, "mixTs"); dmixs = Dep("mixTs")


    def phase_A():
        m = ar.mark()
        cT = ar.alloc([128, KC, 17], F32, "cT"); dcT = Dep("cT")
        scT = ar.alloc([128, KC, 17], F32, "scT"); dscT = Dep("scT")
        ada = ar.alloc([17, 3 * D], F32, "ada"); dada = Dep("ada")
        bb = ar.alloc([17, 3 * D], F32, "bb"); dbb = Dep("bb")
        npb = ar.alloc([17, D], F32, "npb"); dnpb = Dep("npb")
        wsl = [ar.alloc([128, KC, 512], F32, "wa%d" % i) for i in range(2)]
        dws = [Dep("wa%d" % i) for i in range(2)]
        tr.dma("sp", cT, cT_d, dcT)
        tr.dma("sp", bb, b_ada.partition_broadcast(17), dbb)
        tr.dma("sp", npb, npost_d.partition_broadcast(17), dnpb)
        act(scT, cT, AF.Silu, [dcT], [dscT])
        wv = w_ada.rearrange("(k p) c -> p k c", p=128)
        for blk in range(12):
            s = blk % 2
            for hk in range(2):
                tr.dma("sp", wsl[s][:, hk * 8:(hk + 1) * 8, :], wv[:, hk * 8:(hk + 1) * 8, blk * 512:(blk + 1) * 512], dws[s])
            pb = blk % 2
            for k in range(KC):
                mm(PS[pb][0:17, :], scT[:, k, :], wsl[s][:, k, :], k == 0, k == KC - 1, [dscT, dws[s]], [PSd[pb]])
            tt("dve", ada[:, blk * 512:(blk + 1) * 512], PS[pb][0:17, :], bb[:, blk * 512:(blk + 1) * 512], ALU.add,
               [PSd[pb], dbb], [dada])
        for part in range(2):
            pst = PS[2 + part][:, 0:KC * 17].rearrange("p (k r) -> p k r", k=KC)
            for k in range(KC):
                tp(pst[:, k, :], ada[0:17, part * D + k * 128: part * D + (k + 1) * 128], identf[0:17, 0:17],
                   [dada, CONST], [PSd[2 + part]])
            if part == 0:
                cp("dve", shT, pst, [PSd[2]], [dA1])
            else:
                stt("dve", A1T, pst, 1.0, npre.unsqueeze(2).to_broadcast([128, KC, 17]), ALU.add, ALU.mult,
                    [PSd[3], CONST], [dA1])
        tt("dve", G2[0:17, :], ada[:, 2 * D:3 * D], npb, ALU.mult, [dada, dnpb], [dG2])
        ar.release(m)
        tr.barrier()

    def make_hT(xd, ntok, dst_fn, ddst, rowsel, tmp):
        xt, dxt, xn, dxn, junk, djunk, st, dst_ = tmp
        tr.dma("sp", xt[0:ntok, :], xd, dxt)
        act(junk[0:ntok, :], xt[0:ntok, :], AF.Square, [dxt], [djunk, dst_], accum=st[0:ntok, 0:1])
        act(st[0:ntok, 1:2], st[0:ntok, 0:1], AF.Sqrt, [dst_], [dst_], bias=EPS, scale=1.0 / D)
        tr.op("dve", lambda: nc.vector.reciprocal(out=st[0:ntok, 2:3], in_=st[0:ntok, 1:2]), [dst_], [dst_])
        ts("dve", xn[0:ntok, :], xt[0:ntok, :], st[0:ntok, 2:3], None, ALU.mult, None, [dxt, dst_], [dxn])
        for hb in range(2):
            pb = 4 + hb
            pv = psbf(pb)[:, 0:8 * ntok].rearrange("p (k t) -> p k t", k=8)
            for kk in range(8):
                k = hb * 8 + kk
                tp(pv[:, kk, :], xn[0:ntok, k * 128:(k + 1) * 128], identb[0:ntok, 0:ntok], [dxn, CONST], [PSd[pb]])
            for kk in range(8):
                k = hb * 8 + kk
                if rowsel is not None:
                    if kk % 2 == 0:
                        act(dst_fn(k), pv[:, kk, :], AF.Identity, [PSd[pb], dA1], [ddst],
                            bias=shT[:, k, rowsel:rowsel + 1], scale=A1T[:, k, rowsel:rowsel + 1])
                    else:
                        ts("dve", dst_fn(k), pv[:, kk, :], A1T[:, k, rowsel:rowsel + 1], shT[:, k, rowsel:rowsel + 1],
                           ALU.mult, ALU.add, [PSd[pb], dA1], [ddst])
                else:
                    o3 = dst_fn(k).rearrange("p (s t) -> p s t", t=4)
                    i3 = pv[:, kk, :].rearrange("p (s t) -> p s t", t=4)
                    tt("dve", o3, i3, A1T[:, k, 0:NSEQ].unsqueeze(2).to_broadcast([128, NSEQ, 4]), ALU.mult,
                       [PSd[pb], dA1], [ddst])
                    tt("dve", o3, o3, shT[:, k, 0:NSEQ].unsqueeze(2).to_broadcast([128, NSEQ, 4]), ALU.add,
                       [dA1], [ddst])

    def xtmp2():
        xn = ar.alloc([128, D], BF16, "xn"); dxn = Dep("xn")
        junk, djunk = xn, dxn
        res = []
        for i in range(2):
            xt = ar.alloc([128, D], F32, "xt"); st = ar.alloc([128, 4], F32, "st")
            res.append((xt, Dep("xt"), xn, dxn, junk, djunk, st, Dep("st")))
        return res

    WCOLS = 256

    class WStream:
        def __init__(self):
            self.sl = [ar.alloc([128, KC, WCOLS], BF16, "wsl%d" % i) for i in range(2)]
            self.d = [Dep("wsl%d" % i) for i in range(2)]
            self.i = 0
            self.wv = w_in.rearrange("(k p) c -> p k c", p=128)

        def load(self, c0, ncols):
            s = self.i % 2
            self.i += 1
            for hk in range(2):
                tr.dma("pool", self.sl[s][:, hk * 8:(hk + 1) * 8, 0:ncols],
                       self.wv[:, hk * 8:(hk + 1) * 8, c0:c0 + ncols], self.d[s])
            return self.sl[s], self.d[s]

    fm_rr = [0]

    def fm_chunk(wt, dw, c0, ncol, rhs_fn, R, nt):
        pb = fm_rr[0] % 2
        fm_rr[0] += 1
        for k in range(KC):
            mm(PS[pb][0:ncol, 0:nt], wt[:, k, c0:c0 + ncol], rhs_fn(k), k == 0, k == KC - 1, [dw] + R, [PSd[pb]])
        return PS[pb][0:ncol, 0:nt], pb

    tm_rr = [0]

    def tm_block(wt, dw, ncols, lhs_fn, R, ntok):
        pb = 2 + tm_rr[0] % 2
        tm_rr[0] += 1
        for k in range(KC):
            mm(PS[pb][0:ntok, 0:ncols], lhs_fn(k), wt[:, k, 0:ncols], k == 0, k == KC - 1, [dw] + R, [PSd[pb]])
        return PS[pb][0:ntok, 0:ncols], pb

    phase_A()

    mixT = ar.alloc([128, 16, NOWN], BF16, "mixT")
    dmix = [[Dep("mix%d_%d" % (c, t)) for t in range(2)] for c in range(16)]
    qT = ar.alloc([128, 8, NOWN], BF16, "qT"); dq = [Dep("q%d" % h) for h in range(8)]
    kslcT = ar.alloc([128, 2, 2048], BF16, "kslcT"); dkslc = Dep("kslcT")
    kwinT = ar.alloc([128, 2, 1536], BF16, "kwinT"); dkwin = Dep("kwinT")
    vslc = ar.alloc([128, 16, 2, 128], BF16, "vslc"); dvslc = Dep("vslc")
    vwin = ar.alloc([128, 12, 2, 128], BF16, "vwin"); dvwin = Dep("vwin")
    ACC = qT
    dacc = [Dep("acc%d" % j) for j in range(8)]
    kcmpT = mixT[:, 0:4, :].rearrange("p (g a) t -> p g (a t)", g=2).rearrange("p g (ph m) -> p g ph m", ph=16)
    vcmpT = mixT[:, 4:8, :].rearrange("p (g a) t -> p g (a t)", g=2).rearrange("p g (ph m) -> p g ph m", ph=16)
    dkcmp = Dep("kcmpT"); dvcmp = Dep("vcmpT")
    PHASE = ar.mark()

    m0 = ar.mark()
    hT = ar.alloc([128, KC, 1024], BF16, "hT")
    dhT = [Dep("hT%d" % t) for t in range(8)]
    m1 = ar.mark()
    tmpx = xtmp2()
    make_hT(xs, NST, lambda k: hTs[:, k, :], dhTs, None, tmpx[0])
    for t in range(8):
        make_hT(xc[t * 128:(t + 1) * 128, :], 128, (lambda k, t=t: hT[:, k, t * 128:(t + 1) * 128]), dhT[t], 16, tmpx[(t + 1) % 2])
    cp("pool", hThalo, hT[:, :, 992:1024], [dhT[7]], [dhalo])
    ar.release(m1)
    tr.barrier()

    ws = WStream()
    stg = [ar.alloc([128, WCOLS], F32, "stg%d" % i) for i in range(2)]
    dstg = [Dep("stg%d" % i) for i in range(2)]
    stg_rr = [0]

    def hgrp(tg):
        return lambda k: hT[:, k, tg * 512:(tg + 1) * 512]

    def ctx_pass():
        for b in range(6):
            wt, dw = ws.load(C_KV + b * 256, 256)
            if b in (0, 1, 2, 4):
                dstT, dd = {0: (kcmpT, dkcmp), 1: (vcmpT, dvcmp), 2: (kslcT, dkslc), 4: (kwinT, dkwin)}[b]
                for g in range(2):
                    for tg in range(2):
                        if b == 4 and tg == 0:
                            continue
                        ps, pb = fm_chunk(wt, dw, g * 128, 128, hgrp(tg), [dhT[4 * tg + i] for i in range(4)], 512)
                        if b == 4:
                            o = dstT[:, g, 0:512]
                        elif b == 2:
                            o = dstT[:, g, tg * 512:(tg + 1) * 512]
                        else:
                            o = dstT[:, g, :, tg * 32:(tg + 1) * 32].rearrange("p ph m -> p m ph")
                            ps = ps.rearrange("p (m ph) -> p m ph", ph=16)
                        cp("act" if (g + tg) % 2 else "dve", o, ps, [PSd[pb]], [dd])
            else:
                dstV, dd = (vslc, dvslc) if b == 3 else (vwin, dvwin)
                for t in range(8):
                    if b == 5 and t < 4:
                        continue
                    ps, pb = tm_block(wt, dw, 256, (lambda k, t=t: hT[:, k, t * 128:(t + 1) * 128]), [dhT[t]], 128)
                    o = dstV[:, t, :, :] if b == 3 else dstV[:, t - 4, :, :]
                    cp("act" if t % 2 else "dve", o, ps.rearrange("p (g d) -> p g d", g=2), [PSd[pb]], [dd])

    ctx_pass()

    m2 = ar.mark()
    tmpx = xtmp2()
    for t in range(8):
        make_hT(xo[t * 128:(t + 1) * 128, :], 128, (lambda k, t=t: hT[:, k, t * 128:(t + 1) * 128]), dhT[t], 16, tmpx[t % 2])
    ar.release(m2)
    tr.barrier()


    def own_pass():
        allh = lambda tg: [dhT[4 * tg + i] for i in range(4)]
        for b in range(4):
            wt, dw = ws.load(C_ZC + b * 256, 256)
            for cc in range(2):
                j = 2 * b + cc
                for tg in range(2):
                    ps, pb = fm_chunk(wt, dw, cc * 128, 128, hgrp(tg), allh(tg), 512)
                    act(mixT[:, 8 + j, tg * 512:(tg + 1) * 512], ps, AF.Silu, [PSd[pb]], [dmix[8 + j][tg]])
                ps, pb = fm_chunk(wt, dw, cc * 128, 128, lambda k: hTs[:, k, :], [dhTs], NST)
                act(zcs[:, j, :], ps, AF.Silu, [PSd[pb]], [sfm["zc"][1]])
        ubuf = [ar.alloc([128, 32 + NOWN], F32, "ubuf%d" % i) for i in range(2)]
        dub = [Dep("ubuf%d" % i) for i in range(2)]
        sgt = ar.alloc([128, 512], F32, "sgt"); dsgt = Dep("sgt")
        ubb = ar.alloc([128, 32 + NOWN], BF16, "ubb"); dubb = Dep("ubb")
        diagJ = ar.alloc([128, 31, 128], BF16, "diagJ"); ddiag = Dep("diagJ")
        cv_rr = [0]
        for b in range(4):
            wa, dwa = ws.load(C_A + b * 256, 256)
            wg, dwg = ws.load(C_GL + b * 256, 256)
            for cc in range(2):
                j = 2 * b + cc
                ub, du = ubuf[j % 2], dub[j % 2]
                segs = [(lambda k: hThalo[:, k, :], [dhalo], 32, ub[:, 0:32]),
                        (hgrp(0), allh(0), 512, ub[:, 32:544]),
                        (hgrp(1), allh(1), 512, ub[:, 544:1056]),
                        (lambda k: hTs[:, k, :], [dhTs], NST, us[:, j, :])]
                for si, (rf, R, nt, dsta) in enumerate(segs):
                    dd = du if si < 3 else sfm["u"][1]
                    ps, pb = fm_chunk(wa, dwa, cc * 128, 128, rf, R, nt)
                    cp("dve", dsta, ps, [PSd[pb]], [dd])
                    ps, pb = fm_chunk(wg, dwg, cc * 128, 128, rf, R, nt)
                    act(sgt[:, 0:nt], ps, AF.Sigmoid, [PSd[pb]], [dsgt])
                    tt("dve", dsta, dsta, sgt[:, 0:nt], ALU.mult, [dsgt], [dd])
                ts("dve", ub[:, 0:32], ub[:, 0:32], cval[:, 0:1], None, ALU.mult, None, [CONST], [du])
                cp("pool", ulast[:, j, :], ub[:, 1024:1056], [du], [dulast])
                cp("pool", ubb, ub[:, 0:1056], [du], [dubb])
                tt("pool", diagJ, identb.unsqueeze(1).to_broadcast([128, 31, 128]),
                   cdw[:, j, :].unsqueeze(2).to_broadcast([128, 31, 128]), ALU.mult, [CONST], [ddiag])
                for tg in range(2):
                    pb = 2 + cv_rr[0] % 2
                    cv_rr[0] += 1
                    for w in range(31):
                        mm(PS[pb], diagJ[:, w, :], ubb[:, 2 + w + tg * 512: 2 + w + tg * 512 + 512], w == 0, w == 30,
                           [ddiag, dubb], [PSd[pb]])
                    act(ACC[:, j, tg * 512:(tg + 1) * 512], PS[pb], AF.Identity, [PSd[pb], CONST], [dacc[j]], bias=cdb[:, j:j + 1])
        sq = ar.alloc([128, 512], BF16, "sq"); dsq = Dep("sq")
        mean = ar.alloc([128, 512], F32, "mean"); dmean = Dep("mean")
        rstd = ar.alloc([128, 512], F32, "rstd"); drstd = Dep("rstd")
        tmpn = ar.alloc([128, 512], F32, "tmpn"); dtmpn = Dep("tmpn")
        for tg in range(2):
            sl = slice(tg * 512, (tg + 1) * 512)
            for j in range(8):
                mm(PS[4], onesb, ACC[:, j, sl], j == 0, j == 7, [CONST, dacc[j]], [PSd[4]])
            for j in range(8):
                act(sq, ACC[:, j, sl], AF.Square, [dacc[j]], [dsq])
                mm(PS[5], onesb, sq, j == 0, j == 7, [CONST, dsq], [PSd[5]])
            ts("dve", mean, PS[4], 1.0 / 1024, None, ALU.mult, None, [PSd[4]], [dmean])
            tt("dve", tmpn, mean, mean, ALU.mult, [dmean], [dtmpn])
            stt("dve", tmpn, PS[5], 1.0 / 1024, tmpn, ALU.mult, ALU.subtract, [PSd[5]], [dtmpn])
            act(tmpn, tmpn, AF.Sqrt, [dtmpn], [dtmpn], bias=EPS, scale=1.0)
            tr.op("dve", lambda: nc.vector.reciprocal(out=rstd, in_=tmpn), [dtmpn], [drstd])
            for j in range(8):
                tt("dve", tmpn, ACC[:, j, sl], mean, ALU.subtract, [dacc[j], dmean], [dtmpn])
                tt("dve", tmpn, tmpn, rstd, ALU.mult, [drstd], [dtmpn])
                act(tmpn, tmpn, AF.Silu, [dtmpn], [dtmpn], bias=lnb[:, j:j + 1], scale=lng[:, j:j + 1])
                tt("dve", mixT[:, 8 + j, sl], tmpn, mixT[:, 8 + j, sl], ALU.mult, [dtmpn], [dmix[8 + j][tg]])

        for b in range(6):
            wt, dw = ws.load(C_KV + b * 256, 256)
            for t in range(8):
                ps, pb = tm_block(wt, dw, 256, (lambda k, t=t: hT[:, k, t * 128:(t + 1) * 128]), [dhT[t]], 128)
                si = stg_rr[0] % 2
                stg_rr[0] += 1
                cp("dve" if t % 2 else "act", stg[si], ps, [PSd[pb]], [dstg[si]])
                if b < 4:
                    tr.dma("sp", kv_p[t * 128:(t + 1) * 128, b * 256:(b + 1) * 256], stg[si], dstg[si], load=False)
                elif t >= 4:
                    tr.dma("sp", win_p[(t - 4) * 128:(t - 3) * 128, (b - 4) * 256:(b - 3) * 256], stg[si], dstg[si], load=False)
                if b == 3:
                    cp("pool", vslc[:, 8 + t, :, :], stg[si].rearrange("p (g d) -> p g d", g=2), [dstg[si]], [dvslc])
                if b == 5:
                    cp("pool", vwin[:, 4 + t, :, :], stg[si].rearrange("p (g d) -> p g d", g=2), [dstg[si]], [dvwin])
            ps, pb = tm_block(wt, dw, 256, lambda k: hTs[:, k, :], [dhTs], NST)
            cp("dve", kvs[:, b * 256:(b + 1) * 256], ps, [PSd[pb]], [sfm["kvs"][1]])
            if b in (0, 1, 2, 4):
                dstT, dd = {0: (kcmpT, dkcmp), 1: (vcmpT, dvcmp), 2: (kslcT, dkslc), 4: (kwinT, dkwin)}[b]
                for g in range(2):
                    for tg in range(2):
                        ps, pb = fm_chunk(wt, dw, g * 128, 128, hgrp(tg), allh(tg), 512)
                        if b == 4:
                            o = dstT[:, g, 512 + tg * 512: 512 + (tg + 1) * 512]
                        elif b == 2:
                            o = dstT[:, g, 1024 + tg * 512: 1024 + (tg + 1) * 512]
                        else:
                            o = dstT[:, g, :, 64 + tg * 32: 64 + (tg + 1) * 32].rearrange("p ph m -> p m ph")
                            ps = ps.rearrange("p (m ph) -> p m ph", ph=16)
                        cp("act" if (g + tg) % 2 else "dve", o, ps, [PSd[pb]], [dd])
            if b >= 2:
                for g in range(2):
                    ps, pb = fm_chunk(wt, dw, g * 128, 128, lambda k: hTs[:, k, :], [dhTs], NST)
                    cp("dve", ksT[:, 2 * b + g, :], ps, [PSd[pb]], [sfm["ksT"][1]])
            if b == 1:
                compress_prompt()
        tr.dma("sp", kv_s, kvs[:, 0:1024], sfm["kvs"][1], load=False)

        for b in range(4):
            wt, dw = ws.load(C_Q + b * 256, 256)
            for cc in range(2):
                h = 2 * b + cc
                for tg in range(2):
                    ps, pb = fm_chunk(wt, dw, cc * 128, 128, hgrp(tg), allh(tg), 512)
                    cp("act" if tg else "dve", qT[:, h, tg * 512:(tg + 1) * 512], ps, [PSd[pb]], [dq[h]] + dacc)
                ps, pb = fm_chunk(wt, dw, cc * 128, 128, lambda k: hTs[:, k, :], [dhTs], NST)
                cp("dve", qs[:, h, :], ps, [PSd[pb]], [sfm["q"][1]])
        wt, dw = ws.load(C_G, 24)
        for tg in range(2):
            ps, pb = fm_chunk(wt, dw, 0, 24, hgrp(tg), allh(tg), 512)
            act(sg[0:24, tg * 512:(tg + 1) * 512], ps, AF.Sigmoid, [PSd[pb]], [dsg])
        ps, pb = fm_chunk(wt, dw, 0, 24, lambda k: hTs[:, k, :], [dhTs], NST)
        act(sgs[0:24, :], ps, AF.Sigmoid, [PSd[pb]], [sfm["sg"][1]])
        for b in range(4):
            wt, dw = ws.load(C_ZN + b * 256, 256)
            for cc in range(2):
                j = 2 * b + cc
                for tg in range(2):
                    ps, pb = fm_chunk(wt, dw, cc * 128, 128, hgrp(tg), allh(tg), 512)
                    act(mixT[:, j, tg * 512:(tg + 1) * 512], ps, AF.Silu, [PSd[pb]], [dmix[j][tg], dkcmp, dvcmp])
                ps, pb = fm_chunk(wt, dw, cc * 128, 128, lambda k: hTs[:, k, :], [dhTs], NST)
                act(zns[:, j, :], ps, AF.Silu, [PSd[pb]], [sfm["zn"][1]])

    def bias1_setup():
        for s in range(2):
            for l in range(32):
                mm(PS[6][:, s:s + 1], w1[:, s, l, :], peT[:, s, l:l + 1], l == 0, l == 31, [CONSTP], [PSd[6]])
        tt("dve", bias1, PS[6][:, 0:2], b1T, ALU.add, [PSd[6], CONST], [dbias1])

    def compress(srcK, srcV, R, kc_out, vc_out, dkc_o, dvc_o, hbuf, dhbuf):
        for s, src in ((0, srcK), (1, srcV)):
            pv = PS[6][:, 0:254].rearrange("p (g m) -> p g m", g=2)
            for l in range(32):
                mm(pv, w1[:, s, l, :], src[:, :, l % 16, l // 16: l // 16 + 127], l == 0, l == 31, [CONSTP] + R, [PSd[6]])
            act(hbuf, pv, AF.Silu, [PSd[6], dbias1], [dhbuf], bias=bias1[:, s:s + 1])
            if s == 0:
                pk = PS[7][:, 0:254].rearrange("p (g m) -> p g m", g=2)
                mm(pk, w2[:, 0, :], hbuf, True, True, [CONSTP, dhbuf], [PSd[7]])
                act(kc_out, pk, AF.Identity, [PSd[7], CONST], [dkc_o], bias=b2T[:, 0:1])
            else:
                for g in range(2):
                    mm(PS[7][0:127, g * 128:(g + 1) * 128], hbuf[:, g, :], w2[:, 1, :], True, True, [CONSTP, dhbuf], [PSd[7]])
                tt("dve", vc_out, PS[7][0:127, 0:256].rearrange("p (g f) -> p g f", g=2),
                   b2v.unsqueeze(1).to_broadcast([127, 2, 128]), ALU.add, [PSd[7], CONST], [dvc_o])

    hbuf = ar.alloc([128, 2, 127], BF16, "hbuf"); dhbuf = Dep("hbuf")

    def compress_prompt():
        bias1_setup()
        compress(kcmpT, vcmpT, [dkcmp, dvcmp], kcT, vc, dkc, dvc, hbuf, dhbuf)

    own_pass()
    tr.barrier()
    ar.release(m0)

    CONST2 = Dep("const2")
    Esel = cload([128, 2048], BF16, E_d, dep=CONST2); ovl = cload([128, 32], F32, ovl_d, dep=CONST2)
    selg = cload([128, 24, 128], BF16, selg_d, dep=CONST2)
    mKeep = ar.mark()
    masks4 = cload([128, 4, 512], BF16, m4_d, dep=CONST2)
    negcmp = cload([127, 8, 512], BF16, negcmp_d, dep=CONST2)
    selb = cload([128, 8, 32], F32, selb_d, dep=CONST2); allow = cload([128, 8, 32], F32, allow_d, dep=CONST2)
    mcv = ar.mark()
    cvo = ar.alloc([32, 1024], F32, "cvo"); dcvo = Dep("cvo")
    for j in range(8):
        tp(PS[6 + j // 4][0:32, (j % 4) * 128:(j % 4 + 1) * 128], ulast[:, j, :], identf, [dulast, CONST], [PSd[6 + j // 4]])
    for hb in range(2):
        cp("dve", cvo[:, hb * 512:(hb + 1) * 512], PS[6 + hb][0:32, :], [PSd[6 + hb]], [dcvo])
    tr.dma("sp", conv_p, cvo, dcvo, load=False)


    def attn_prompt():
        Pb = [ar.alloc([128, 512], BF16, "Pb%d" % i) for i in range(2)]
        dPb = [Dep("Pb%d" % i) for i in range(2)]
        Pf = ar.alloc([128, 512], F32, "Pf"); dPf = Dep("Pf")
        pn = ar.alloc([128, 512], F32, "pn"); dpn = Dep("pn")
        psT = ar.alloc([128, 128], F32, "psT"); dpsT = Dep("psT")
        rsB = ar.alloc([128, 512], F32, "rsB"); drs = Dep("rsB")
        coef = ar.alloc([128, 512], F32, "coef"); dcoef = Dep("coef")
        tmpo = ar.alloc([128, 512], F32, "tmpo"); dtmpo = Dep("tmpo")
        oacc = ar.alloc([128, 512], F32, "oacc"); doacc = Dep("oacc")
        score = ar.alloc([128, 32], F32, "score"); dscore = Dep("score")
        work = ar.alloc([128, 32], F32, "work"); dwork = Dep("work")
        m8 = ar.alloc([128, 16], F32, "m8"); dm8 = Dep("m8")
        selt = ar.alloc([128, 32], F32, "selt"); dselt = Dep("selt")
        nsT4 = ar.alloc([128, 4, 128], BF16, "nsT4"); dnsT4 = Dep("nsT4")
        srr = [0]
        prr = [0]
        tr.op("pool", lambda: nc.gpsimd.memset(Pf, 0.0), [], [dPf])
        tr.op("pool", lambda: nc.gpsimd.memset(psT, 0.0), [], [dpsT])
        tr.op("pool", lambda: nc.gpsimd.memset(nsT4, 0.0), [], [dnsT4])

        accsel = [0]

        def acc_banks():
            return (2, 3) if accsel[0] % 2 == 0 else (6, 7)

        def finish_branch(br, first):
            bo, bs_ = acc_banks()
            accsel[0] += 1
            if first:
                ts("dve", rsB, PS[bs_], 1e-30, None, ALU.max, None, [PSd[bs_]], [drs])
                tr.op("dve", lambda: nc.vector.reciprocal(out=rsB, in_=rsB), [drs], [drs])
            else:
                act(rsB, PS[bs_], AF.Ln, [PSd[bs_]], [drs], bias=1e-18, scale=1.0)
                act(rsB, rsB, AF.Exp, [drs], [drs], scale=-1.0)
            tt("dve", coef, PS[4], rsB, ALU.mult, [PSd[4], drs], [dcoef])
            if first:
                tt("dve", oacc, PS[bo], coef, ALU.mult, [PSd[bo], dcoef], [doacc])
            else:
                tt("dve", tmpo, PS[bo], coef, ALU.mult, [PSd[bo], dcoef], [dtmpo])
                tt("pool", oacc, oacc, tmpo, ALU.add, [dtmpo], [doacc])

        def gates(br, g, qi):
            for r in range(4):
                mm(PS[4][:, r * 128:(r + 1) * 128], selg[:, br * 8 + 4 * g + r, :], sg[:, qi * 128:(qi + 1) * 128],
                   True, True, [CONST2, dsg], [PSd[4]])

        pend = [None]

        def pv_part(t):
            v_ap, pbi, Rk, first, last, npart = t
            bo, bs_ = acc_banks()
            mm(PS[bo], v_ap, Pb[pbi][0:npart, :], first, last, [dPb[pbi]] + Rk, [PSd[bo]])
            mm(PS[bs_], onesb[0:npart, :], Pb[pbi][0:npart, :], first, last, [dPb[pbi], CONST], [PSd[bs_]])

        def flush_pv():
            if pend[0] is not None:
                pv_part(pend[0])
                pend[0] = None

        def tile_attend(kT_ap, v_ap, masks, qrhs, Rk, first, last, npart=128):
            sb = srr[0] % 2; srr[0] += 1
            pbi = prr[0] % 2; prr[0] += 1
            S = PS[sb][0:npart, :]
            mm(S, kT_ap, qrhs, True, len(masks) == 0, Rk, [PSd[sb]])
            for mi, (ml, mr, Rm) in enumerate(masks):
                mm(S, ml, mr, False, mi == len(masks) - 1, Rm, [PSd[sb]])
            act(Pb[pbi][0:npart, :], S, AF.Exp, [PSd[sb]], [dPb[pbi]], scale=SCL)
            flush_pv()
            pend[0] = (v_ap, pbi, Rk, first, last, npart)

        oaccs = [oacc, ar.alloc([128, 512], F32, "oacc1")]; doaccs = [doacc, Dep("oacc1")]
        nsT4s = [nsT4, ar.alloc([128, 4, 128], BF16, "nsT4b")]; dnsT4s = [dnsT4, Dep("nsT4b")]
        tr.op("pool", lambda: nc.gpsimd.memset(nsT4s[1], 0.0), [], [dnsT4s[1]])
        its = [(qi, g) for qi in range(8) for g in range(2)]
        gS = [[ar.alloc([128, 512], F32, "gS%d_%d" % (ib, br)) for br in range(3)] for ib in range(2)]
        dgS = [[Dep("gS%d_%d" % (ib, br)) for br in range(3)] for ib in range(2)]
        rsA = ar.alloc([128, 512], F32, "rsA"); drsA = Dep("rsA")

        def all_gates(i):
            qi, g = its[i]
            for br in range(3):
                gates(br, g, qi)
                cp("act", gS[i % 2][br], PS[4], [PSd[4]], [dgS[i % 2][br]])

        def finish_branch2(br, first, ib):
            bo, bs_ = acc_banks()
            accsel[0] += 1
            oa, doa = oaccs[ib], doaccs[ib]
            act(rsA, PS[bs_], AF.Ln, [PSd[bs_]], [drsA], bias=1e-18, scale=1.0)
            act(rsA, rsA, AF.Exp, [drsA], [drsA], scale=-1.0)
            tt("dve", coef, gS[ib][br], rsA, ALU.mult, [dgS[ib][br], drsA], [dcoef])
            if first:
                tt("dve", oa, PS[bo], coef, ALU.mult, [PSd[bo], dcoef], [doa])
                ts("dve", rsB, PS[bs_], 1e-30, None, ALU.max, None, [PSd[bs_]], [drs])
                tr.op("dve", lambda: nc.vector.reciprocal(out=rsB, in_=rsB), [drs], [drs])
            else:
                tt("dve", tmpo, PS[bo], coef, ALU.mult, [PSd[bo], dcoef], [dtmpo])
                tt("dve", oa, oa, tmpo, ALU.add, [dtmpo], [doa])

        def qinfo(i):
            qi, g = its[i]
            return qi, g, qT[:, 4 * g:4 * g + 4, qi * 128:(qi + 1) * 128], [dq[4 * g + r] for r in range(4)]

        def C1(i):
            qi, g, qrhs, Rq = qinfo(i)
            sb = srr[0] % 2; srr[0] += 1
            S = PS[sb][0:127, :]
            mm(S, kcT[:, g, :], qrhs, True, False, Rq + [dkc], [PSd[sb]])
            mm(S, identb[0:127, 0:127], negcmp[:, qi, :], False, True, [CONST, CONST2], [PSd[sb]])
            act(Pf[0:127, :], S, AF.Exp, [PSd[sb]], [dPf], scale=SCL)
            pbi = prr[0] % 2; prr[0] += 1
            act(Pb[pbi][0:127, :], S, AF.Exp, [PSd[sb]], [dPb[pbi]], scale=SCL)
            bo, bs_ = acc_banks()
            mm(PS[bs_], onesf, Pf, True, True, [dPf, CONST], [PSd[bs_]])
            mm(PS[bo], vc[:, g, :], Pb[pbi][0:127, :], True, True, [dPb[pbi], dvc], [PSd[bo]])
            all_gates(i)
            finish_branch2(0, True, i % 2)
            tt("dve", pn[0:127, :], Pf[0:127, :], rsB[0:127, :], ALU.mult, [dPf, drs], [dpn])
            tr.op("dve", lambda: nc.vector.tensor_reduce(
                out=psT[0:127, :], in_=pn[0:127, :].rearrange("p (r i) -> p i r", r=4), axis=AX.X, op=ALU.add),
                [dpn], [dpsT])

        def C2(i):
            qi, g, qrhs, Rq = qinfo(i)
            mm(PS[5][:, 0:32], psT, ovl, True, True, [dpsT, CONST2], [PSd[5]])
            tt("dve", score, PS[5][:, 0:32], selb[:, qi, :], ALU.add, [PSd[5], CONST2], [dscore])
            tr.op("dve", lambda: nc.vector.max(out=m8[:, 0:8], in_=score), [dscore], [dm8])
            tr.op("dve", lambda: nc.vector.match_replace(out=work, in_to_replace=m8[:, 0:8], in_values=score,
                                                         imm_value=-3.0e38), [dm8, dscore], [dwork])
            tr.op("dve", lambda: nc.vector.max(out=m8[:, 8:16], in_=work), [dwork], [dm8])
            ts("dve", selt, score, m8[:, 15:16], None, ALU.is_ge, None, [dscore, dm8], [dselt])
            tt("dve", selt, selt, allow[:, qi, :], ALU.mult, [CONST2], [dselt])
            ts("dve", selt, selt, -1.0, BIG, ALU.add, ALU.mult, [], [dselt])

        def C3(i):
            tp(PS[5][0:32, 128:256], selt, identf, [dselt, CONST], [PSd[5]])
            cp("act", nsT4s[i % 2][0:32], PS[5][0:32, 128:256].unsqueeze(1).to_broadcast([32, 4, 128]), [PSd[5]], [dnsT4s[i % 2]])

        def B(i, nxt):
            qi, g, qrhs, Rq = qinfo(i)
            qt = 8 + qi
            ns, dns = nsT4s[i % 2], dnsT4s[i % 2]
            for kt in range(qt + 1):
                masks = [(Esel[:, kt * 128:(kt + 1) * 128], ns, [CONST2, dns])]
                if kt == qt:
                    masks.append((identb, masks4[:, 0, :], [CONST, CONST2]))
                tile_attend(kslcT[:, g, kt * 128:(kt + 1) * 128], vslc[:, kt, g, :], masks, qrhs,
                            Rq + [dkslc, dvslc], kt == 0, kt == qt)
                if kt == 3 and nxt is not None:
                    C2(nxt)
            flush_pv()
            if nxt is not None:
                C3(nxt)
            finish_branch2(1, False, i % 2)
            for wi in range(5):
                kt = qt - 4 + wi
                w = kt - 4
                masks = []
                if wi == 0:
                    masks.append((identb, masks4[:, 3 if kt < 8 else 1, :], [CONST, CONST2]))
                elif wi == 4:
                    masks.append((identb, masks4[:, 0, :], [CONST, CONST2]))
                elif kt < 8:
                    masks.append((identb, masks4[:, 2, :], [CONST, CONST2]))
                tile_attend(kwinT[:, g, w * 128:(w + 1) * 128], vwin[:, w, g, :], masks, qrhs,
                            Rq + [dkwin, dvwin], wi == 0, wi == 4)
            flush_pv()
            finish_branch2(2, False, i % 2)
            dm = [dmix[4 * g + r][qi // 4] for r in range(4)]
            mo = mixT[:, 4 * g:4 * g + 4, qi * 128:(qi + 1) * 128]
            tt("dve", mo, oaccs[i % 2].rearrange("p (r i) -> p r i", r=4), mo, ALU.mult, [doaccs[i % 2]], dm)

        C1(0); C2(0); C3(0)
        for i in range(len(its)):
            nxt = i + 1 if i + 1 < len(its) else None
            if nxt is not None:
                C1(nxt)
            B(i, nxt)

    mF = ar.mark()
    attn_prompt()
    tr.barrier()
    ar.release(mF)


    def sample_phase():
        nonlocal ar
        SOFF = ar.offs["qT"]
        assert SOFF + 44 * 1024 <= m0
        ar.top = mKeep
        CS = Dep("const_s")
        sbias = cload([128, 32], F32, sbias_d, dep=CS); mnew = cload([128, 16], BF16, mnew_d, dep=CS)
        mwin0 = cload([128, 16], BF16, mwin0_d, dep=CS)
        perm = cload([128, 128], BF16, perm_d, dep=CS)
        pti = ar.alloc([128, NSEQ * 16], I32, "pti"); dpti = Dep("pti")
        iop = ar.alloc([128, 1], I32, "iop"); diop = Dep("iop")
        idx = ar.alloc([128, NSEQ * 16], I32, "idx"); didx = Dep("idx")
        tr.dma("sp", pti, ptab_d.rearrange("(o s) p -> o (s p)", o=1).partition_broadcast(128), dpti)
        tr.op("pool", lambda: nc.gpsimd.iota(iop, pattern=[[0, 1]], base=0, channel_multiplier=1), [], [diop])
        ts("dve", idx, pti, 128, iop[:, 0:1], ALU.mult, ALU.add, [dpti, diop], [didx])
        pgs = [ar.alloc_at(SOFF, [128, 16, 1024], BF16, "pg0"), ar.alloc([128, 16, 1024], BF16, "pg1")]
        dpgs = [Dep("pg0"), Dep("pg1")]
        kcmpS = ar.alloc_at(SOFF + 32768, [128, 2, 16, 128], BF16, "kcmpS"); dkcmpS = Dep("kcmpS")
        vcmpS = ar.alloc([128, 2, 16, 128], BF16, "vcmpS"); dvcmpS = Dep("vcmpS")
        kslcS = ar.alloc([128, 2, 2048], BF16, "kslcS"); dkslcS = Dep("kslcS")
        wps = [ar.alloc_at(SOFF + 40960, [128, 4, 512], BF16, "wp0"), ar.alloc([128, 4, 512], BF16, "wp1")]
        dwps = [Dep("wp0"), Dep("wp1")]
        kwS = ar.alloc([128, 2, 512], BF16, "kwS"); dkwS = Dep("kwS")
        vnew = ar.alloc([128, 4, 128], BF16, "vnew"); dvnew = Dep("vnew")
        kcS = ar.alloc([128, 2, 127], BF16, "kcS"); dkcS = Dep("kcS")
        vcS = ar.alloc([127, 2, 128], BF16, "vcS"); dvcS = Dep("vcS")
        hb2 = ar.alloc([128, 2, 127], BF16, "hb2"); dhb2 = Dep("hb2")
        PfS = ar.alloc([128, 32], F32, "PfS"); dPfS = Dep("PfS")
        PbS = ar.alloc([128, 32], BF16, "PbS"); dPbS = Dep("PbS")
        pnS = ar.alloc([128, 32], F32, "pnS"); dpnS = Dep("pnS")
        psTS = ar.alloc([128, 8], F32, "psTS"); dpsTS = Dep("psTS")
        rsS = ar.alloc([128, 32], F32, "rsS"); drsS = Dep("rsS")
        coefS = ar.alloc([128, 32], F32, "coefS"); dcoefS = Dep("coefS")
        tmpS = ar.alloc([128, 32], F32, "tmpS"); dtmpS = Dep("tmpS")
        accS = ar.alloc([128, 32], F32, "accS"); daccS = Dep("accS")
        scS = ar.alloc([128, 32], F32, "scS"); dscS = Dep("scS")
        wkS = ar.alloc([128, 32], F32, "wkS"); dwkS = Dep("wkS")
        m8S = ar.alloc([128, 16], F32, "m8S"); dm8S = Dep("m8S")
        selS = ar.alloc([128, 32], F32, "selS"); dselS = Dep("selS")
        nsS = ar.alloc([128, 2, 4, 4], BF16, "nsS"); dnsS = Dep("nsS")
        Pk = ar.alloc([128, 17, 16], BF16, "Pk"); dPk = Dep("Pk")
        gB = ar.alloc([128, 24, NST], F32, "gB"); dgB = Dep("gB")
        for t_, d_ in ((PfS, dPfS), (psTS, dpsTS), (nsS, dnsS), (Pk, dPk), (vnew, dvnew), (scS, dscS), (selS, dselS)):
            tr.op("pool", lambda t_=t_: nc.gpsimd.memset(t_, 0.0), [], [d_])
        dsgs = sfm["sg"][1]; dqs = sfm["q"][1]; dks = sfm["ksT"][1]
        for k in range(24):
            pb = k // 8
            mm(PS[pb][:, (k % 8) * 64:(k % 8 + 1) * 64], selg[:, k, :], sgs, True, True, [CONST2, dsgs], [PSd[pb]])
            if k % 8 == 7:
                cp("dve", gB[:, pb * 8:(pb + 1) * 8, :], PS[pb].rearrange("p (k t) -> p k t", k=8), [PSd[pb]], [dgB])
        trr = [0]
        srr = [0]

        def trbank():
            b = trr[0] % 2; trr[0] += 1
            return b

        def sbank():
            b = 2 + srr[0] % 2; srr[0] += 1
            return b

        def branch_finish(g, br, first, s):
            sl = slice(g * 16, (g + 1) * 16)
            ts("dve", rsS[:, sl], PS[5][:, 0:16], 1e-30, None, ALU.max, None, [PSd[5]], [drsS])
            tr.op("dve", lambda: nc.vector.reciprocal(out=rsS[:, sl], in_=rsS[:, sl]), [drsS], [drsS])
            gate = gB[:, br * 8 + 4 * g: br * 8 + 4 * g + 4, 4 * s:4 * s + 4]
            tt("dve", coefS[:, sl].rearrange("p (r t) -> p r t", r=4), rsS[:, sl].rearrange("p (r t) -> p r t", r=4), gate,
               ALU.mult, [drsS, dgB], [dcoefS])
            tt("dve", tmpS[:, sl], PS[4][:, 0:16], coefS[:, sl], ALU.mult, [PSd[4], dcoefS], [dtmpS])
            tt("dve", accS[:, sl], accS[:, sl], tmpS[:, sl], ALU.add, [dtmpS], [daccS])

        def prefetch(s):
            for p_ in range(16):
                tr.idma(pgs[s % 2][:, p_, :], cache, idx[:, s * 16 + p_: s * 16 + p_ + 1], dpgs[s % 2], extraR=[didx])
            tr.dma("pool", wps[s % 2], swin[s].rearrange("(a p) c -> p a c", p=128), dwps[s % 2])

        prefetch(0)
        for s in range(NSEQ):
            if s + 1 < NSEQ:
                prefetch(s + 1)
            pg, dpg, wp, dwp = pgs[s % 2], dpgs[s % 2], wps[s % 2], dwps[s % 2]
            for slot, (dstT, dd) in enumerate(((kcmpS, dkcmpS), (vcmpS, dvcmpS))):
                for g in range(2):
                    for pgp in range(4):
                        b = trbank()
                        for a in range(4):
                            mm(PS[b][:, a * 128:(a + 1) * 128], pg[:, pgp * 4 + a, slot * 256 + g * 128: slot * 256 + (g + 1) * 128],
                               perm, True, True, [dpg, CS], [PSd[b]])
                        cp("act" if (g + pgp) % 2 else "dve",
                           dstT[:, g, :, pgp * 32:(pgp + 1) * 32].rearrange("p ph (a m) -> p ph a m", a=4),
                           PS[b].rearrange("p (a ph m) -> p ph a m", a=4, ph=16), [PSd[b]], [dd])
            compress(kcmpS, vcmpS, [dkcmpS, dvcmpS], kcS, vcS, dkcS, dvcS, hb2, dhb2)
            sb = sbank()
            for g in range(2):
                mm(PS[sb][0:127, g * 16:(g + 1) * 16], kcS[:, g, :], qs[:, 4 * g:4 * g + 4, 4 * s:4 * s + 4], True, True,
                   [dkcS, dqs], [PSd[sb]])
            act(PfS[0:127, :], PS[sb][0:127, 0:32], AF.Exp, [PSd[sb]], [dPfS], scale=SCL)
            act(PbS[0:127, :], PS[sb][0:127, 0:32], AF.Exp, [PSd[sb]], [dPbS], scale=SCL)
            mm(PS[5][:, 0:32], onesf, PfS, True, True, [dPfS, CONST], [PSd[5]])
            for g in range(2):
                mm(PS[4][:, g * 16:(g + 1) * 16], vcS[:, g, :], PbS[0:127, g * 16:(g + 1) * 16], True, True, [dPbS, dvcS], [PSd[4]])
            ts("dve", rsS, PS[5][:, 0:32], 1e-30, None, ALU.max, None, [PSd[5]], [drsS])
            tr.op("dve", lambda: nc.vector.reciprocal(out=rsS, in_=rsS), [drsS], [drsS])
            tt("dve", coefS.rearrange("p (h t) -> p h t", h=8), rsS.rearrange("p (h t) -> p h t", h=8),
               gB[:, 0:8, 4 * s:4 * s + 4], ALU.mult, [drsS, dgB], [dcoefS])
            tt("dve", accS, PS[4][:, 0:32], coefS, ALU.mult, [PSd[4], dcoefS], [daccS])
            tt("dve", pnS[0:127, :], PfS[0:127, :], rsS[0:127, :], ALU.mult, [dPfS, drsS], [dpnS])
            tr.op("dve", lambda: nc.vector.tensor_reduce(
                out=psTS[0:127, :].rearrange("p (g t) -> p g t", g=2),
                in_=pnS[0:127, :].rearrange("p (g r t) -> p g t r", g=2, r=4), axis=AX.X, op=ALU.add), [dpnS], [dpsTS])
            for g in range(2):
                for hb in range(2):
                    b = trbank()
                    pv = psbf(b).rearrange("p (a t) -> p a t", a=8)
                    for a in range(8):
                        tp(pv[:, a, :], pg[:, hb * 8 + a, 512 + g * 128: 512 + (g + 1) * 128], identb, [dpg, CONST], [PSd[b]])
                    cp("act", kslcS[:, g, hb * 1024:(hb + 1) * 1024], psbf(b), [PSd[b]], [dkslcS])
            sb = sbank()
            mm(PS[sb][0:8, 0:32], psTS, ovl, True, True, [dpsTS, CONST2], [PSd[sb]])
            tt("dve", scS[0:8, :], PS[sb][0:8, 0:32], sbias[0:8, :], ALU.add, [PSd[sb], CS], [dscS])
            tr.op("dve", lambda: nc.vector.max(out=m8S[0:8, 0:8], in_=scS[0:8, :]), [dscS], [dm8S])
            tr.op("dve", lambda: nc.vector.match_replace(out=wkS[0:8, :], in_to_replace=m8S[0:8, 0:8], in_values=scS[0:8, :],
                                                         imm_value=-3.0e38), [dm8S, dscS], [dwkS])
            tr.op("dve", lambda: nc.vector.max(out=m8S[0:8, 8:16], in_=wkS[0:8, :]), [dwkS], [dm8S])
            ts("dve", selS[0:8, :], scS[0:8, :], m8S[0:8, 14:15], None, ALU.is_ge, None, [dscS, dm8S], [dselS])
            ts("dve", selS[0:8, :], selS[0:8, :], -1.0, BIG, ALU.add, ALU.mult, [], [dselS])
            b = trbank()
            pv = psbf(b).rearrange("p (g a t) -> p g a t", g=2, a=4)
            for g in range(2):
                for a in range(4):
                    tp(pv[:, g, a, :], wp[:, a, g * 128:(g + 1) * 128], identb, [dwp, CONST], [PSd[b]])
            cp("act", kwS, psbf(b).rearrange("p (g t) -> p g t", g=2), [PSd[b]], [dkwS])
            b = trbank()
            for c4 in range(4):
                src_chunk = (6 + c4) if c4 < 2 else (10 + c4 - 2)
                tp(psbf(b)[0:4, c4 * 128:(c4 + 1) * 128], ksT[:, src_chunk, 4 * s:4 * s + 4], identb, [dks, CONST], [PSd[b]])
            cp("act", vnew[0:4], psbf(b)[0:4, 0:512].rearrange("p (c d) -> p c d", c=4), [PSd[b]], [dvnew])
            sb = sbank()
            tp(PS[sb][0:32, 0:8], selS[0:8, :], identf[0:8, 0:8], [dselS, CONST], [PSd[sb]])
            cp("dve", nsS[0:32], PS[sb][0:32, 0:8].rearrange("p (g t) -> p g t", g=2).unsqueeze(2).to_broadcast([32, 2, 4, 4]),
               [PSd[sb]], [dnsS])
            for g in range(2):
                qg = qs[:, 4 * g:4 * g + 4, 4 * s:4 * s + 4]
                sb = sbank()
                for kt in range(16):
                    mm(PS[sb][:, kt * 16:(kt + 1) * 16], kslcS[:, g, kt * 128:(kt + 1) * 128], qg, True, False, [dkslcS, dqs], [PSd[sb]])
                    mm(PS[sb][:, kt * 16:(kt + 1) * 16], Esel[:, kt * 128:(kt + 1) * 128], nsS[:, g], False, True, [CONST2, dnsS], [PSd[sb]])
                mm(PS[sb][0:4, 256:272], ksT[:, 4 + g, 4 * s:4 * s + 4], qg, True, False, [dks, dqs], [PSd[sb]])
                mm(PS[sb][0:4, 256:272], identb[:, 0:4], mnew, False, True, [CONST, CS], [PSd[sb]])
                act(Pk[:, 0:16, :], PS[sb][:, 0:256].rearrange("p (k c) -> p k c", k=16), AF.Exp, [PSd[sb]], [dPk], scale=SCL)
                act(Pk[0:4, 16, :], PS[sb][0:4, 256:272], AF.Exp, [PSd[sb]], [dPk], scale=SCL)
                for kt in range(17):
                    va = pg[:, kt, 768 + g * 128: 768 + (g + 1) * 128] if kt < 16 else vnew[:, g, :]
                    mm(PS[4][:, 0:16], va, Pk[:, kt, :], kt == 0, kt == 16, [dpg, dvnew, dPk], [PSd[4]])
                for kt in range(17):
                    mm(PS[5][:, 0:16], onesb, Pk[:, kt, :], kt == 0, kt == 16, [CONST, dPk], [PSd[5]])
                branch_finish(g, 1, False, s)
                sb = sbank()
                for a in range(4):
                    mm(PS[sb][:, a * 16:(a + 1) * 16], kwS[:, g, a * 128:(a + 1) * 128], qg, True, a != 0, [dkwS, dqs], [PSd[sb]])
                    if a == 0:
                        mm(PS[sb][:, 0:16], identb, mwin0, False, True, [CONST, CS], [PSd[sb]])
                mm(PS[sb][0:4, 256:272], ksT[:, 8 + g, 4 * s:4 * s + 4], qg, True, False, [dks, dqs], [PSd[sb]])
                mm(PS[sb][0:4, 256:272], identb[:, 0:4], mnew, False, True, [CONST, CS], [PSd[sb]])
                act(Pk[:, 0:4, :], PS[sb][:, 0:64].rearrange("p (k c) -> p k c", k=4), AF.Exp, [PSd[sb]], [dPk], scale=SCL)
                act(Pk[0:4, 16, :], PS[sb][0:4, 256:272], AF.Exp, [PSd[sb]], [dPk], scale=SCL)
                for a in range(5):
                    va = wp[:, a, 256 + g * 128: 256 + (g + 1) * 128] if a < 4 else vnew[:, 2 + g, :]
                    pk = Pk[:, a, :] if a < 4 else Pk[:, 16, :]
                    mm(PS[4][:, 0:16], va, pk, a == 0, a == 4, [dwp, dvnew, dPk], [PSd[4]])
                for a in range(5):
                    pk = Pk[:, a, :] if a < 4 else Pk[:, 16, :]
                    mm(PS[5][:, 0:16], onesb, pk, a == 0, a == 4, [CONST, dPk], [PSd[5]])
                branch_finish(g, 2, False, s)
            tt("dve", mixTs[:, 0:8, 4 * s:4 * s + 4], accS.rearrange("p (h t) -> p h t", h=8), zns[:, :, 4 * s:4 * s + 4],
               ALU.mult, [daccS, sfm["zn"][1]], [dmixs])

        tr.barrier()
        ar_main = ar
        ar = Arena(nc, lo=SOFF, hi=m0)
        ucat = ar.alloc([128, 8, NSEQ, 34], F32, "ucat"); ducat = Dep("ucat")
        sct = [ar.alloc([30, 1024], F32, "sct%d" % i) for i in range(2)]; dsct = [Dep("sct%d" % i) for i in range(2)]
        dus = sfm["u"][1]
        for s in range(NSEQ):
            st_, dst2 = sct[s % 2], dsct[s % 2]
            tr.dma("sp", st_, sconv[s], dst2)
            b = s % 2
            for j in range(8):
                tp(PS[b][:, j * 30:(j + 1) * 30], st_[:, j * 128:(j + 1) * 128], identf[0:30, 0:30], [dst2, CONST], [PSd[b]])
            cp("dve" if s % 2 else "act", ucat[:, :, s, 0:30], PS[b][:, 0:240].rearrange("p (j t) -> p j t", j=8), [PSd[b]], [ducat])
        cp("dve", ucat[:, :, :, 30:34], us.rearrange("p j (s t) -> p j s t", t=4), [dus], [ducat])
        cacS = ar.alloc([128, 8, NSEQ, 4], F32, "cacS"); dcacS = Dep("cacS")
        ctmp = ar.alloc([128, 8, NSEQ, 4], F32, "ctmp"); dctmp = Dep("ctmp")
        accb = ar.alloc([128, 8, NST], BF16, "accb"); daccb = Dep("accb")

        def bc(ap2):
            return ap2.unsqueeze(2).unsqueeze(3).to_broadcast([128, 8, NSEQ, 4])

        for w in range(31):
            src = ucat[:, :, :, w:w + 4]
            if w == 0:
                tt("dve", cacS, src, bc(cdw[:, :, 0]), ALU.mult, [ducat, CONST], [dcacS])
                tt("dve", cacS, cacS, bc(cdb), ALU.add, [CONST], [dcacS])
            else:
                tt("dve", ctmp, src, bc(cdw[:, :, w]), ALU.mult, [ducat, CONST], [dctmp])
                tt("dve", cacS, cacS, ctmp, ALU.add, [dctmp], [dcacS])
        cp("dve", accb, cacS.rearrange("p j s t -> p j (s t)"), [dcacS], [daccb])
        sqS = ar.alloc([128, NST], BF16, "sqS"); dsqS = Dep("sqS")
        meanS = ar.alloc([128, NST], F32, "meanS"); dmeanS = Dep("meanS")
        rstdS = ar.alloc([128, NST], F32, "rstdS"); drstdS = Dep("rstdS")
        tnS = ar.alloc([128, NST], F32, "tnS"); dtnS = Dep("tnS")
        for j in range(8):
            mm(PS[6][:, 0:NST], onesb, accb[:, j, :], j == 0, j == 7, [CONST, daccb], [PSd[6]])
        for j in range(8):
            act(sqS, accb[:, j, :], AF.Square, [daccb], [dsqS])
            mm(PS[7][:, 0:NST], onesb, sqS, j == 0, j == 7, [CONST, dsqS], [PSd[7]])
        ts("dve", meanS, PS[6][:, 0:NST], 1.0 / 1024, None, ALU.mult, None, [PSd[6]], [dmeanS])
        tt("dve", tnS, meanS, meanS, ALU.mult, [dmeanS], [dtnS])
        stt("dve", tnS, PS[7][:, 0:NST], 1.0 / 1024, tnS, ALU.mult, ALU.subtract, [PSd[7]], [dtnS])
        act(tnS, tnS, AF.Sqrt, [dtnS], [dtnS], bias=EPS, scale=1.0)
        tr.op("dve", lambda: nc.vector.reciprocal(out=rstdS, in_=tnS), [dtnS], [drstdS])
        for j in range(8):
            tt("dve", tnS, accb[:, j, :], meanS, ALU.subtract, [daccb, dmeanS], [dtnS])
            tt("dve", tnS, tnS, rstdS, ALU.mult, [drstdS], [dtnS])
            act(tnS, tnS, AF.Silu, [dtnS], [dtnS], bias=lnb[:, j:j + 1], scale=lng[:, j:j + 1])
            tt("dve", mixTs[:, 8 + j, :], tnS, zcs[:, j, :], ALU.mult, [dtnS, sfm["zc"][1]], [dmixs])
        dd1 = Dep("o_conv_s"); dd2 = Dep("o_win_s")
        tr.dma("sp", conv_s[:, 0:26, :], sconv[:, 4:30, :], dd1)
        tr.dma("sp", win_s[:, 0:508, :], swin[:, 4:512, :], dd2)
        utm = ar.alloc([NST, 1024], F32, "utm"); dutm = Dep("utm")
        for j in range(8):
            tp(PS[j // 4][0:NST, (j % 4) * 128:(j % 4 + 1) * 128], us[:, j, :], identf, [dus, CONST], [PSd[j // 4]])
        for hb in range(2):
            cp("dve", utm[:, hb * 512:(hb + 1) * 512], PS[hb][0:NST, :], [PSd[hb]], [dutm])
        tr.dma("sp", conv_s[:, 26:30, :], utm, dutm, load=False)
        tr.dma("sp", win_s[:, 508:512, :], kvs[:, 1024:1536], sfm["kvs"][1], load=False)
        ar = ar_main

    sample_phase()
    tr.barrier()

    WOFF = ar.offs["qT"]
    wo = ar.alloc_at(WOFF, [128, KC, D], BF16, "wo"); dwo = Dep("wo")
    GTOP = WOFF + KC * D * 2
    wov = w_out.rearrange("(k p) c -> p k c", p=128)
    for cb in range(4):
        for hk in range(2):
            tr.dma("pool", wo[:, hk * 8:(hk + 1) * 8, cb * 512:(cb + 1) * 512],
                   wov[:, hk * 8:(hk + 1) * 8, cb * 512:(cb + 1) * 512], dwo)
    ar.top = GTOP
    G2p = ar.alloc([128, D], F32, "G2p"); dG2p = Dep("G2p")
    for cb in range(4):
        mm(PS[cb], rp, G2[:, cb * 512:(cb + 1) * 512], True, True, [CONST, dG2], [PSd[cb]])
        cp("dve", G2p[:, cb * 512:(cb + 1) * 512], PS[cb], [PSd[cb]], [dG2p])
    mixo = [ar.alloc([128, D], F32, "mixo%d" % i) for i in range(2)]; dmixo = [Dep("mixo%d" % i) for i in range(2)]
    xr = [ar.alloc([128, D], F32, "xr%d" % i) for i in range(2)]; dxr = [Dep("xr%d" % i) for i in range(2)]
    sso = ar.alloc([128, 8], F32, "sso"); dsso = Dep("sso")
    jk = ar.alloc([128, 512], BF16, "jk"); djk = Dep("jk")

    def out_proj(ntok, lhs_fn, Rl, G2x, dG2x, xsrc, ydst, it):
        mo, dmo = mixo[it % 2], dmixo[it % 2]
        xt_, dxt_ = xr[it % 2], dxr[it % 2]
        tr.dma("sp", xt_[0:ntok, :], xsrc, dxt_)
        for cb in range(4):
            pb = (it % 2) * 4 + cb
            for k in range(KC):
                mm(PS[pb][0:ntok, :], lhs_fn(k), wo[:, k, cb * 512:(cb + 1) * 512], k == 0, k == KC - 1, Rl + [dwo], [PSd[pb]])
            act(jk[0:ntok, :], PS[pb][0:ntok, :], AF.Square, [PSd[pb]], [djk, dsso], accum=sso[0:ntok, cb:cb + 1])
            cp("dve", mo[0:ntok, cb * 512:(cb + 1) * 512], PS[pb][0:ntok, :], [PSd[pb], djk], [dmo])
        tr.op("dve", lambda: nc.vector.tensor_reduce(out=sso[0:ntok, 4:5], in_=sso[0:ntok, 0:4], axis=AX.X, op=ALU.add),
              [dsso], [dsso])
        act(sso[0:ntok, 5:6], sso[0:ntok, 4:5], AF.Sqrt, [dsso], [dsso], bias=EPS, scale=1.0 / D)
        tr.op("dve", lambda: nc.vector.reciprocal(out=sso[0:ntok, 6:7], in_=sso[0:ntok, 5:6]), [dsso], [dsso])
        stt("dve", mo[0:ntok, :], mo[0:ntok, :], sso[0:ntok, 6:7], G2x[0:ntok, :], ALU.mult, ALU.mult, [dsso, dG2x], [dmo])
        tt("pool", mo[0:ntok, :], mo[0:ntok, :], xt_[0:ntok, :], ALU.add, [dxt_], [dmo])
        tr.dma("sp", ydst, mo[0:ntok, :], dmo, load=False)

    G2s = ar.alloc([NST, D], F32, "G2s"); dG2s = Dep("G2s")
    for cb in range(4):
        mm(PS[cb][0:NST, :], rsel, G2[:, cb * 512:(cb + 1) * 512], True, True, [CONST, dG2], [PSd[cb]])
        cp("dve", G2s[:, cb * 512:(cb + 1) * 512], PS[cb][0:NST, :], [PSd[cb]], [dG2s])
    out_proj(NST, (lambda k: mixTs[:, k, :]), [dmixs], G2s, dG2s, xs, y_s, 8)
    for t in range(8):
        out_proj(128, (lambda k, t=t: mixT[:, k, t * 128:(t + 1) * 128]), [dmix[c][t // 4] for c in range(16)],
                 G2p, dG2p, xo[t * 128:(t + 1) * 128, :], y_p[t * 128:(t + 1) * 128, :], t)

    if dbg:
        o = dout("d_mixT", list(mixT.shape), BF16)
        tr.dma("sp", o, mixT, Dep("d_mixT"), load=False, extraR=[d for row in dmix for d in row])

    tr.finish()
    return nc


def _consts(half):
    v = float(half)
    c = {}
    c["ident_bf"] = np.eye(128, dtype=np.float32).astype(ml_dtypes.bfloat16)
    c["ident_f"] = np.eye(128, dtype=np.float32)
    c["ones_bf"] = np.ones((128, 128), np.float32).astype(ml_dtypes.bfloat16)
    c["ones_f"] = np.ones((128, 128), np.float32)
    j = np.arange(128)[:, None]
    i = np.arange(128)[None, :]
    negtri = np.where(j <= i, 0.0, -BIG)
    neglow = np.where(j >= i, 0.0, -BIG)
    ctxmid = np.full((128, 128), -BIG * (1 - v))
    ctxlow = np.minimum(neglow, ctxmid) if v == 0 else neglow
    m4 = np.stack([np.tile(m[:, None, :], (1, 4, 1)).reshape(128, 512) for m in (negtri, neglow, ctxmid, ctxlow)], axis=1)
    c["masks4"] = m4.astype(np.float32).astype(ml_dtypes.bfloat16)
    m = np.arange(127)[:, None, None]
    qi = np.arange(8)[None, :, None]
    ii = np.arange(128)[None, None, :]
    valid = (16 * m + 31 <= 1024 + 128 * qi + ii) & (m >= 64 * (1 - half))
    ncm = np.where(valid, 0.0, -BIG)
    c["negcmp"] = np.tile(ncm[:, :, None, :], (1, 1, 4, 1)).reshape(127, 8, 512).astype(np.float32).astype(ml_dtypes.bfloat16)
    s = np.arange(128)[:, None]
    key = np.arange(2048)[None, :]
    c["Esel"] = (key // 64 == s).astype(np.float32).astype(ml_dtypes.bfloat16)
    mm_ = np.arange(127)[:, None] * 16
    sj = np.arange(32)[None, :] * 64
    c["overlap"] = np.concatenate([((mm_ < sj + 64) & (mm_ + 32 > sj)).astype(np.float32), np.zeros((1, 32), np.float32)], 0)
    tl = 1024 + 128 * np.arange(8)[None, :, None] + np.arange(128)[:, None, None]
    sb = np.arange(32)[None, None, :]
    s0 = 16 * (1 - half)
    allowed = (64 * sb <= tl) & (sb >= s0)
    cur = tl // 64
    forced = (sb == s0) | (sb == cur) | (sb == cur - 1)
    c["selbias"] = np.where(allowed, np.where(forced, 1e6, 0.0), -1e30).astype(np.float32)
    c["allowed"] = allowed.astype(np.float32)
    sg_ = np.zeros((128, 24, 128), np.float32)
    for k in range(24):
        sg_[k, k, :] = 1.0
    c["selg"] = sg_.astype(ml_dtypes.bfloat16)
    rp = np.zeros((128, 128), np.float32); rp[16, :] = 1.0
    rs = np.zeros((128, 64), np.float32)
    for s_ in range(16):
        rs[s_, 4 * s_:4 * s_ + 4] = 1.0
    c["rp"] = rp; c["rs"] = rs
    c["cval"] = np.full((128, 1), v, np.float32)
    sb_ = np.zeros((128, 32), np.float32); sb_[:, 0] = 1e6; sb_[:, 31] = 1e6
    c["sbias"] = sb_
    pm = np.zeros((128, 128), np.float32)
    for mp in range(8):
        for ph in range(16):
            pm[16 * mp + ph, ph * 8 + mp] = 1.0
    c["perm16"] = pm.astype(ml_dtypes.bfloat16)
    tq = np.arange(16)[None, :] % 4
    jp = np.arange(128)[:, None]
    c["mnew"] = np.where((jp < 4) & (jp > tq), -BIG, 0.0).astype(np.float32).astype(ml_dtypes.bfloat16)
    c["mwin0"] = np.where(jp < tq, -BIG, 0.0).astype(np.float32).astype(ml_dtypes.bfloat16)
    return c


def fm16(vec):
    return np.ascontiguousarray(vec.reshape(-1, 128).T)


def core_inputs(inp, core, nb_prompt, seqs):
    b, half = core // 2, core % 2
    f = lambda a: np.ascontiguousarray(np.asarray(a, dtype=np.float32))
    xp = np.asarray(inp["x_prompt"])[b]
    m = {}
    m["xo"] = f(xp[half * 1024:(half + 1) * 1024])
    m["xc"] = f(xp[0:1024])
    m["xs"] = f(np.asarray(inp["x_sample"])[seqs].reshape(NST, D))
    c17 = np.concatenate([np.asarray(inp["c_sample"])[seqs], np.asarray(inp["c_prompt"])[b:b + 1]], 0)
    m["cT"] = f(c17.T.reshape(KC, 128, 17).transpose(1, 0, 2))
    m["w_ada"] = f(inp["w_ada"][0]); m["b_ada"] = f(inp["b_ada"][0][None, :])
    m["npre_fm"] = f(fm16(np.asarray(inp["norm_pre"][0]))); m["npost"] = f(np.asarray(inp["norm_post"][0])[None, :])
    m["w_in"] = f(inp["w_in"][0]); m["w_out"] = f(inp["w_out"][0])
    m["cmp_w1"] = f(inp["cmp_w1"][0])
    m["cmp_peT"] = f(np.asarray(inp["cmp_pe"][0]).transpose(2, 0, 1))
    m["cmp_b1T"] = f(np.asarray(inp["cmp_b1"][0]).T); m["cmp_w2"] = f(inp["cmp_w2"][0])
    m["cmp_b2T"] = f(np.asarray(inp["cmp_b2"][0]).T); m["cmp_b2v"] = f(np.asarray(inp["cmp_b2"][0])[1:2, :])
    m["conv_dwT"] = f(np.asarray(inp["conv_dw"][0]).T.reshape(8, 128, 31).transpose(1, 0, 2))
    for nm, key in (("conv_dbT", "conv_db"), ("conv_lngT", "conv_ln_g"), ("conv_lnbT", "conv_ln_b")):
        m[nm] = f(np.asarray(inp[key][0]).reshape(8, 128).T)
    ck = np.asarray(inp["cache_kv"][0])
    m["cache"] = ck.reshape(ck.shape[0] * 128, 1024)
    m["ptab"] = np.ascontiguousarray(np.asarray(inp["page_table"])[seqs].astype(np.int32))
    m["swin"] = f(np.asarray(inp["state_win_kv"][0])[seqs].reshape(NSEQ, 512, 512))
    m["sconv"] = f(np.asarray(inp["state_conv"][0])[seqs])
    m.update(_consts(half))
    return m


_NC_CACHE = {}


def run(inputs, cores, n_phys, dbg=False):
    key = (n_phys, dbg)
    nc = build(n_phys=n_phys, dbg=dbg)
    in_maps = []
    for ci, core in enumerate(cores):
        seqs = np.arange(ci * NSEQ, (ci + 1) * NSEQ)
        in_maps.append(core_inputs(inputs, core, None, seqs))
    res = run_bass_kernel_spmd(nc, in_maps, core_ids=list(range(len(cores))))
    return res.results


def kernel(**inputs):
    n_phys = int(np.asarray(inputs["cache_kv"]).shape[1])
    res = run(inputs, list(range(8)), n_phys)
    B = 4
    y_p = np.zeros((B, 2048, D), np.float32)
    kv_p = np.zeros((1, B, 2048, 4, 2, 128), np.float32)
    win_p = np.zeros((1, B, 512, 2, 2, 128), np.float32)
    conv_p = np.zeros((1, B, 30, 1024), np.float32)
    y_s = np.zeros((128, 4, D), np.float32)
    kv_s = np.zeros((1, 128, 4, 4, 2, 128), np.float32)
    win_s = np.zeros((1, 128, 512, 2, 2, 128), np.float32)
    conv_s = np.zeros((1, 128, 30, 1024), np.float32)
    for c in range(8):
        r = res[c]
        b, half = c // 2, c % 2
        y_p[b, half * 1024:(half + 1) * 1024] = r["y_p"]
        kv_p[0, b, half * 1024:(half + 1) * 1024] = r["kv_p"].reshape(1024, 4, 2, 128)
        if half == 1:
            win_p[0, b] = r["win_p"].reshape(512, 2, 2, 128)
            conv_p[0, b] = r["conv_p"][2:32]
        sl = slice(c * NSEQ, (c + 1) * NSEQ)
        y_s[sl] = r["y_s"].reshape(NSEQ, 4, D)
        kv_s[0, sl] = r["kv_s"].reshape(NSEQ, 4, 4, 2, 128)
        win_s[0, sl] = r["win_s"].reshape(NSEQ, 512, 2, 2, 128)
        conv_s[0, sl] = r["conv_s"]
    return (y_p, y_s, kv_p, kv_s, win_p, win_s, conv_p, conv_s)
```

```python
import numpy as np
import ml_dtypes
import concourse.bass as bass
import concourse.mybir as mybir
from concourse.bass_utils import run_bass_kernel_spmd

F32 = mybir.dt.float32
BF16 = mybir.dt.bfloat16
I32 = mybir.dt.int32
AF = mybir.ActivationFunctionType
ALU = mybir.AluOpType
AX = mybir.AxisListType

D = 2048
KC = 16
NOWN = 1024
NCTX = 1024
NSEQ = 16
NST = 64
PAST = 2048
INC = 6680
C_Q, C_KV, C_WIN, C_G, C_ZN, C_A, C_GL, C_ZC = 0, 1024, 2048, 2560, 2584, 3608, 4632, 5656
BIG = 30000.0
SCL = 128 ** -0.5
EPS = 1e-6


class Dep:
    __slots__ = ("w", "r", "name", "dsem", "excl", "npair")

    def __init__(self, name="", excl=False):
        self.w = None
        self.r = {}
        self.name = name
        self.dsem = None
        self.npair = 0
        self.excl = excl


class Tr:
    def __init__(self, nc):
        self.nc = nc
        self.E = {"pe": nc.tensor, "act": nc.scalar, "dve": nc.vector, "pool": nc.gpsimd, "sp": nc.sync}
        self.sem = {k: nc.alloc_semaphore("S_" + k) for k in ("pe", "act", "dve", "pool")}
        self.cnt = {k: 0 for k in self.sem}
        self.seen = {k: {} for k in self.E}
        self.dsems = {}
        self.ninst = 0

    def _wait(self, eng, key, val):
        if self.seen[eng].get(key, 0) >= val:
            return
        self.seen[eng][key] = val
        sem = self.sem[key] if key in self.sem else self.dsems[key][0]
        self.E[eng].wait_ge(sem, val)

    def _need(self, eng, key, val):
        if key == eng and eng == "pe":
            return
        self._wait(eng, key, val)

    def _deps(self, eng, R, W):
        for d in R:
            if d.w is not None:
                self._need(eng, *d.w)
            if d.excl:
                for k, v in d.r.items():
                    if k != eng:
                        self._need(eng, k, v)
        for d in W:
            if d.w is not None:
                self._need(eng, *d.w)
            for k, v in d.r.items():
                self._need(eng, k, v)

    def op(self, eng, fn, R=(), W=()):
        self._deps(eng, R, W)
        ins = fn()
        self.cnt[eng] += 1
        self.ninst += 1
        ins.then_inc(self.sem[eng], 1)
        c = self.cnt[eng]
        for d in R:
            if d.r.get(eng, 0) < c:
                d.r[eng] = c
        for d in W:
            d.w = (eng, c)
            d.r = {}

    def dma(self, q, out, in_, dep, load=True, extraR=(), extraW=(), **kw):
        if load:
            same = dep.w is not None and dep.dsem is not None and dep.w[0] == dep.dsem and not dep.r
            if q != "sp":
                dep.npair = (getattr(dep, "npair", 0) + 1) if same else 0
                same = same and dep.npair % 2 == 1
            if same:
                self._deps(q, extraR, list(extraW))
            else:
                self._deps(q, extraR, [dep] + list(extraW))
        else:
            self._deps(q, [dep] + list(extraR), extraW)
        if dep.dsem is None:
            name = "D%d_%s" % (len(self.dsems), dep.name)
            dep.dsem = name
            self.dsems[name] = [self.nc.alloc_semaphore(name), 0]
        ent = self.dsems[dep.dsem]
        ent[1] += 16
        self.ninst += 1
        self.E[q].dma_start(out=out, in_=in_, **kw).then_inc(ent[0], 16)
        tok = (dep.dsem, ent[1])
        if load:
            dep.w = tok
            dep.r = {}
            for d in extraW:
                d.w = tok
                d.r = {}
        else:
            dep.r[tok[0]] = tok[1]
        for d in extraR:
            d.r[tok[0]] = tok[1]
        return tok

    def idma(self, out, in_, idx_ap, dep, extraR=()):
        q = "pool"
        if dep.w is not None and dep.dsem is not None and dep.w[0] == dep.dsem and not dep.r:
            self._deps(q, extraR, [])
        else:
            self._deps(q, extraR, [dep])
        if dep.dsem is None:
            name = "D%d_%s" % (len(self.dsems), dep.name)
            dep.dsem = name
            self.dsems[name] = [self.nc.alloc_semaphore(name), 0]
        ent = self.dsems[dep.dsem]
        ent[1] += 16
        self.ninst += 1
        self.nc.gpsimd.indirect_dma_start(
            out=out, out_offset=None, in_=in_,
            in_offset=bass.IndirectOffsetOnAxis(ap=idx_ap, axis=0)).then_inc(ent[0], 16)
        tok = (dep.dsem, ent[1])
        dep.w = tok
        dep.r = {}
        for d in extraR:
            d.r[tok[0]] = tok[1]

    def barrier(self):
        for e in self.E:
            for k in self.sem:
                if self.cnt[k]:
                    self._wait(e, k, self.cnt[k])
            for name, (sem, cnt) in self.dsems.items():
                if cnt:
                    self._wait(e, name, cnt)

    def finish(self):
        for name, (sem, cnt) in self.dsems.items():
            if cnt:
                self._wait("sp", name, cnt)
        for k in self.sem:
            if self.cnt[k]:
                self._wait("sp", k, self.cnt[k])


class Arena:
    def __init__(self, nc, lo=16512, hi=229344):
        self.nc, self.lo, self.hi = nc, lo, hi
        self.top = lo
        self.n = 0
        self.offs = {}

    def alloc(self, shape, dt, name="t"):
        esz = 2 if dt == BF16 else 4
        fre = 1
        for s in shape[1:]:
            fre *= s
        nbytes = (fre * esz + 63) // 64 * 64
        off = self.top
        assert off + nbytes <= self.hi, ("SBUF overflow", name, off + nbytes - self.hi)
        self.top += nbytes
        self.n += 1
        t = self.nc.alloc_sbuf_tensor_at("%s_%d" % (name, self.n), list(shape), dt, offset=off)
        self.offs[name] = off
        return t.ap()

    def alloc_at(self, off, shape, dt, name="t"):
        self.n += 1
        t = self.nc.alloc_sbuf_tensor_at("%s_%d" % (name, self.n), list(shape), dt, offset=off)
        return t.ap()

    def mark(self):
        return self.top

    def release(self, m):
        self.top = m


def build(n_phys=2560, dbg=False):
    nc = bass.Bass("TRN2", target_bir_lowering=False)
    tr = Tr(nc)
    ar = Arena(nc)

    def din(name, shape, dt=F32):
        return nc.dram_tensor(name, list(shape), dt, kind="ExternalInput").ap()

    def dout(name, shape, dt=F32):
        return nc.dram_tensor(name, list(shape), dt, kind="ExternalOutput").ap()

    xo = din("xo", [NOWN, D]); xc = din("xc", [NCTX, D]); xs = din("xs", [NST, D])
    cT_d = din("cT", [128, KC, 17])
    w_ada = din("w_ada", [D, 3 * D]); b_ada = din("b_ada", [1, 3 * D])
    npre_d = din("npre_fm", [128, KC]); npost_d = din("npost", [1, D])
    w_in = din("w_in", [D, INC]); w_out = din("w_out", [D, D])
    w1_d = din("cmp_w1", [2, 32, 128, 128]); pe_d = din("cmp_peT", [128, 2, 32])
    b1_d = din("cmp_b1T", [128, 2]); w2_d = din("cmp_w2", [2, 128, 128]); b2T_d = din("cmp_b2T", [128, 2])
    b2v_d = din("cmp_b2v", [1, 128])
    cdw_d = din("conv_dwT", [128, 8, 31]); cdb_d = din("conv_dbT", [128, 8])
    lng_d = din("conv_lngT", [128, 8]); lnb_d = din("conv_lnbT", [128, 8])
    cache = din("cache", [n_phys * 128, 1024]); ptab_d = din("ptab", [NSEQ, 16], I32)
    swin = din("swin", [NSEQ, 512, 512]); sconv = din("sconv", [NSEQ, 30, 1024])
    identb_d = din("ident_bf", [128, 128], BF16); identf_d = din("ident_f", [128, 128])
    onesb_d = din("ones_bf", [128, 128], BF16); onesf_d = din("ones_f", [128, 128])
    m4_d = din("masks4", [128, 4, 512], BF16)
    negcmp_d = din("negcmp", [127, 8, 512], BF16)
    E_d = din("Esel", [128, 2048], BF16); ovl_d = din("overlap", [128, 32])
    selb_d = din("selbias", [128, 8, 32]); allow_d = din("allowed", [128, 8, 32])
    selg_d = din("selg", [128, 24, 128], BF16)
    rp_d = din("rp", [128, 128]); rs_d = din("rs", [128, 64])
    cval_d = din("cval", [128, 1])
    perm_d = din("perm16", [128, 128], BF16)
    sbias_d = din("sbias", [128, 32]); mnew_d = din("mnew", [128, 16], BF16); mwin0_d = din("mwin0", [128, 16], BF16)
    y_p = dout("y_p", [NOWN, D]); kv_p = dout("kv_p", [NOWN, 1024]); win_p = dout("win_p", [512, 512])
    conv_p = dout("conv_p", [32, 1024])
    y_s = dout("y_s", [NST, D]); kv_s = dout("kv_s", [NST, 1024]); win_s = dout("win_s", [NSEQ, 512, 512])
    conv_s = dout("conv_s", [NSEQ, 30, 1024])
    dbg_o = {}

    PS = [nc.alloc_psum_tensor("ps%d" % i, [128, 512], F32).ap() for i in range(8)]
    PSd = [Dep("ps%d" % i, excl=True) for i in range(8)]

    def psbf(i):
        return PS[i].bitcast(BF16)

    def mm(out, lhsT, rhs, start, stop, R, W):
        tr.op("pe", lambda: nc.tensor.matmul(out, lhsT=lhsT, rhs=rhs, start=start, stop=stop), R, W)

    def tp(out, in_, ident, R, W):
        tr.op("pe", lambda: nc.tensor.transpose(out, in_, ident), R, W)

    def act(out, in_, func, R, W, bias=None, scale=None, accum=None):
        kw = {}
        if bias is not None:
            kw["bias"] = bias
        if scale is not None:
            kw["scale"] = scale
        if accum is not None:
            kw["accum_out"] = accum
        tr.op("act", lambda: nc.scalar.activation(out=out, in_=in_, func=func, **kw), R, W)

    def tt(eng, out, a, b, op, R, W):
        e = nc.vector if eng == "dve" else nc.gpsimd
        tr.op(eng, lambda: e.tensor_tensor(out=out, in0=a, in1=b, op=op), R, W)

    def ts(eng, out, a, s1, s2, op0, op1, R, W):
        e = nc.vector if eng == "dve" else nc.gpsimd
        if s2 is None:
            tr.op(eng, lambda: e.tensor_scalar(out=out, in0=a, scalar1=s1, scalar2=None, op0=op0), R, W)
        else:
            tr.op(eng, lambda: e.tensor_scalar(out=out, in0=a, scalar1=s1, scalar2=s2, op0=op0, op1=op1), R, W)

    def stt(eng, out, a, s, b, op0, op1, R, W):
        e = nc.vector if eng == "dve" else nc.gpsimd
        tr.op(eng, lambda: e.scalar_tensor_tensor(out=out, in0=a, scalar=s, in1=b, op0=op0, op1=op1), R, W)

    def cp(eng, out, in_, R, W):
        if eng == "act":
            tr.op("act", lambda: nc.scalar.copy(out=out, in_=in_), R, W)
        else:
            e = nc.vector if eng == "dve" else nc.gpsimd
            tr.op(eng, lambda: e.tensor_copy(out=out, in_=in_), R, W)

    CONST = Dep("const")

    def cload(shape, dt, src, q="sp", name="c", dep=None):
        t = ar.alloc(shape, dt, name)
        tr.dma(q, t, src, dep or CONST)
        return t

    identb = cload([128, 128], BF16, identb_d); identf = cload([128, 128], F32, identf_d)
    onesb = cload([128, 128], BF16, onesb_d); onesf = cload([128, 128], F32, onesf_d)
    rp = cload([128, 128], F32, rp_d); rsel = cload([128, 64], F32, rs_d)
    cval = cload([128, 1], F32, cval_d)
    npre = cload([128, KC], F32, npre_d)
    cdw = cload([128, 8, 31], F32, cdw_d); cdb = cload([128, 8], F32, cdb_d)
    lng = cload([128, 8], F32, lng_d); lnb = cload([128, 8], F32, lnb_d)
    b1T = cload([128, 2], F32, b1_d); b2T = cload([128, 2], F32, b2T_d)
    b2v = cload([127, 128], F32, b2v_d.partition_broadcast(127))
    CONSTP = Dep("constp")
    w1 = ar.alloc([128, 2, 32, 128], BF16, "w1")
    for s in range(2):
        tr.dma("pool", w1[:, s], w1_d[s].rearrange("l d e -> d l e"), CONSTP)
    peT = ar.alloc([128, 2, 32], BF16, "peT")
    tr.dma("pool", peT, pe_d, CONSTP)
    w2 = ar.alloc([128, 2, 128], BF16, "w2")
    tr.dma("pool", w2, w2_d.rearrange("s e f -> e s f"), CONSTP)
    A1T = ar.alloc([128, KC, 17], F32, "A1T"); shT = ar.alloc([128, KC, 17], F32, "shT")
    G2 = ar.alloc([128, D], F32, "G2")
    dA1 = Dep("A1T"); dG2 = Dep("G2")
    tr.op("pool", lambda: nc.gpsimd.memset(G2, 0.0), [], [dG2])
    hTs = ar.alloc([128, KC, NST], BF16, "hTs"); dhTs = Dep("hTs")
    hThalo = ar.alloc([128, KC, 32], BF16, "hThalo"); dhalo = Dep("hThalo")
    bias1 = ar.alloc([128, 2], F32, "bias1"); dbias1 = Dep("bias1")
    kcT = ar.alloc([128, 2, 127], BF16, "kcT"); vc = ar.alloc([127, 2, 128], BF16, "vc")
    dkc = Dep("kcT"); dvc = Dep("vc")
    sg = ar.alloc([128, NOWN], BF16, "sg"); dsg = Dep("sg")
    tr.op("pool", lambda: nc.gpsimd.memset(sg, 0.0), [], [dsg])
    ulast = ar.alloc([128, 8, 32], F32, "ulast"); dulast = Dep("ulast")
    sfm = {}
    zcs = ar.alloc([128, 8, NST], BF16, "zcs"); sfm["zc"] = (zcs, Dep("zcs"))
    us = ar.alloc([128, 8, NST], F32, "us"); sfm["u"] = (us, Dep("us"))
    kvs = ar.alloc([NST, 1536], F32, "kvs"); sfm["kvs"] = (kvs, Dep("kvs"))
    ksT = ar.alloc([128, 12, NST], BF16, "ksT"); sfm["ksT"] = (ksT, Dep("ksT"))
    qs = ar.alloc([128, 8, NST], BF16, "qs"); sfm["q"] = (qs, Dep("qs"))
    sgs = ar.alloc([128, NST], BF16, "sgs"); sfm["sg"] = (sgs, Dep("sgs"))
    tr.op("pool", lambda: nc.gpsimd.memset(sgs, 0.0), [], [sfm["sg"][1]])
    zns = ar.alloc([128, 8, NST], BF16, "zns"); sfm["zn"] = (zns, Dep("zns"))
    mixTs = ar.alloc([128, 16, NST], BF16, "mixTs"); dmixs = Dep("mixTs")


    def phase_A():
        m = ar.mark()
        cT = ar.alloc([128, KC, 17], F32, "cT"); dcT = Dep("cT")
        scT = ar.alloc([128, KC, 17], F32, "scT"); dscT = Dep("scT")
        ada = ar.alloc([17, 3 * D], F32, "ada"); dada = Dep("ada")
        bb = ar.alloc([17, 3 * D], F32, "bb"); dbb = Dep("bb")
        npb = ar.alloc([17, D], F32, "npb"); dnpb = Dep("npb")
        wsl = [ar.alloc([128, KC, 512], F32, "wa%d" % i) for i in range(2)]
        dws = [Dep("wa%d" % i) for i in range(2)]
        tr.dma("sp", cT, cT_d, dcT)
        tr.dma("sp", bb, b_ada.partition_broadcast(17), dbb)
        tr.dma("sp", npb, npost_d.partition_broadcast(17), dnpb)
        act(scT, cT, AF.Silu, [dcT], [dscT])
        wv = w_ada.rearrange("(k p) c -> p k c", p=128)
        for blk in range(12):
            s = blk % 2
            for hk in range(2):
                tr.dma("sp", wsl[s][:, hk * 8:(hk + 1) * 8, :], wv[:, hk * 8:(hk + 1) * 8, blk * 512:(blk + 1) * 512], dws[s])
            pb = blk % 2
            for k in range(KC):
                mm(PS[pb][0:17, :], scT[:, k, :], wsl[s][:, k, :], k == 0, k == KC - 1, [dscT, dws[s]], [PSd[pb]])
            tt("dve", ada[:, blk * 512:(blk + 1) * 512], PS[pb][0:17, :], bb[:, blk * 512:(blk + 1) * 512], ALU.add,
               [PSd[pb], dbb], [dada])
        for part in range(2):
            pst = PS[2 + part][:, 0:KC * 17].rearrange("p (k r) -> p k r", k=KC)
            for k in range(KC):
                tp(pst[:, k, :], ada[0:17, part * D + k * 128: part * D + (k + 1) * 128], identf[0:17, 0:17],
                   [dada, CONST], [PSd[2 + part]])
            if part == 0:
                cp("dve", shT, pst, [PSd[2]], [dA1])
            else:
                stt("dve", A1T, pst, 1.0, npre.unsqueeze(2).to_broadcast([128, KC, 17]), ALU.add, ALU.mult,
                    [PSd[3], CONST], [dA1])
        tt("dve", G2[0:17, :], ada[:, 2 * D:3 * D], npb, ALU.mult, [dada, dnpb], [dG2])
        ar.release(m)
        tr.barrier()

    def make_hT(xd, ntok, dst_fn, ddst, rowsel, tmp):
        xt, dxt, xn, dxn, junk, djunk, st, dst_ = tmp
        tr.dma("sp", xt[0:ntok, :], xd, dxt)
        act(junk[0:ntok, :], xt[0:ntok, :], AF.Square, [dxt], [djunk, dst_], accum=st[0:ntok, 0:1])
        act(st[0:ntok, 1:2], st[0:ntok, 0:1], AF.Sqrt, [dst_], [dst_], bias=EPS, scale=1.0 / D)
        tr.op("dve", lambda: nc.vector.reciprocal(out=st[0:ntok, 2:3], in_=st[0:ntok, 1:2]), [dst_], [dst_])
        ts("dve", xn[0:ntok, :], xt[0:ntok, :], st[0:ntok, 2:3], None, ALU.mult, None, [dxt, dst_], [dxn])
        for hb in range(2):
            pb = 4 + hb
            pv = psbf(pb)[:, 0:8 * ntok].rearrange("p (k t) -> p k t", k=8)
            for kk in range(8):
                k = hb * 8 + kk
                tp(pv[:, kk, :], xn[0:ntok, k * 128:(k + 1) * 128], identb[0:ntok, 0:ntok], [dxn, CONST], [PSd[pb]])
            for kk in range(8):
                k = hb * 8 + kk
                if rowsel is not None:
                    if kk % 2 == 0:
                        act(dst_fn(k), pv[:, kk, :], AF.Identity, [PSd[pb], dA1], [ddst],
                            bias=shT[:, k, rowsel:rowsel + 1], scale=A1T[:, k, rowsel:rowsel + 1])
                    else:
                        ts("dve", dst_fn(k), pv[:, kk, :], A1T[:, k, rowsel:rowsel + 1], shT[:, k, rowsel:rowsel + 1],
                           ALU.mult, ALU.add, [PSd[pb], dA1], [ddst])
                else:
                    o3 = dst_fn(k).rearrange("p (s t) -> p s t", t=4)
                    i3 = pv[:, kk, :].rearrange("p (s t) -> p s t", t=4)
                    tt("dve", o3, i3, A1T[:, k, 0:NSEQ].unsqueeze(2).to_broadcast([128, NSEQ, 4]), ALU.mult,
                       [PSd[pb], dA1], [ddst])
                    tt("dve", o3, o3, shT[:, k, 0:NSEQ].unsqueeze(2).to_broadcast([128, NSEQ, 4]), ALU.add,
                       [dA1], [ddst])

    def xtmp2():
        xn = ar.alloc([128, D], BF16, "xn"); dxn = Dep("xn")
        junk, djunk = xn, dxn
        res = []
        for i in range(2):
            xt = ar.alloc([128, D], F32, "xt"); st = ar.alloc([128, 4], F32, "st")
            res.append((xt, Dep("xt"), xn, dxn, junk, djunk, st, Dep("st")))
        return res

    WCOLS = 256

    class WStream:
        def __init__(self):
            self.sl = [ar.alloc([128, KC, WCOLS], BF16, "wsl%d" % i) for i in range(2)]
            self.d = [Dep("wsl%d" % i) for i in range(2)]
            self.i = 0
            self.wv = w_in.rearrange("(k p) c -> p k c", p=128)

        def load(self, c0, ncols):
            s = self.i % 2
            self.i += 1
            for hk in range(2):
                tr.dma("pool", self.sl[s][:, hk * 8:(hk + 1) * 8, 0:ncols],
                       self.wv[:, hk * 8:(hk + 1) * 8, c0:c0 + ncols], self.d[s])
            return self.sl[s], self.d[s]

    fm_rr = [0]

    def fm_chunk(wt, dw, c0, ncol, rhs_fn, R, nt):
        pb = fm_rr[0] % 2
        fm_rr[0] += 1
        for k in range(KC):
            mm(PS[pb][0:ncol, 0:nt], wt[:, k, c0:c0 + ncol], rhs_fn(k), k == 0, k == KC - 1, [dw] + R, [PSd[pb]])
        return PS[pb][0:ncol, 0:nt], pb

    tm_rr = [0]

    def tm_block(wt, dw, ncols, lhs_fn, R, ntok):
        pb = 2 + tm_rr[0] % 2
        tm_rr[0] += 1
        for k in range(KC):
            mm(PS[pb][0:ntok, 0:ncols], lhs_fn(k), wt[:, k, 0:ncols], k == 0, k == KC - 1, [dw] + R, [PSd[pb]])
        return PS[pb][0:ntok, 0:ncols], pb

    phase_A()

    mixT = ar.alloc([128, 16, NOWN], BF16, "mixT")
    dmix = [[Dep("mix%d_%d" % (c, t)) for t in range(2)] for c in range(16)]
    qT = ar.alloc([128, 8, NOWN], BF16, "qT"); dq = [Dep("q%d" % h) for h in range(8)]
    kslcT = ar.alloc([128, 2, 2048], BF16, "kslcT"); dkslc = Dep("kslcT")
    kwinT = ar.alloc([128, 2, 1536], BF16, "kwinT"); dkwin = Dep("kwinT")
    vslc = ar.alloc([128, 16, 2, 128], BF16, "vslc"); dvslc = Dep("vslc")
    vwin = ar.alloc([128, 12, 2, 128], BF16, "vwin"); dvwin = Dep("vwin")
    ACC = qT
    dacc = [Dep("acc%d" % j) for j in range(8)]
    kcmpT = mixT[:, 0:4, :].rearrange("p (g a) t -> p g (a t)", g=2).rearrange("p g (ph m) -> p g ph m", ph=16)
    vcmpT = mixT[:, 4:8, :].rearrange("p (g a) t -> p g (a t)", g=2).rearrange("p g (ph m) -> p g ph m", ph=16)
    dkcmp = Dep("kcmpT"); dvcmp = Dep("vcmpT")
    PHASE = ar.mark()

    m0 = ar.mark()
    hT = ar.alloc([128, KC, 1024], BF16, "hT")
    dhT = [Dep("hT%d" % t) for t in range(8)]
    m1 = ar.mark()
    tmpx = xtmp2()
    make_hT(xs, NST, lambda k: hTs[:, k, :], dhTs, None, tmpx[0])
    for t in range(8):
        make_hT(xc[t * 128:(t + 1) * 128, :], 128, (lambda k, t=t: hT[:, k, t * 128:(t + 1) * 128]), dhT[t], 16, tmpx[(t + 1) % 2])
    cp("pool", hThalo, hT[:, :, 992:1024], [dhT[7]], [dhalo])
    ar.release(m1)
    tr.barrier()

    ws = WStream()
    stg = [ar.alloc([128, WCOLS], F32, "stg%d" % i) for i in range(2)]
    dstg = [Dep("stg%d" % i) for i in range(2)]
    stg_rr = [0]

    def hgrp(tg):
        return lambda k: hT[:, k, tg * 512:(tg + 1) * 512]

    def ctx_pass():
        for b in range(6):
            wt, dw = ws.load(C_KV + b * 256, 256)
            if b in (0, 1, 2, 4):
                dstT, dd = {0: (kcmpT, dkcmp), 1: (vcmpT, dvcmp), 2: (kslcT, dkslc), 4: (kwinT, dkwin)}[b]
                for g in range(2):
                    for tg in range(2):
                        if b == 4 and tg == 0:
                            continue
                        ps, pb = fm_chunk(wt, dw, g * 128, 128, hgrp(tg), [dhT[4 * tg + i] for i in range(4)], 512)
                        if b == 4:
                            o = dstT[:, g, 0:512]
                        elif b == 2:
                            o = dstT[:, g, tg * 512:(tg + 1) * 512]
                        else:
                            o = dstT[:, g, :, tg * 32:(tg + 1) * 32].rearrange("p ph m -> p m ph")
                            ps = ps.rearrange("p (m ph) -> p m ph", ph=16)
                        cp("act" if (g + tg) % 2 else "dve", o, ps, [PSd[pb]], [dd])
            else:
                dstV, dd = (vslc, dvslc) if b == 3 else (vwin, dvwin)
                for t in range(8):
                    if b == 5 and t < 4:
                        continue
                    ps, pb = tm_block(wt, dw, 256, (lambda k, t=t: hT[:, k, t * 128:(t + 1) * 128]), [dhT[t]], 128)
                    o = dstV[:, t, :, :] if b == 3 else dstV[:, t - 4, :, :]
                    cp("act" if t % 2 else "dve", o, ps.rearrange("p (g d) -> p g d", g=2), [PSd[pb]], [dd])

    ctx_pass()

    m2 = ar.mark()
    tmpx = xtmp2()
    for t in range(8):
        make_hT(xo[t * 128:(t + 1) * 128, :], 128, (lambda k, t=t: hT[:, k, t * 128:(t + 1) * 128]), dhT[t], 16, tmpx[t % 2])
    ar.release(m2)
    tr.barrier()


    def own_pass():
        allh = lambda tg: [dhT[4 * tg + i] for i in range(4)]
        for b in range(4):
            wt, dw = ws.load(C_ZC + b * 256, 256)
            for cc in range(2):
                j = 2 * b + cc
                for tg in range(2):
                    ps, pb = fm_chunk(wt, dw, cc * 128, 128, hgrp(tg), allh(tg), 512)
                    act(mixT[:, 8 + j, tg * 512:(tg + 1) * 512], ps, AF.Silu, [PSd[pb]], [dmix[8 + j][tg]])
                ps, pb = fm_chunk(wt, dw, cc * 128, 128, lambda k: hTs[:, k, :], [dhTs], NST)
                act(zcs[:, j, :], ps, AF.Silu, [PSd[pb]], [sfm["zc"][1]])
        ubuf = [ar.alloc([128, 32 + NOWN], F32, "ubuf%d" % i) for i in range(2)]
        dub = [Dep("ubuf%d" % i) for i in range(2)]
        sgt = ar.alloc([128, 512], F32, "sgt"); dsgt = Dep("sgt")
        ubb = ar.alloc([128, 32 + NOWN], BF16, "ubb"); dubb = Dep("ubb")
        diagJ = ar.alloc([128, 31, 128], BF16, "diagJ"); ddiag = Dep("diagJ")
        cv_rr = [0]
        for b in range(4):
            wa, dwa = ws.load(C_A + b * 256, 256)
            wg, dwg = ws.load(C_GL + b * 256, 256)
            for cc in range(2):
                j = 2 * b + cc
                ub, du = ubuf[j % 2], dub[j % 2]
                segs = [(lambda k: hThalo[:, k, :], [dhalo], 32, ub[:, 0:32]),
                        (hgrp(0), allh(0), 512, ub[:, 32:544]),
                        (hgrp(1), allh(1), 512, ub[:, 544:1056]),
                        (lambda k: hTs[:, k, :], [dhTs], NST, us[:, j, :])]
                for si, (rf, R, nt, dsta) in enumerate(segs):
                    dd = du if si < 3 else sfm["u"][1]
                    ps, pb = fm_chunk(wa, dwa, cc * 128, 128, rf, R, nt)
                    cp("dve", dsta, ps, [PSd[pb]], [dd])
                    ps, pb = fm_chunk(wg, dwg, cc * 128, 128, rf, R, nt)
                    act(sgt[:, 0:nt], ps, AF.Sigmoid, [PSd[pb]], [dsgt])
                    tt("dve", dsta, dsta, sgt[:, 0:nt], ALU.mult, [dsgt], [dd])
                ts("dve", ub[:, 0:32], ub[:, 0:32], cval[:, 0:1], None, ALU.mult, None, [CONST], [du])
                cp("pool", ulast[:, j, :], ub[:, 1024:1056], [du], [dulast])
                cp("dve", ubb, ub[:, 0:1056], [du], [dubb])
                tt("dve", diagJ, identb.unsqueeze(1).to_broadcast([128, 31, 128]),
                   cdw[:, j, :].unsqueeze(2).to_broadcast([128, 31, 128]), ALU.mult, [CONST], [ddiag])
                for tg in range(2):
                    pb = 2 + cv_rr[0] % 2
                    cv_rr[0] += 1
                    for w in range(31):
                        mm(PS[pb], diagJ[:, w, :], ubb[:, 2 + w + tg * 512: 2 + w + tg * 512 + 512], w == 0, w == 30,
                           [ddiag, dubb], [PSd[pb]])
                    act(ACC[:, j, tg * 512:(tg + 1) * 512], PS[pb], AF.Identity, [PSd[pb], CONST], [dacc[j]], bias=cdb[:, j:j + 1])
        sq = ar.alloc([128, 512], BF16, "sq"); dsq = Dep("sq")
        mean = ar.alloc([128, 512], F32, "mean"); dmean = Dep("mean")
        rstd = ar.alloc([128, 512], F32, "rstd"); drstd = Dep("rstd")
        tmpn = ar.alloc([128, 512], F32, "tmpn"); dtmpn = Dep("tmpn")
        for tg in range(2):
            sl = slice(tg * 512, (tg + 1) * 512)
            for j in range(8):
                mm(PS[4], onesb, ACC[:, j, sl], j == 0, j == 7, [CONST, dacc[j]], [PSd[4]])
            for j in range(8):
                act(sq, ACC[:, j, sl], AF.Square, [dacc[j]], [dsq])
                mm(PS[5], onesb, sq, j == 0, j == 7, [CONST, dsq], [PSd[5]])
            ts("dve", mean, PS[4], 1.0 / 1024, None, ALU.mult, None, [PSd[4]], [dmean])
            tt("dve", tmpn, mean, mean, ALU.mult, [dmean], [dtmpn])
            stt("dve", tmpn, PS[5], 1.0 / 1024, tmpn, ALU.mult, ALU.subtract, [PSd[5]], [dtmpn])
            act(tmpn, tmpn, AF.Sqrt, [dtmpn], [dtmpn], bias=EPS, scale=1.0)
            tr.op("dve", lambda: nc.vector.reciprocal(out=rstd, in_=tmpn), [dtmpn], [drstd])
            for j in range(8):
                tt("dve", tmpn, ACC[:, j, sl], mean, ALU.subtract, [dacc[j], dmean], [dtmpn])
                tt("dve", tmpn, tmpn, rstd, ALU.mult, [drstd], [dtmpn])
                act(tmpn, tmpn, AF.Silu, [dtmpn], [dtmpn], bias=lnb[:, j:j + 1], scale=lng[:, j:j + 1])
                tt("dve", mixT[:, 8 + j, sl], tmpn, mixT[:, 8 + j, sl], ALU.mult, [dtmpn], [dmix[8 + j][tg]])

        for b in range(6):
            wt, dw = ws.load(C_KV + b * 256, 256)
            for t in range(8):
                ps, pb = tm_block(wt, dw, 256, (lambda k, t=t: hT[:, k, t * 128:(t + 1) * 128]), [dhT[t]], 128)
                si = stg_rr[0] % 2
                stg_rr[0] += 1
                cp("dve" if t % 2 else "act", stg[si], ps, [PSd[pb]], [dstg[si]])
                if b < 4:
                    tr.dma("sp", kv_p[t * 128:(t + 1) * 128, b * 256:(b + 1) * 256], stg[si], dstg[si], load=False)
                elif t >= 4:
                    tr.dma("sp", win_p[(t - 4) * 128:(t - 3) * 128, (b - 4) * 256:(b - 3) * 256], stg[si], dstg[si], load=False)
                if b == 3:
                    cp("pool", vslc[:, 8 + t, :, :], stg[si].rearrange("p (g d) -> p g d", g=2), [dstg[si]], [dvslc])
                if b == 5:
                    cp("pool", vwin[:, 4 + t, :, :], stg[si].rearrange("p (g d) -> p g d", g=2), [dstg[si]], [dvwin])
            ps, pb = tm_block(wt, dw, 256, lambda k: hTs[:, k, :], [dhTs], NST)
            cp("dve", kvs[:, b * 256:(b + 1) * 256], ps, [PSd[pb]], [sfm["kvs"][1]])
            if b in (0, 1, 2, 4):
                dstT, dd = {0: (kcmpT, dkcmp), 1: (vcmpT, dvcmp), 2: (kslcT, dkslc), 4: (kwinT, dkwin)}[b]
                for g in range(2):
                    for tg in range(2):
                        ps, pb = fm_chunk(wt, dw, g * 128, 128, hgrp(tg), allh(tg), 512)
                        if b == 4:
                            o = dstT[:, g, 512 + tg * 512: 512 + (tg + 1) * 512]
                        elif b == 2:
                            o = dstT[:, g, 1024 + tg * 512: 1024 + (tg + 1) * 512]
                        else:
                            o = dstT[:, g, :, 64 + tg * 32: 64 + (tg + 1) * 32].rearrange("p ph m -> p m ph")
                            ps = ps.rearrange("p (m ph) -> p m ph", ph=16)
                        cp("act" if (g + tg) % 2 else "dve", o, ps, [PSd[pb]], [dd])
            if b >= 2:
                for g in range(2):
                    ps, pb = fm_chunk(wt, dw, g * 128, 128, lambda k: hTs[:, k, :], [dhTs], NST)
                    cp("dve", ksT[:, 2 * b + g, :], ps, [PSd[pb]], [sfm["ksT"][1]])
            if b == 1:
                compress_prompt()
        tr.dma("sp", kv_s, kvs[:, 0:1024], sfm["kvs"][1], load=False)

        for b in range(4):
            wt, dw = ws.load(C_Q + b * 256, 256)
            for cc in range(2):
                h = 2 * b + cc
                for tg in range(2):
                    ps, pb = fm_chunk(wt, dw, cc * 128, 128, hgrp(tg), allh(tg), 512)
                    cp("act" if tg else "dve", qT[:, h, tg * 512:(tg + 1) * 512], ps, [PSd[pb]], [dq[h]] + dacc)
                ps, pb = fm_chunk(wt, dw, cc * 128, 128, lambda k: hTs[:, k, :], [dhTs], NST)
                cp("dve", qs[:, h, :], ps, [PSd[pb]], [sfm["q"][1]])
        wt, dw = ws.load(C_G, 24)
        for tg in range(2):
            ps, pb = fm_chunk(wt, dw, 0, 24, hgrp(tg), allh(tg), 512)
            act(sg[0:24, tg * 512:(tg + 1) * 512], ps, AF.Sigmoid, [PSd[pb]], [dsg])
        ps, pb = fm_chunk(wt, dw, 0, 24, lambda k: hTs[:, k, :], [dhTs], NST)
        act(sgs[0:24, :], ps, AF.Sigmoid, [PSd[pb]], [sfm["sg"][1]])
        for b in range(4):
            wt, dw = ws.load(C_ZN + b * 256, 256)
            for cc in range(2):
                j = 2 * b + cc
                for tg in range(2):
                    ps, pb = fm_chunk(wt, dw, cc * 128, 128, hgrp(tg), allh(tg), 512)
                    act(mixT[:, j, tg * 512:(tg + 1) * 512], ps, AF.Silu, [PSd[pb]], [dmix[j][tg], dkcmp, dvcmp])
                ps, pb = fm_chunk(wt, dw, cc * 128, 128, lambda k: hTs[:, k, :], [dhTs], NST)
                act(zns[:, j, :], ps, AF.Silu, [PSd[pb]], [sfm["zn"][1]])

    def bias1_setup():
        for s in range(2):
            for l in range(32):
                mm(PS[6][:, s:s + 1], w1[:, s, l, :], peT[:, s, l:l + 1], l == 0, l == 31, [CONSTP], [PSd[6]])
        tt("dve", bias1, PS[6][:, 0:2], b1T, ALU.add, [PSd[6], CONST], [dbias1])

    def compress(srcK, srcV, R, kc_out, vc_out, dkc_o, dvc_o, hbuf, dhbuf):
        for s, src in ((0, srcK), (1, srcV)):
            pv = PS[6][:, 0:254].rearrange("p (g m) -> p g m", g=2)
            for l in range(32):
                mm(pv, w1[:, s, l, :], src[:, :, l % 16, l // 16: l // 16 + 127], l == 0, l == 31, [CONSTP] + R, [PSd[6]])
            act(hbuf, pv, AF.Silu, [PSd[6], dbias1], [dhbuf], bias=bias1[:, s:s + 1])
            if s == 0:
                pk = PS[7][:, 0:254].rearrange("p (g m) -> p g m", g=2)
                mm(pk, w2[:, 0, :], hbuf, True, True, [CONSTP, dhbuf], [PSd[7]])
                act(kc_out, pk, AF.Identity, [PSd[7], CONST], [dkc_o], bias=b2T[:, 0:1])
            else:
                for g in range(2):
                    mm(PS[7][0:127, g * 128:(g + 1) * 128], hbuf[:, g, :], w2[:, 1, :], True, True, [CONSTP, dhbuf], [PSd[7]])
                tt("dve", vc_out, PS[7][0:127, 0:256].rearrange("p (g f) -> p g f", g=2),
                   b2v.unsqueeze(1).to_broadcast([127, 2, 128]), ALU.add, [PSd[7], CONST], [dvc_o])

    hbuf = ar.alloc([128, 2, 127], BF16, "hbuf"); dhbuf = Dep("hbuf")

    def compress_prompt():
        bias1_setup()
        compress(kcmpT, vcmpT, [dkcmp, dvcmp], kcT, vc, dkc, dvc, hbuf, dhbuf)

    own_pass()
    tr.barrier()
    ar.release(m0)

    CONST2 = Dep("const2")
    Esel = cload([128, 2048], BF16, E_d, dep=CONST2); ovl = cload([128, 32], F32, ovl_d, dep=CONST2)
    selg = cload([128, 24, 128], BF16, selg_d, dep=CONST2)
    mKeep = ar.mark()
    masks4 = cload([128, 4, 512], BF16, m4_d, dep=CONST2)
    negcmp = cload([127, 8, 512], BF16, negcmp_d, dep=CONST2)
    selb = cload([128, 8, 32], F32, selb_d, dep=CONST2); allow = cload([128, 8, 32], F32, allow_d, dep=CONST2)
    mcv = ar.mark()
    cvo = ar.alloc([32, 1024], F32, "cvo"); dcvo = Dep("cvo")
    for j in range(8):
        tp(PS[6 + j // 4][0:32, (j % 4) * 128:(j % 4 + 1) * 128], ulast[:, j, :], identf, [dulast, CONST], [PSd[6 + j // 4]])
    for hb in range(2):
        cp("dve", cvo[:, hb * 512:(hb + 1) * 512], PS[6 + hb][0:32, :], [PSd[6 + hb]], [dcvo])
    tr.dma("sp", conv_p, cvo, dcvo, load=False)


    def attn_prompt():
        Pb = [ar.alloc([128, 512], BF16, "Pb%d" % i) for i in range(2)]
        dPb = [Dep("Pb%d" % i) for i in range(2)]
        Pf = ar.alloc([128, 512], F32, "Pf"); dPf = Dep("Pf")
        pn = ar.alloc([128, 512], F32, "pn"); dpn = Dep("pn")
        psT = ar.alloc([128, 128], F32, "psT"); dpsT = Dep("psT")
        rsB = ar.alloc([128, 512], F32, "rsB"); drs = Dep("rsB")
        coef = ar.alloc([128, 512], F32, "coef"); dcoef = Dep("coef")
        tmpo = ar.alloc([128, 512], F32, "tmpo"); dtmpo = Dep("tmpo")
        oacc = ar.alloc([128, 512], F32, "oacc"); doacc = Dep("oacc")
        score = ar.alloc([128, 32], F32, "score"); dscore = Dep("score")
        work = ar.alloc([128, 32], F32, "work"); dwork = Dep("work")
        m8 = ar.alloc([128, 16], F32, "m8"); dm8 = Dep("m8")
        selt = ar.alloc([128, 32], F32, "selt"); dselt = Dep("selt")
        nsT4 = ar.alloc([128, 4, 128], BF16, "nsT4"); dnsT4 = Dep("nsT4")
        srr = [0]
        prr = [0]
        tr.op("pool", lambda: nc.gpsimd.memset(Pf, 0.0), [], [dPf])
        tr.op("pool", lambda: nc.gpsimd.memset(psT, 0.0), [], [dpsT])
        tr.op("pool", lambda: nc.gpsimd.memset(nsT4, 0.0), [], [dnsT4])

        accsel = [0]

        def acc_banks():
            return (2, 3) if accsel[0] % 2 == 0 else (6, 7)

        def finish_branch(br, first):
            bo, bs_ = acc_banks()
            accsel[0] += 1
            if first:
                ts("dve", rsB, PS[bs_], 1e-30, None, ALU.max, None, [PSd[bs_]], [drs])
                tr.op("dve", lambda: nc.vector.reciprocal(out=rsB, in_=rsB), [drs], [drs])
            else:
                act(rsB, PS[bs_], AF.Ln, [PSd[bs_]], [drs], bias=1e-18, scale=1.0)
                act(rsB, rsB, AF.Exp, [drs], [drs], scale=-1.0)
            tt("dve", coef, PS[4], rsB, ALU.mult, [PSd[4], drs], [dcoef])
            if first:
                tt("dve", oacc, PS[bo], coef, ALU.mult, [PSd[bo], dcoef], [doacc])
            else:
                tt("dve", tmpo, PS[bo], coef, ALU.mult, [PSd[bo], dcoef], [dtmpo])
                tt("pool", oacc, oacc, tmpo, ALU.add, [dtmpo], [doacc])

        def gates(br, g, qi):
            for r in range(4):
                mm(PS[4][:, r * 128:(r + 1) * 128], selg[:, br * 8 + 4 * g + r, :], sg[:, qi * 128:(qi + 1) * 128],
                   True, True, [CONST2, dsg], [PSd[4]])

        pend = [None]

        def pv_part(t):
            v_ap, pbi, Rk, first, last, npart = t
            bo, bs_ = acc_banks()
            mm(PS[bo], v_ap, Pb[pbi][0:npart, :], first, last, [dPb[pbi]] + Rk, [PSd[bo]])
            mm(PS[bs_], onesb[0:npart, :], Pb[pbi][0:npart, :], first, last, [dPb[pbi], CONST], [PSd[bs_]])

        def flush_pv():
            if pend[0] is not None:
                pv_part(pend[0])
                pend[0] = None

        def tile_attend(kT_ap, v_ap, masks, qrhs, Rk, first, last, npart=128):
            sb = srr[0] % 2; srr[0] += 1
            pbi = prr[0] % 2; prr[0] += 1
            S = PS[sb][0:npart, :]
            mm(S, kT_ap, qrhs, True, len(masks) == 0, Rk, [PSd[sb]])
            for mi, (ml, mr, Rm) in enumerate(masks):
                mm(S, ml, mr, False, mi == len(masks) - 1, Rm, [PSd[sb]])
            act(Pb[pbi][0:npart, :], S, AF.Exp, [PSd[sb]], [dPb[pbi]], scale=SCL)
            flush_pv()
            pend[0] = (v_ap, pbi, Rk, first, last, npart)

        oaccs = [oacc, ar.alloc([128, 512], F32, "oacc1")]; doaccs = [doacc, Dep("oacc1")]
        nsT4s = [nsT4, ar.alloc([128, 4, 128], BF16, "nsT4b")]; dnsT4s = [dnsT4, Dep("nsT4b")]
        tr.op("pool", lambda: nc.gpsimd.memset(nsT4s[1], 0.0), [], [dnsT4s[1]])
        its = [(qi, g) for qi in range(8) for g in range(2)]
        gS = [[ar.alloc([128, 512], F32, "gS%d_%d" % (ib, br)) for br in range(3)] for ib in range(2)]
        dgS = [[Dep("gS%d_%d" % (ib, br)) for br in range(3)] for ib in range(2)]
        rsA = ar.alloc([128, 512], F32, "rsA"); drsA = Dep("rsA")

        def all_gates(i):
            qi, g = its[i]
            for br in range(3):
                gates(br, g, qi)
                cp("act", gS[i % 2][br], PS[4], [PSd[4]], [dgS[i % 2][br]])

        def finish_branch2(br, first, ib):
            bo, bs_ = acc_banks()
            accsel[0] += 1
            oa, doa = oaccs[ib], doaccs[ib]
            act(rsA, PS[bs_], AF.Ln, [PSd[bs_]], [drsA], bias=1e-18, scale=1.0)
            act(rsA, rsA, AF.Exp, [drsA], [drsA], scale=-1.0)
            tt("dve", coef, gS[ib][br], rsA, ALU.mult, [dgS[ib][br], drsA], [dcoef])
            if first:
                tt("dve", oa, PS[bo], coef, ALU.mult, [PSd[bo], dcoef], [doa])
                ts("dve", rsB, PS[bs_], 1e-30, None, ALU.max, None, [PSd[bs_]], [drs])
                tr.op("dve", lambda: nc.vector.reciprocal(out=rsB, in_=rsB), [drs], [drs])
            else:
                tt("dve", tmpo, PS[bo], coef, ALU.mult, [PSd[bo], dcoef], [dtmpo])
                tt("dve", oa, oa, tmpo, ALU.add, [dtmpo], [doa])

        def qinfo(i):
            qi, g = its[i]
            return qi, g, qT[:, 4 * g:4 * g + 4, qi * 128:(qi + 1) * 128], [dq[4 * g + r] for r in range(4)]

        def C1(i):
            qi, g, qrhs, Rq = qinfo(i)
            sb = srr[0] % 2; srr[0] += 1
            S = PS[sb][0:127, :]
            mm(S, kcT[:, g, :], qrhs, True, False, Rq + [dkc], [PSd[sb]])
            mm(S, identb[0:127, 0:127], negcmp[:, qi, :], False, True, [CONST, CONST2], [PSd[sb]])
            act(Pf[0:127, :], S, AF.Exp, [PSd[sb]], [dPf], scale=SCL)
            pbi = prr[0] % 2; prr[0] += 1
            act(Pb[pbi][0:127, :], S, AF.Exp, [PSd[sb]], [dPb[pbi]], scale=SCL)
            bo, bs_ = acc_banks()
            mm(PS[bs_], onesf, Pf, True, True, [dPf, CONST], [PSd[bs_]])
            mm(PS[bo], vc[:, g, :], Pb[pbi][0:127, :], True, True, [dPb[pbi], dvc], [PSd[bo]])
            all_gates(i)
            finish_branch2(0, True, i % 2)
            tt("dve", pn[0:127, :], Pf[0:127, :], rsB[0:127, :], ALU.mult, [dPf, drs], [dpn])
            tr.op("dve", lambda: nc.vector.tensor_reduce(
                out=psT[0:127, :], in_=pn[0:127, :].rearrange("p (r i) -> p i r", r=4), axis=AX.X, op=ALU.add),
                [dpn], [dpsT])

        def C2(i):
            qi, g, qrhs, Rq = qinfo(i)
            mm(PS[5][:, 0:32], psT, ovl, True, True, [dpsT, CONST2], [PSd[5]])
            tt("dve", score, PS[5][:, 0:32], selb[:, qi, :], ALU.add, [PSd[5], CONST2], [dscore])
            tr.op("dve", lambda: nc.vector.max(out=m8[:, 0:8], in_=score), [dscore], [dm8])
            tr.op("dve", lambda: nc.vector.match_replace(out=work, in_to_replace=m8[:, 0:8], in_values=score,
                                                         imm_value=-3.0e38), [dm8, dscore], [dwork])
            tr.op("dve", lambda: nc.vector.max(out=m8[:, 8:16], in_=work), [dwork], [dm8])
            ts("dve", selt, score, m8[:, 15:16], None, ALU.is_ge, None, [dscore, dm8], [dselt])
            tt("dve", selt, selt, allow[:, qi, :], ALU.mult, [CONST2], [dselt])
            ts("dve", selt, selt, -1.0, BIG, ALU.add, ALU.mult, [], [dselt])

        def C3(i):
            tp(PS[5][0:32, 128:256], selt, identf, [dselt, CONST], [PSd[5]])
            cp("act", nsT4s[i % 2][0:32], PS[5][0:32, 128:256].unsqueeze(1).to_broadcast([32, 4, 128]), [PSd[5]], [dnsT4s[i % 2]])

        def B(i, nxt):
            qi, g, qrhs, Rq = qinfo(i)
            qt = 8 + qi
            ns, dns = nsT4s[i % 2], dnsT4s[i % 2]
            for kt in range(qt + 1):
                masks = [(Esel[:, kt * 128:(kt + 1) * 128], ns, [CONST2, dns])]
                if kt == qt:
                    masks.append((identb, masks4[:, 0, :], [CONST, CONST2]))
                tile_attend(kslcT[:, g, kt * 128:(kt + 1) * 128], vslc[:, kt, g, :], masks, qrhs,
                            Rq + [dkslc, dvslc], kt == 0, kt == qt)
                if kt == 3 and nxt is not None:
                    C2(nxt)
            flush_pv()
            if nxt is not None:
                C3(nxt)
            finish_branch2(1, False, i % 2)
            for wi in range(5):
                kt = qt - 4 + wi
                w = kt - 4
                masks = []
                if wi == 0:
                    masks.append((identb, masks4[:, 3 if kt < 8 else 1, :], [CONST, CONST2]))
                elif wi == 4:
                    masks.append((identb, masks4[:, 0, :], [CONST, CONST2]))
                elif kt < 8:
                    masks.append((identb, masks4[:, 2, :], [CONST, CONST2]))
                tile_attend(kwinT[:, g, w * 128:(w + 1) * 128], vwin[:, w, g, :], masks, qrhs,
                            Rq + [dkwin, dvwin], wi == 0, wi == 4)
            flush_pv()
            finish_branch2(2, False, i % 2)
            dm = [dmix[4 * g + r][qi // 4] for r in range(4)]
            mo = mixT[:, 4 * g:4 * g + 4, qi * 128:(qi + 1) * 128]
            tt("dve", mo, oaccs[i % 2].rearrange("p (r i) -> p r i", r=4), mo, ALU.mult, [doaccs[i % 2]], dm)

        C1(0); C2(0); C3(0)
        for i in range(len(its)):
            nxt = i + 1 if i + 1 < len(its) else None
            if nxt is not None:
                C1(nxt)
            B(i, nxt)

    mF = ar.mark()
    attn_prompt()
    tr.barrier()
    ar.release(mF)


    def sample_phase():
        nonlocal ar
        SOFF = ar.offs["qT"]
        assert SOFF + 44 * 1024 <= m0
        ar.top = mKeep
        CS = Dep("const_s")
        sbias = cload([128, 32], F32, sbias_d, dep=CS); mnew = cload([128, 16], BF16, mnew_d, dep=CS)
        mwin0 = cload([128, 16], BF16, mwin0_d, dep=CS)
        perm = cload([128, 128], BF16, perm_d, dep=CS)
        pti = ar.alloc([128, NSEQ * 16], I32, "pti"); dpti = Dep("pti")
        iop = ar.alloc([128, 1], I32, "iop"); diop = Dep("iop")
        idx = ar.alloc([128, NSEQ * 16], I32, "idx"); didx = Dep("idx")
        tr.dma("sp", pti, ptab_d.rearrange("(o s) p -> o (s p)", o=1).partition_broadcast(128), dpti)
        tr.op("pool", lambda: nc.gpsimd.iota(iop, pattern=[[0, 1]], base=0, channel_multiplier=1), [], [diop])
        ts("dve", idx, pti, 128, iop[:, 0:1], ALU.mult, ALU.add, [dpti, diop], [didx])
        pgs = [ar.alloc_at(SOFF, [128, 16, 1024], BF16, "pg0"), ar.alloc([128, 16, 1024], BF16, "pg1")]
        dpgs = [Dep("pg0"), Dep("pg1")]
        kcmpS = ar.alloc_at(SOFF + 32768, [128, 2, 16, 128], BF16, "kcmpS"); dkcmpS = Dep("kcmpS")
        vcmpS = ar.alloc([128, 2, 16, 128], BF16, "vcmpS"); dvcmpS = Dep("vcmpS")
        kslcS = ar.alloc([128, 2, 2048], BF16, "kslcS"); dkslcS = Dep("kslcS")
        wps = [ar.alloc_at(SOFF + 40960, [128, 4, 512], BF16, "wp0"), ar.alloc([128, 4, 512], BF16, "wp1")]
        dwps = [Dep("wp0"), Dep("wp1")]
        kwS = ar.alloc([128, 2, 512], BF16, "kwS"); dkwS = Dep("kwS")
        vnew = ar.alloc([128, 4, 128], BF16, "vnew"); dvnew = Dep("vnew")
        kcS = ar.alloc([128, 2, 127], BF16, "kcS"); dkcS = Dep("kcS")
        vcS = ar.alloc([127, 2, 128], BF16, "vcS"); dvcS = Dep("vcS")
        hb2 = ar.alloc([128, 2, 127], BF16, "hb2"); dhb2 = Dep("hb2")
        PfS = ar.alloc([128, 32], F32, "PfS"); dPfS = Dep("PfS")
        PbS = ar.alloc([128, 32], BF16, "PbS"); dPbS = Dep("PbS")
        pnS = ar.alloc([128, 32], F32, "pnS"); dpnS = Dep("pnS")
        psTS = ar.alloc([128, 8], F32, "psTS"); dpsTS = Dep("psTS")
        rsS = ar.alloc([128, 32], F32, "rsS"); drsS = Dep("rsS")
        coefS = ar.alloc([128, 32], F32, "coefS"); dcoefS = Dep("coefS")
        tmpS = ar.alloc([128, 32], F32, "tmpS"); dtmpS = Dep("tmpS")
        accS = ar.alloc([128, 32], F32, "accS"); daccS = Dep("accS")
        scS = ar.alloc([128, 32], F32, "scS"); dscS = Dep("scS")
        wkS = ar.alloc([128, 32], F32, "wkS"); dwkS = Dep("wkS")
        m8S = ar.alloc([128, 16], F32, "m8S"); dm8S = Dep("m8S")
        selS = ar.alloc([128, 32], F32, "selS"); dselS = Dep("selS")
        nsS = ar.alloc([128, 2, 4, 4], BF16, "nsS"); dnsS = Dep("nsS")
        Pk = ar.alloc([128, 17, 16], BF16, "Pk"); dPk = Dep("Pk")
        gB = ar.alloc([128, 24, NST], F32, "gB"); dgB = Dep("gB")
        for t_, d_ in ((PfS, dPfS), (psTS, dpsTS), (nsS, dnsS), (Pk, dPk), (vnew, dvnew), (scS, dscS), (selS, dselS)):
            tr.op("pool", lambda t_=t_: nc.gpsimd.memset(t_, 0.0), [], [d_])
        dsgs = sfm["sg"][1]; dqs = sfm["q"][1]; dks = sfm["ksT"][1]
        for k in range(24):
            pb = k // 8
            mm(PS[pb][:, (k % 8) * 64:(k % 8 + 1) * 64], selg[:, k, :], sgs, True, True, [CONST2, dsgs], [PSd[pb]])
            if k % 8 == 7:
                cp("dve", gB[:, pb * 8:(pb + 1) * 8, :], PS[pb].rearrange("p (k t) -> p k t", k=8), [PSd[pb]], [dgB])
        trr = [0]
        srr = [0]

        def trbank():
            b = trr[0] % 2; trr[0] += 1
            return b

        def sbank():
            b = 2 + srr[0] % 2; srr[0] += 1
            return b

        def branch_finish(g, br, first, s):
            sl = slice(g * 16, (g + 1) * 16)
            ts("dve", rsS[:, sl], PS[5][:, 0:16], 1e-30, None, ALU.max, None, [PSd[5]], [drsS])
            tr.op("dve", lambda: nc.vector.reciprocal(out=rsS[:, sl], in_=rsS[:, sl]), [drsS], [drsS])
            gate = gB[:, br * 8 + 4 * g: br * 8 + 4 * g + 4, 4 * s:4 * s + 4]
            tt("dve", coefS[:, sl].rearrange("p (r t) -> p r t", r=4), rsS[:, sl].rearrange("p (r t) -> p r t", r=4), gate,
               ALU.mult, [drsS, dgB], [dcoefS])
            tt("dve", tmpS[:, sl], PS[4][:, 0:16], coefS[:, sl], ALU.mult, [PSd[4], dcoefS], [dtmpS])
            tt("dve", accS[:, sl], accS[:, sl], tmpS[:, sl], ALU.add, [dtmpS], [daccS])

        def prefetch(s):
            for p_ in range(16):
                tr.idma(pgs[s % 2][:, p_, :], cache, idx[:, s * 16 + p_: s * 16 + p_ + 1], dpgs[s % 2], extraR=[didx])
            tr.dma("pool", wps[s % 2], swin[s].rearrange("(a p) c -> p a c", p=128), dwps[s % 2])

        prefetch(0)
        for s in range(NSEQ):
            if s + 1 < NSEQ:
                prefetch(s + 1)
            pg, dpg, wp, dwp = pgs[s % 2], dpgs[s % 2], wps[s % 2], dwps[s % 2]
            for slot, (dstT, dd) in enumerate(((kcmpS, dkcmpS), (vcmpS, dvcmpS))):
                for g in range(2):
                    for pgp in range(4):
                        b = trbank()
                        for a in range(4):
                            mm(PS[b][:, a * 128:(a + 1) * 128], pg[:, pgp * 4 + a, slot * 256 + g * 128: slot * 256 + (g + 1) * 128],
                               perm, True, True, [dpg, CS], [PSd[b]])
                        cp("act" if (g + pgp) % 2 else "dve",
                           dstT[:, g, :, pgp * 32:(pgp + 1) * 32].rearrange("p ph (a m) -> p ph a m", a=4),
                           PS[b].rearrange("p (a ph m) -> p ph a m", a=4, ph=16), [PSd[b]], [dd])
            compress(kcmpS, vcmpS, [dkcmpS, dvcmpS], kcS, vcS, dkcS, dvcS, hb2, dhb2)
            sb = sbank()
            for g in range(2):
                mm(PS[sb][0:127, g * 16:(g + 1) * 16], kcS[:, g, :], qs[:, 4 * g:4 * g + 4, 4 * s:4 * s + 4], True, True,
                   [dkcS, dqs], [PSd[sb]])
            act(PfS[0:127, :], PS[sb][0:127, 0:32], AF.Exp, [PSd[sb]], [dPfS], scale=SCL)
            act(PbS[0:127, :], PS[sb][0:127, 0:32], AF.Exp, [PSd[sb]], [dPbS], scale=SCL)
            mm(PS[5][:, 0:32], onesf, PfS, True, True, [dPfS, CONST], [PSd[5]])
            for g in range(2):
                mm(PS[4][:, g * 16:(g + 1) * 16], vcS[:, g, :], PbS[0:127, g * 16:(g + 1) * 16], True, True, [dPbS, dvcS], [PSd[4]])
            ts("dve", rsS, PS[5][:, 0:32], 1e-30, None, ALU.max, None, [PSd[5]], [drsS])
            tr.op("dve", lambda: nc.vector.reciprocal(out=rsS, in_=rsS), [drsS], [drsS])
            tt("dve", coefS.rearrange("p (h t) -> p h t", h=8), rsS.rearrange("p (h t) -> p h t", h=8),
               gB[:, 0:8, 4 * s:4 * s + 4], ALU.mult, [drsS, dgB], [dcoefS])
            tt("dve", accS, PS[4][:, 0:32], coefS, ALU.mult, [PSd[4], dcoefS], [daccS])
            tt("dve", pnS[0:127, :], PfS[0:127, :], rsS[0:127, :], ALU.mult, [dPfS, drsS], [dpnS])
            tr.op("dve", lambda: nc.vector.tensor_reduce(
                out=psTS[0:127, :].rearrange("p (g t) -> p g t", g=2),
                in_=pnS[0:127, :].rearrange("p (g r t) -> p g t r", g=2, r=4), axis=AX.X, op=ALU.add), [dpnS], [dpsTS])
            for g in range(2):
                for hb in range(2):
                    b = trbank()
                    pv = psbf(b).rearrange("p (a t) -> p a t", a=8)
                    for a in range(8):
                        tp(pv[:, a, :], pg[:, hb * 8 + a, 512 + g * 128: 512 + (g + 1) * 128], identb, [dpg, CONST], [PSd[b]])
                    cp("act", kslcS[:, g, hb * 1024:(hb + 1) * 1024], psbf(b), [PSd[b]], [dkslcS])
            sb = sbank()
            mm(PS[sb][0:8, 0:32], psTS, ovl, True, True, [dpsTS, CONST2], [PSd[sb]])
            tt("dve", scS[0:8, :], PS[sb][0:8, 0:32], sbias[0:8, :], ALU.add, [PSd[sb], CS], [dscS])
            tr.op("dve", lambda: nc.vector.max(out=m8S[0:8, 0:8], in_=scS[0:8, :]), [dscS], [dm8S])
            tr.op("dve", lambda: nc.vector.match_replace(out=wkS[0:8, :], in_to_replace=m8S[0:8, 0:8], in_values=scS[0:8, :],
                                                         imm_value=-3.0e38), [dm8S, dscS], [dwkS])
            tr.op("dve", lambda: nc.vector.max(out=m8S[0:8, 8:16], in_=wkS[0:8, :]), [dwkS], [dm8S])
            ts("dve", selS[0:8, :], scS[0:8, :], m8S[0:8, 14:15], None, ALU.is_ge, None, [dscS, dm8S], [dselS])
            ts("dve", selS[0:8, :], selS[0:8, :], -1.0, BIG, ALU.add, ALU.mult, [], [dselS])
            b = trbank()
            pv = psbf(b).rearrange("p (g a t) -> p g a t", g=2, a=4)
            for g in range(2):
                for a in range(4):
                    tp(pv[:, g, a, :], wp[:, a, g * 128:(g + 1) * 128], identb, [dwp, CONST], [PSd[b]])
            cp("act", kwS, psbf(b).rearrange("p (g t) -> p g t", g=2), [PSd[b]], [dkwS])
            b = trbank()
            for c4 in range(4):
                src_chunk = (6 + c4) if c4 < 2 else (10 + c4 - 2)
                tp(psbf(b)[0:4, c4 * 128:(c4 + 1) * 128], ksT[:, src_chunk, 4 * s:4 * s + 4], identb, [dks, CONST], [PSd[b]])
            cp("act", vnew[0:4], psbf(b)[0:4, 0:512].rearrange("p (c d) -> p c d", c=4), [PSd[b]], [dvnew])
            sb = sbank()
            tp(PS[sb][0:32, 0:8], selS[0:8, :], identf[0:8, 0:8], [dselS, CONST], [PSd[sb]])
            cp("dve", nsS[0:32], PS[sb][0:32, 0:8].rearrange("p (g t) -> p g t", g=2).unsqueeze(2).to_broadcast([32, 2, 4, 4]),
               [PSd[sb]], [dnsS])
            for g in range(2):
                qg = qs[:, 4 * g:4 * g + 4, 4 * s:4 * s + 4]
                sb = sbank()
                for kt in range(16):
                    mm(PS[sb][:, kt * 16:(kt + 1) * 16], kslcS[:, g, kt * 128:(kt + 1) * 128], qg, True, False, [dkslcS, dqs], [PSd[sb]])
                    mm(PS[sb][:, kt * 16:(kt + 1) * 16], Esel[:, kt * 128:(kt + 1) * 128], nsS[:, g], False, True, [CONST2, dnsS], [PSd[sb]])
                mm(PS[sb][0:4, 256:272], ksT[:, 4 + g, 4 * s:4 * s + 4], qg, True, False, [dks, dqs], [PSd[sb]])
                mm(PS[sb][0:4, 256:272], identb[:, 0:4], mnew, False, True, [CONST, CS], [PSd[sb]])
                act(Pk[:, 0:16, :], PS[sb][:, 0:256].rearrange("p (k c) -> p k c", k=16), AF.Exp, [PSd[sb]], [dPk], scale=SCL)
                act(Pk[0:4, 16, :], PS[sb][0:4, 256:272], AF.Exp, [PSd[sb]], [dPk], scale=SCL)
                for kt in range(17):
                    va = pg[:, kt, 768 + g * 128: 768 + (g + 1) * 128] if kt < 16 else vnew[:, g, :]
                    mm(PS[4][:, 0:16], va, Pk[:, kt, :], kt == 0, kt == 16, [dpg, dvnew, dPk], [PSd[4]])
                for kt in range(17):
                    mm(PS[5][:, 0:16], onesb, Pk[:, kt, :], kt == 0, kt == 16, [CONST, dPk], [PSd[5]])
                branch_finish(g, 1, False, s)
                sb = sbank()
                for a in range(4):
                    mm(PS[sb][:, a * 16:(a + 1) * 16], kwS[:, g, a * 128:(a + 1) * 128], qg, True, a != 0, [dkwS, dqs], [PSd[sb]])
                    if a == 0:
                        mm(PS[sb][:, 0:16], identb, mwin0, False, True, [CONST, CS], [PSd[sb]])
                mm(PS[sb][0:4, 256:272], ksT[:, 8 + g, 4 * s:4 * s + 4], qg, True, False, [dks, dqs], [PSd[sb]])
                mm(PS[sb][0:4, 256:272], identb[:, 0:4], mnew, False, True, [CONST, CS], [PSd[sb]])
                act(Pk[:, 0:4, :], PS[sb][:, 0:64].rearrange("p (k c) -> p k c", k=4), AF.Exp, [PSd[sb]], [dPk], scale=SCL)
                act(Pk[0:4, 16, :], PS[sb][0:4, 256:272], AF.Exp, [PSd[sb]], [dPk], scale=SCL)
                for a in range(5):
                    va = wp[:, a, 256 + g * 128: 256 + (g + 1) * 128] if a < 4 else vnew[:, 2 + g, :]
                    pk = Pk[:, a, :] if a < 4 else Pk[:, 16, :]
                    mm(PS[4][:, 0:16], va, pk, a == 0, a == 4, [dwp, dvnew, dPk], [PSd[4]])
                for a in range(5):
                    pk = Pk[:, a, :] if a < 4 else Pk[:, 16, :]
                    mm(PS[5][:, 0:16], onesb, pk, a == 0, a == 4, [CONST, dPk], [PSd[5]])
                branch_finish(g, 2, False, s)
            tt("dve", mixTs[:, 0:8, 4 * s:4 * s + 4], accS.rearrange("p (h t) -> p h t", h=8), zns[:, :, 4 * s:4 * s + 4],
               ALU.mult, [daccS, sfm["zn"][1]], [dmixs])

        tr.barrier()
        ar_main = ar
        WOFF = SOFF
        wo_box.append(ar_main.alloc_at(WOFF, [128, KC, D], BF16, "wo"))
        GTOP = WOFF + KC * D * 2
        wov = w_out.rearrange("(k p) c -> p k c", p=128)
        for cb in range(4):
            for hk in range(2):
                tr.dma("pool", wo_box[0][:, hk * 8:(hk + 1) * 8, cb * 512:(cb + 1) * 512],
                       wov[:, hk * 8:(hk + 1) * 8, cb * 512:(cb + 1) * 512], dwo)
        ar = Arena(nc, lo=GTOP, hi=ar_main.hi)
        ucat = ar.alloc([128, 8, NSEQ, 34], F32, "ucat"); ducat = Dep("ucat")
        sct = [ar.alloc([30, 1024], F32, "sct%d" % i) for i in range(2)]; dsct = [Dep("sct%d" % i) for i in range(2)]
        dus = sfm["u"][1]
        for s in range(NSEQ):
            st_, dst2 = sct[s % 2], dsct[s % 2]
            tr.dma("sp", st_, sconv[s], dst2)
            b = s % 2
            for j in range(8):
                tp(PS[b][:, j * 30:(j + 1) * 30], st_[:, j * 128:(j + 1) * 128], identf[0:30, 0:30], [dst2, CONST], [PSd[b]])
            cp("dve" if s % 2 else "act", ucat[:, :, s, 0:30], PS[b][:, 0:240].rearrange("p (j t) -> p j t", j=8), [PSd[b]], [ducat])
        cp("dve", ucat[:, :, :, 30:34], us.rearrange("p j (s t) -> p j s t", t=4), [dus], [ducat])
        cacS = ar.alloc([128, 8, NSEQ, 4], F32, "cacS"); dcacS = Dep("cacS")
        ctmp = ar.alloc([128, 8, NSEQ, 4], F32, "ctmp"); dctmp = Dep("ctmp")
        accb = ar.alloc([128, 8, NST], BF16, "accb"); daccb = Dep("accb")

        def bc(ap2):
            return ap2.unsqueeze(2).unsqueeze(3).to_broadcast([128, 8, NSEQ, 4])

        for w in range(31):
            src = ucat[:, :, :, w:w + 4]
            if w == 0:
                tt("dve", cacS, src, bc(cdw[:, :, 0]), ALU.mult, [ducat, CONST], [dcacS])
                tt("dve", cacS, cacS, bc(cdb), ALU.add, [CONST], [dcacS])
            else:
                tt("dve", ctmp, src, bc(cdw[:, :, w]), ALU.mult, [ducat, CONST], [dctmp])
                tt("dve", cacS, cacS, ctmp, ALU.add, [dctmp], [dcacS])
        cp("dve", accb, cacS.rearrange("p j s t -> p j (s t)"), [dcacS], [daccb])
        sqS = ar.alloc([128, NST], BF16, "sqS"); dsqS = Dep("sqS")
        meanS = ar.alloc([128, NST], F32, "meanS"); dmeanS = Dep("meanS")
        rstdS = ar.alloc([128, NST], F32, "rstdS"); drstdS = Dep("rstdS")
        tnS = ar.alloc([128, NST], F32, "tnS"); dtnS = Dep("tnS")
        for j in range(8):
            mm(PS[6][:, 0:NST], onesb, accb[:, j, :], j == 0, j == 7, [CONST, daccb], [PSd[6]])
        for j in range(8):
            act(sqS, accb[:, j, :], AF.Square, [daccb], [dsqS])
            mm(PS[7][:, 0:NST], onesb, sqS, j == 0, j == 7, [CONST, dsqS], [PSd[7]])
        ts("dve", meanS, PS[6][:, 0:NST], 1.0 / 1024, None, ALU.mult, None, [PSd[6]], [dmeanS])
        tt("dve", tnS, meanS, meanS, ALU.mult, [dmeanS], [dtnS])
        stt("dve", tnS, PS[7][:, 0:NST], 1.0 / 1024, tnS, ALU.mult, ALU.subtract, [PSd[7]], [dtnS])
        act(tnS, tnS, AF.Sqrt, [dtnS], [dtnS], bias=EPS, scale=1.0)
        tr.op("dve", lambda: nc.vector.reciprocal(out=rstdS, in_=tnS), [dtnS], [drstdS])
        for j in range(8):
            tt("dve", tnS, accb[:, j, :], meanS, ALU.subtract, [daccb, dmeanS], [dtnS])
            tt("dve", tnS, tnS, rstdS, ALU.mult, [drstdS], [dtnS])
            act(tnS, tnS, AF.Silu, [dtnS], [dtnS], bias=lnb[:, j:j + 1], scale=lng[:, j:j + 1])
            tt("dve", mixTs[:, 8 + j, :], tnS, zcs[:, j, :], ALU.mult, [dtnS, sfm["zc"][1]], [dmixs])
        dd1 = Dep("o_conv_s"); dd2 = Dep("o_win_s")
        tr.dma("sp", conv_s[:, 0:26, :], sconv[:, 4:30, :], dd1)
        tr.dma("sp", win_s[:, 0:508, :], swin[:, 4:512, :], dd2)
        utm = ar.alloc([NST, 1024], F32, "utm"); dutm = Dep("utm")
        for j in range(8):
            tp(PS[j // 4][0:NST, (j % 4) * 128:(j % 4 + 1) * 128], us[:, j, :], identf, [dus, CONST], [PSd[j // 4]])
        for hb in range(2):
            cp("dve", utm[:, hb * 512:(hb + 1) * 512], PS[hb][0:NST, :], [PSd[hb]], [dutm])
        tr.dma("sp", conv_s[:, 26:30, :], utm, dutm, load=False)
        tr.dma("sp", win_s[:, 508:512, :], kvs[:, 1024:1536], sfm["kvs"][1], load=False)
        ar = ar_main

    wo_box = []
    dwo = Dep("wo")
    sample_phase()
    tr.barrier()

    WOFF = ar.offs["qT"]
    wo = wo_box[0]
    GTOP = WOFF + KC * D * 2
    ar.top = GTOP
    G2p = ar.alloc([128, D], F32, "G2p"); dG2p = Dep("G2p")
    for cb in range(4):
        mm(PS[cb], rp, G2[:, cb * 512:(cb + 1) * 512], True, True, [CONST, dG2], [PSd[cb]])
        cp("dve", G2p[:, cb * 512:(cb + 1) * 512], PS[cb], [PSd[cb]], [dG2p])
    mixo = [ar.alloc([128, D], F32, "mixo%d" % i) for i in range(2)]; dmixo = [Dep("mixo%d" % i) for i in range(2)]
    xr = [ar.alloc([128, D], F32, "xr%d" % i) for i in range(2)]; dxr = [Dep("xr%d" % i) for i in range(2)]
    sso = ar.alloc([128, 8], F32, "sso"); dsso = Dep("sso")
    jk = ar.alloc([128, 512], BF16, "jk"); djk = Dep("jk")

    def out_proj(ntok, lhs_fn, Rl, G2x, dG2x, xsrc, ydst, it):
        mo, dmo = mixo[it % 2], dmixo[it % 2]
        xt_, dxt_ = xr[it % 2], dxr[it % 2]
        tr.dma("sp", xt_[0:ntok, :], xsrc, dxt_)
        for cb in range(4):
            pb = (it % 2) * 4 + cb
            for k in range(KC):
                mm(PS[pb][0:ntok, :], lhs_fn(k), wo[:, k, cb * 512:(cb + 1) * 512], k == 0, k == KC - 1, Rl + [dwo], [PSd[pb]])
            act(jk[0:ntok, :], PS[pb][0:ntok, :], AF.Square, [PSd[pb]], [djk, dsso], accum=sso[0:ntok, cb:cb + 1])
            cp("dve", mo[0:ntok, cb * 512:(cb + 1) * 512], PS[pb][0:ntok, :], [PSd[pb], djk], [dmo])
        tr.op("dve", lambda: nc.vector.tensor_reduce(out=sso[0:ntok, 4:5], in_=sso[0:ntok, 0:4], axis=AX.X, op=ALU.add),
              [dsso], [dsso])
        act(sso[0:ntok, 5:6], sso[0:ntok, 4:5], AF.Sqrt, [dsso], [dsso], bias=EPS, scale=1.0 / D)
        tr.op("dve", lambda: nc.vector.reciprocal(out=sso[0:ntok, 6:7], in_=sso[0:ntok, 5:6]), [dsso], [dsso])
        stt("dve", mo[0:ntok, :], mo[0:ntok, :], sso[0:ntok, 6:7], G2x[0:ntok, :], ALU.mult, ALU.mult, [dsso, dG2x], [dmo])
        tt("pool", mo[0:ntok, :], mo[0:ntok, :], xt_[0:ntok, :], ALU.add, [dxt_], [dmo])
        tr.dma("sp", ydst, mo[0:ntok, :], dmo, load=False)

    G2s = ar.alloc([NST, D], F32, "G2s"); dG2s = Dep("G2s")
    for cb in range(4):
        mm(PS[cb][0:NST, :], rsel, G2[:, cb * 512:(cb + 1) * 512], True, True, [CONST, dG2], [PSd[cb]])
        cp("dve", G2s[:, cb * 512:(cb + 1) * 512], PS[cb][0:NST, :], [PSd[cb]], [dG2s])
    out_proj(NST, (lambda k: mixTs[:, k, :]), [dmixs], G2s, dG2s, xs, y_s, 8)
    for t in range(8):
        out_proj(128, (lambda k, t=t: mixT[:, k, t * 128:(t + 1) * 128]), [dmix[c][t // 4] for c in range(16)],
                 G2p, dG2p, xo[t * 128:(t + 1) * 128, :], y_p[t * 128:(t + 1) * 128, :], t)

    if dbg:
        o = dout("d_mixT", list(mixT.shape), BF16)
        tr.dma("sp", o, mixT, Dep("d_mixT"), load=False, extraR=[d for row in dmix for d in row])

    tr.finish()
    return nc


def _consts(half):
    v = float(half)
    c = {}
    c["ident_bf"] = np.eye(128, dtype=np.float32).astype(ml_dtypes.bfloat16)
    c["ident_f"] = np.eye(128, dtype=np.float32)
    c["ones_bf"] = np.ones((128, 128), np.float32).astype(ml_dtypes.bfloat16)
    c["ones_f"] = np.ones((128, 128), np.float32)
    j = np.arange(128)[:, None]
    i = np.arange(128)[None, :]
    negtri = np.where(j <= i, 0.0, -BIG)
    neglow = np.where(j >= i, 0.0, -BIG)
    ctxmid = np.full((128, 128), -BIG * (1 - v))
    ctxlow = np.minimum(neglow, ctxmid) if v == 0 else neglow
    m4 = np.stack([np.tile(m[:, None, :], (1, 4, 1)).reshape(128, 512) for m in (negtri, neglow, ctxmid, ctxlow)], axis=1)
    c["masks4"] = m4.astype(np.float32).astype(ml_dtypes.bfloat16)
    m = np.arange(127)[:, None, None]
    qi = np.arange(8)[None, :, None]
    ii = np.arange(128)[None, None, :]
    valid = (16 * m + 31 <= 1024 + 128 * qi + ii) & (m >= 64 * (1 - half))
    ncm = np.where(valid, 0.0, -BIG)
    c["negcmp"] = np.tile(ncm[:, :, None, :], (1, 1, 4, 1)).reshape(127, 8, 512).astype(np.float32).astype(ml_dtypes.bfloat16)
    s = np.arange(128)[:, None]
    key = np.arange(2048)[None, :]
    c["Esel"] = (key // 64 == s).astype(np.float32).astype(ml_dtypes.bfloat16)
    mm_ = np.arange(127)[:, None] * 16
    sj = np.arange(32)[None, :] * 64
    c["overlap"] = np.concatenate([((mm_ < sj + 64) & (mm_ + 32 > sj)).astype(np.float32), np.zeros((1, 32), np.float32)], 0)
    tl = 1024 + 128 * np.arange(8)[None, :, None] + np.arange(128)[:, None, None]
    sb = np.arange(32)[None, None, :]
    s0 = 16 * (1 - half)
    allowed = (64 * sb <= tl) & (sb >= s0)
    cur = tl // 64
    forced = (sb == s0) | (sb == cur) | (sb == cur - 1)
    c["selbias"] = np.where(allowed, np.where(forced, 1e6, 0.0), -1e30).astype(np.float32)
    c["allowed"] = allowed.astype(np.float32)
    sg_ = np.zeros((128, 24, 128), np.float32)
    for k in range(24):
        sg_[k, k, :] = 1.0
    c["selg"] = sg_.astype(ml_dtypes.bfloat16)
    rp = np.zeros((128, 128), np.float32); rp[16, :] = 1.0
    rs = np.zeros((128, 64), np.float32)
    for s_ in range(16):
        rs[s_, 4 * s_:4 * s_ + 4] = 1.0
    c["rp"] = rp; c["rs"] = rs
    c["cval"] = np.full((128, 1), v, np.float32)
    sb_ = np.zeros((128, 32), np.float32); sb_[:, 0] = 1e6; sb_[:, 31] = 1e6
    c["sbias"] = sb_
    pm = np.zeros((128, 128), np.float32)
    for mp in range(8):
        for ph in range(16):
            pm[16 * mp + ph, ph * 8 + mp] = 1.0
    c["perm16"] = pm.astype(ml_dtypes.bfloat16)
    tq = np.arange(16)[None, :] % 4
    jp = np.arange(128)[:, None]
    c["mnew"] = np.where((jp < 4) & (jp > tq), -BIG, 0.0).astype(np.float32).astype(ml_dtypes.bfloat16)
    c["mwin0"] = np.where(jp < tq, -BIG, 0.0).astype(np.float32).astype(ml_dtypes.bfloat16)
    return c


def fm16(vec):
    return np.ascontiguousarray(vec.reshape(-1, 128).T)


def core_inputs(inp, core, nb_prompt, seqs):
    b, half = core // 2, core % 2
    f = lambda a: np.ascontiguousarray(np.asarray(a, dtype=np.float32))
    xp = np.asarray(inp["x_prompt"])[b]
    m = {}
    m["xo"] = f(xp[half * 1024:(half + 1) * 1024])
    m["xc"] = f(xp[0:1024])
    m["xs"] = f(np.asarray(inp["x_sample"])[seqs].reshape(NST, D))
    c17 = np.concatenate([np.asarray(inp["c_sample"])[seqs], np.asarray(inp["c_prompt"])[b:b + 1]], 0)
    m["cT"] = f(c17.T.reshape(KC, 128, 17).transpose(1, 0, 2))
    m["w_ada"] = f(inp["w_ada"][0]); m["b_ada"] = f(inp["b_ada"][0][None, :])
    m["npre_fm"] = f(fm16(np.asarray(inp["norm_pre"][0]))); m["npost"] = f(np.asarray(inp["norm_post"][0])[None, :])
    m["w_in"] = f(inp["w_in"][0]); m["w_out"] = f(inp["w_out"][0])
    m["cmp_w1"] = f(inp["cmp_w1"][0])
    m["cmp_peT"] = f(np.asarray(inp["cmp_pe"][0]).transpose(2, 0, 1))
    m["cmp_b1T"] = f(np.asarray(inp["cmp_b1"][0]).T); m["cmp_w2"] = f(inp["cmp_w2"][0])
    m["cmp_b2T"] = f(np.asarray(inp["cmp_b2"][0]).T); m["cmp_b2v"] = f(np.asarray(inp["cmp_b2"][0])[1:2, :])
    m["conv_dwT"] = f(np.asarray(inp["conv_dw"][0]).T.reshape(8, 128, 31).transpose(1, 0, 2))
    for nm, key in (("conv_dbT", "conv_db"), ("conv_lngT", "conv_ln_g"), ("conv_lnbT", "conv_ln_b")):
        m[nm] = f(np.asarray(inp[key][0]).reshape(8, 128).T)
    ck = np.asarray(inp["cache_kv"][0])
    m["cache"] = ck.reshape(ck.shape[0] * 128, 1024)
    m["ptab"] = np.ascontiguousarray(np.asarray(inp["page_table"])[seqs].astype(np.int32))
    m["swin"] = f(np.asarray(inp["state_win_kv"][0])[seqs].reshape(NSEQ, 512, 512))
    m["sconv"] = f(np.asarray(inp["state_conv"][0])[seqs])
    m.update(_consts(half))
    return m


_NC_CACHE = {}


def run(inputs, cores, n_phys, dbg=False):
    key = (n_phys, dbg)
    nc = build(n_phys=n_phys, dbg=dbg)
    in_maps = []
    for ci, core in enumerate(cores):
        seqs = np.arange(ci * NSEQ, (ci + 1) * NSEQ)
        in_maps.append(core_inputs(inputs, core, None, seqs))
    res = run_bass_kernel_spmd(nc, in_maps, core_ids=list(range(len(cores))))
    return res.results


def kernel(**inputs):
    n_phys = int(np.asarray(inputs["cache_kv"]).shape[1])
    res = run(inputs, list(range(8)), n_phys)
    B = 4
    y_p = np.zeros((B, 2048, D), np.float32)
    kv_p = np.zeros((1, B, 2048, 4, 2, 128), np.float32)
    win_p = np.zeros((1, B, 512, 2, 2, 128), np.float32)
    conv_p = np.zeros((1, B, 30, 1024), np.float32)
    y_s = np.zeros((128, 4, D), np.float32)
    kv_s = np.zeros((1, 128, 4, 4, 2, 128), np.float32)
    win_s = np.zeros((1, 128, 512, 2, 2, 128), np.float32)
    conv_s = np.zeros((1, 128, 30, 1024), np.float32)
    for c in range(8):
        r = res[c]
        b, half = c // 2, c % 2
        y_p[b, half * 1024:(half + 1) * 1024] = r["y_p"]
        kv_p[0, b, half * 1024:(half + 1) * 1024] = r["kv_p"].reshape(1024, 4, 2, 128)
        if half == 1:
            win_p[0, b] = r["win_p"].reshape(512, 2, 2, 128)
            conv_p[0, b] = r["conv_p"][2:32]
        sl = slice(c * NSEQ, (c + 1) * NSEQ)
        y_s[sl] = r["y_s"].reshape(NSEQ, 4, D)
        kv_s[0, sl] = r["kv_s"].reshape(NSEQ, 4, 4, 2, 128)
        win_s[0, sl] = r["win_s"].reshape(NSEQ, 512, 2, 2, 128)
        conv_s[0, sl] = r["conv_s"]
    return (y_p, y_s, kv_p, kv_s, win_p, win_s, conv_p, conv_s)
```
